# Optimizing a Trainium2 kernel written in Bass

```python
import jax, jax.numpy as jnp
from jax import lax
import numpy as np

D_MODEL = 1024
BATCH = 4
SEQ = 4096
DEPTH = 4
DEC_BATCH = 128
DEC_SEQ = 8
PAST_LEN = 2048
PAGE_SIZE = 128

N_A_LAYERS = DEPTH // 2
N_B_LAYERS = DEPTH - N_A_LAYERS
HEAD_DIM = 64
N_HEADS = D_MODEL // HEAD_DIM
CONV_WIDTH = 31
SB_BLOCK = 128
SB_BIAS_INIT = -6.0
PEER_HEADS = 8
PEER_TOPK = 16
PEER_NKEYS = 128
PEER_EXPERTS = PEER_NKEYS * PEER_NKEYS
PEER_DQ = D_MODEL // 4
PEER_DHALF = PEER_DQ // 2
PEER_CHUNK = 256
EPS = 1e-6

kernel_name = 'yoco_conformer_stickbreak_peer'


def rms_norm(x, g):
    x32 = x.astype(jnp.float32)
    y = x32 * lax.rsqrt(jnp.mean(x32 * x32, axis=-1, keepdims=True) + EPS)
    return (y * g.astype(jnp.float32)).astype(x.dtype)


def layer_norm(x, g, b):
    x32 = x.astype(jnp.float32)
    mu = jnp.mean(x32, axis=-1, keepdims=True)
    xc = x32 - mu
    var = jnp.mean(xc * xc, axis=-1, keepdims=True)
    return (xc * lax.rsqrt(var + EPS) * g.astype(jnp.float32) + b.astype(jnp.float32)).astype(x.dtype)


def ada_params(c, w, b):
    m = jax.nn.silu(c) @ w + b
    return jnp.split(m[:, None, :], 6, axis=-1)


def conv_module(h, hist, w_pw1, b_pw1, w_dw, b_dw, ln_g, ln_b, w_pw2, b_pw2):
    a = h @ w_pw1 + b_pw1
    glu = a[..., :D_MODEL] * jax.nn.sigmoid(a[..., D_MODEL:])
    full = jnp.concatenate([hist.astype(glu.dtype), glu], axis=1)
    y = lax.conv_general_dilated(full, w_dw[:, None, :].astype(full.dtype), window_strides=(1,),
                                 padding='VALID', dimension_numbers=('NWC', 'WIO', 'NWC'),
                                 feature_group_count=D_MODEL) + b_dw
    y = jax.nn.silu(layer_norm(y, ln_g, ln_b))
    return y @ w_pw2 + b_pw2, full[:, full.shape[1] - (CONV_WIDTH - 1):]


def stick_breaking(q, k, v, bias, q_off):
    tq = q.shape[1]
    bias32 = bias.astype(jnp.float32)[None, :, None, None]
    outs = []
    for start in range(0, tq, SB_BLOCK):
        end = min(start + SB_BLOCK, tq)
        kl = q_off + end
        z = jnp.einsum('bqhd,bkhd->bhqk', q[:, start:end], k[:, :kl]).astype(jnp.float32) * (HEAD_DIM ** -0.5) + bias32
        qpos = q_off + jnp.arange(start, end)
        kpos = jnp.arange(kl)
        mask = kpos[None, :] < qpos[:, None]
        log_rest = jnp.where(mask, jax.nn.log_sigmoid(-z), 0.0)
        after = lax.cumsum(log_rest, axis=3, reverse=True) - log_rest
        log_a = jnp.where(mask, jax.nn.log_sigmoid(z) + after, -jnp.inf)
        a = jnp.exp(log_a).astype(v.dtype)
        outs.append(jnp.einsum('bhqk,bkhd->bqhd', a, v[:, :kl]))
    return jnp.concatenate(outs, axis=1)


def peer(h, w_pq, sub_keys, expert_u, expert_v):
    b, t, d = h.shape
    xt = h.reshape(b * t, d)
    n = xt.shape[0]
    xt = jnp.pad(xt, ((0, (-n) % PEER_CHUNK), (0, 0))).reshape(-1, PEER_CHUNK, d)

    def block(xc):
        q = (xc @ w_pq).reshape(PEER_CHUNK, PEER_HEADS, 2, PEER_DHALF)
        s = jnp.einsum('nhpd,hpkd->nhpk', q, sub_keys).astype(jnp.float32)
        sv, si = lax.top_k(s, PEER_TOPK)
        cand = sv[:, :, 0, :, None] + sv[:, :, 1, None, :]
        cidx = si[:, :, 0, :, None] * PEER_NKEYS + si[:, :, 1, None, :]
        cv, ci = lax.top_k(cand.reshape(PEER_CHUNK, PEER_HEADS, -1), PEER_TOPK)
        eidx = jnp.take_along_axis(cidx.reshape(PEER_CHUNK, PEER_HEADS, -1), ci, axis=-1)
        g = jax.nn.softmax(cv, axis=-1)
        act = jax.nn.gelu(jnp.einsum('nd,nhkd->nhk', xc, expert_u[eidx]).astype(jnp.float32))
        w = (g * act).astype(xc.dtype)
        return jnp.einsum('nhk,nhkd->nd', w, expert_v[eidx])

    out = lax.map(block, xt).reshape(-1, d)[:n]
    return out.reshape(b, t, d)


def trunk(x, c, conv_hist, k_past, v_past, q_off, p):
    b, t, _ = x.shape
    new_hist = []
    k_new = v_new = k_all = v_all = None
    for l in range(DEPTH):
        sh1, sc1, g1, sh2, sc2, g2 = ada_params(c, p['w_ada'][l], p['b_ada'][l])
        h = rms_norm(x, p['norm_mix'][l]) * (1 + sc1) + sh1
        if l < N_A_LAYERS:
            hist = jnp.zeros((b, CONV_WIDTH - 1, D_MODEL), x.dtype) if conv_hist is None else conv_hist[l]
            out, nh = conv_module(h, hist, p['w_pw1'][l], p['b_pw1'][l], p['w_dw'][l], p['b_dw'][l],
                                  p['ln_g'][l], p['ln_b'][l], p['w_pw2'][l], p['b_pw2'][l])
            new_hist.append(nh)
        else:
            if l == N_A_LAYERS:
                hk = rms_norm(x, p['norm_kv'])
                k_new = (hk @ p['w_k']).reshape(b, t, N_HEADS, HEAD_DIM)
                v_new = (hk @ p['w_v']).reshape(b, t, N_HEADS, HEAD_DIM)
                k_all = k_new if k_past is None else jnp.concatenate([k_past.astype(k_new.dtype), k_new], axis=1)
                v_all = v_new if v_past is None else jnp.concatenate([v_past.astype(v_new.dtype), v_new], axis=1)
            j = l - N_A_LAYERS
            q = (h @ p['w_q'][j]).reshape(b, t, N_HEADS, HEAD_DIM)
            att = stick_breaking(q, k_all, v_all, p['b_sb'][j], q_off)
            out = att.reshape(b, t, N_HEADS * HEAD_DIM) @ p['w_o'][j]
        x = x + g1 * out
        h2 = rms_norm(x, p['norm_ffn'][l]) * (1 + sc2) + sh2
        x = x + g2 * peer(h2, p['w_pq'][l], p['sub_keys'][l], p['expert_u'][l], p['expert_v'][l])
    return rms_norm(x, p['final_norm']), jnp.stack(new_hist), k_new, v_new


def setup_inputs(seed: int = 0) -> dict:
    key = jax.random.key(seed)
    ks = iter(jax.random.split(key, 40))

    def nrm(shape, s):
        return jax.random.normal(next(ks), shape, jnp.float32) * s

    n_pages = PAST_LEN // PAGE_SIZE
    n_used = DEC_BATCH * n_pages
    n_pool = n_used + n_used // 4
    hd = N_HEADS * HEAD_DIM
    return {
        'x_prompt': nrm((BATCH, SEQ, D_MODEL), 1.0),
        'x_sample': nrm((DEC_BATCH, DEC_SEQ, D_MODEL), 1.0),
        'cache_k': nrm((n_pool, PAGE_SIZE, N_HEADS, HEAD_DIM), 1.0),
        'cache_v': nrm((n_pool, PAGE_SIZE, N_HEADS, HEAD_DIM), 1.0),
        'state_conv': nrm((N_A_LAYERS, DEC_BATCH, CONV_WIDTH - 1, D_MODEL), 0.5),
        'page_table': jax.random.permutation(next(ks), n_pool)[:n_used].reshape(DEC_BATCH, n_pages).astype(jnp.int32),
        'c_prompt': nrm((BATCH, D_MODEL), 1.0),
        'c_sample': nrm((DEC_BATCH, D_MODEL), 1.0),
        'w_ada': nrm((DEPTH, D_MODEL, 6 * D_MODEL), 0.5 * D_MODEL ** -0.5),
        'b_ada': nrm((DEPTH, 6 * D_MODEL), 0.01),
        'norm_mix': 1.0 + nrm((DEPTH, D_MODEL), 0.02),
        'norm_ffn': 1.0 + nrm((DEPTH, D_MODEL), 0.02),
        'w_pw1': nrm((N_A_LAYERS, D_MODEL, 2 * D_MODEL), D_MODEL ** -0.5),
        'b_pw1': nrm((N_A_LAYERS, 2 * D_MODEL), 0.01),
        'w_dw': nrm((N_A_LAYERS, CONV_WIDTH, D_MODEL), CONV_WIDTH ** -0.5),
        'b_dw': nrm((N_A_LAYERS, D_MODEL), 0.01),
        'ln_g': 1.0 + nrm((N_A_LAYERS, D_MODEL), 0.02),
        'ln_b': nrm((N_A_LAYERS, D_MODEL), 0.01),
        'w_pw2': nrm((N_A_LAYERS, D_MODEL, D_MODEL), D_MODEL ** -0.5),
        'b_pw2': nrm((N_A_LAYERS, D_MODEL), 0.01),
        'norm_kv': 1.0 + nrm((D_MODEL,), 0.02),
        'w_k': nrm((D_MODEL, hd), D_MODEL ** -0.5),
        'w_v': nrm((D_MODEL, hd), D_MODEL ** -0.5),
        'w_q': nrm((N_B_LAYERS, D_MODEL, hd), D_MODEL ** -0.5),
        'b_sb': SB_BIAS_INIT + nrm((N_B_LAYERS, N_HEADS), 0.1),
        'w_o': nrm((N_B_LAYERS, hd, D_MODEL), hd ** -0.5),
        'w_pq': nrm((DEPTH, D_MODEL, PEER_HEADS * PEER_DQ), D_MODEL ** -0.5),
        'sub_keys': nrm((DEPTH, PEER_HEADS, 2, PEER_NKEYS, PEER_DHALF), PEER_DHALF ** -0.5),
        'expert_u': nrm((DEPTH, PEER_EXPERTS, D_MODEL), D_MODEL ** -0.5),
        'expert_v': nrm((DEPTH, PEER_EXPERTS, D_MODEL), 0.1),
        'final_norm': 1.0 + nrm((D_MODEL,), 0.02),
    }


def reference(x_prompt, x_sample, cache_k, cache_v, state_conv, page_table, c_prompt, c_sample,
              w_ada, b_ada, norm_mix, norm_ffn, w_pw1, b_pw1, w_dw, b_dw, ln_g, ln_b, w_pw2, b_pw2,
              norm_kv, w_k, w_v, w_q, b_sb, w_o, w_pq, sub_keys, expert_u, expert_v, final_norm):
    p = {'w_ada': w_ada, 'b_ada': b_ada, 'norm_mix': norm_mix, 'norm_ffn': norm_ffn,
         'w_pw1': w_pw1, 'b_pw1': b_pw1, 'w_dw': w_dw, 'b_dw': b_dw, 'ln_g': ln_g, 'ln_b': ln_b,
         'w_pw2': w_pw2, 'b_pw2': b_pw2, 'norm_kv': norm_kv, 'w_k': w_k, 'w_v': w_v,
         'w_q': w_q, 'b_sb': b_sb, 'w_o': w_o, 'w_pq': w_pq, 'sub_keys': sub_keys,
         'expert_u': expert_u, 'expert_v': expert_v, 'final_norm': final_norm}
    y_prompt, conv_prompt, k_prompt, v_prompt = trunk(x_prompt, c_prompt, None, None, None, 0, p)
    n_seq, n_pages = page_table.shape
    k_past = cache_k[page_table].reshape(n_seq, n_pages * PAGE_SIZE, N_HEADS, HEAD_DIM)
    v_past = cache_v[page_table].reshape(n_seq, n_pages * PAGE_SIZE, N_HEADS, HEAD_DIM)
    y_sample, conv_sample, k_sample, v_sample = trunk(x_sample, c_sample, state_conv, k_past, v_past,
                                                      n_pages * PAGE_SIZE, p)
    return (y_prompt, y_sample, conv_prompt, conv_sample, k_prompt, v_prompt, k_sample, v_sample)
```

```python
from contextlib import ExitStack
import numpy as np
import concourse.bass as bass
import concourse.mybir as mybir
from concourse.bass_utils import run_bass_kernel_spmd

F32 = mybir.dt.float32
BF16 = mybir.dt.bfloat16
I32 = mybir.dt.int32
U32 = mybir.dt.uint32
ALU = mybir.AluOpType
AF = mybir.ActivationFunctionType
AX = mybir.AxisListType

D = 1024
NCH = 8
SEQ = 4096
NB = 4
DEC_B = 128
DEC_T = 8
PAST = 2048
PAGE = 128
NPAGES = 16
NPOOL = 2560
NH = 16
DH = 64
CW = 31
HIST = 30
DEPTH = 4
NA = 2
PH = 8
PK = 16
NKEYS = 128
NEXP = 16384
EPS = 1e-6
TP = 256
DO_PEER = True
DBG_ATT = 9
DBG_KV = 9
NCORES = 8
SPC = DEC_B // NCORES


class Tk:
    def __init__(self, t, name, sem=None):
        self.t = t
        self.name = name
        self.w = None
        self.r = []
        self.dsem = sem
        self.dcnt = 0

    def __getitem__(self, idx):
        return self.t[idx]


class Ctx:
    def __init__(self, nc, stack):
        self.nc = nc
        self.stack = stack
        self.engs = {'pe': nc.tensor, 'dve': nc.vector, 'act': nc.scalar, 'pool': nc.gpsimd, 'sp': nc.sync}
        self.sem = {k: stack.enter_context(nc.semaphore("s_" + k)) for k in self.engs}
        self.cnt = {k: 0 for k in self.engs}
        self.waited = {k: {} for k in self.engs}
        self.prog = {k: [] for k in self.engs}
        self.out_events = []
        self.ntile = 0

    def sb(self, shape, dt, name, dma=False):
        t = self.stack.enter_context(self.nc.sbuf_tensor(name, list(shape), dt))
        self.sbytes = getattr(self, "sbytes", 0) + int(np.prod(shape[1:])) * (2 if dt == BF16 else 4)
        sem = self.stack.enter_context(self.nc.semaphore("d_" + name)) if dma else None
        return Tk(t, name, sem)

    def ps(self, name):
        t = self.stack.enter_context(self.nc.psum_tensor(name, [128, 512], F32))
        tk = Tk(t, name)
        tk.excl = True
        return tk

    def dram(self, name, shape, dt, dma=False):
        t = self.nc.dram_tensor(name, list(shape), dt, kind="ExternalOutput").ap()
        sem = self.stack.enter_context(self.nc.semaphore("d_" + name)) if dma else None
        return Tk(t, name, sem)

    def _deps(self, e, reads, writes):
        evs = []
        for tk in reads:
            if tk.w is not None:
                evs.append(tk.w)
            if getattr(tk, "excl", False):
                evs.extend(tk.r)
        for tk in writes:
            if tk.w is not None:
                evs.append(tk.w)
            evs.extend(tk.r)
        for (sem, val, src) in evs:
            if e == 'pe' and src == 'pe':
                continue
            key = id(sem)
            if self.waited[e].get(key, 0) >= val:
                continue
            self.waited[e][key] = val
            self.prog[e].append(('wait', sem, val))

    def _mark(self, ev, reads, writes):
        for tk in reads:
            tk.r.append(ev)
        for tk in writes:
            tk.w = ev
            tk.r = []

    def op(self, e, fn, reads=(), writes=()):
        self._deps(e, reads, writes)
        self.cnt[e] += 1
        ev = (self.sem[e], self.cnt[e], e)
        self.prog[e].append(('op', fn, self.sem[e], 1))
        self._mark(ev, reads, writes)
        return ev

    def dma(self, q, fn, sbt, reads=(), writes=(), out_dram=False):
        self._deps(q, reads, writes)
        sbt.dcnt += 1
        ev = (sbt.dsem, 16 * sbt.dcnt, 'dma')
        self.prog[q].append(('op', fn, sbt.dsem, 16))
        self._mark(ev, reads, writes)
        if out_dram:
            self.out_events.append(ev)
        return ev

    def emit(self):
        nc = self.nc
        last = {}
        for (sem, val, _) in self.out_events:
            k = id(sem)
            if k not in last or last[k][1] < val:
                last[k] = (sem, val)
        for (sem, val) in last.values():
            self.prog['sp'].append(('wait', sem, val))

        def run(eng, lst):
            for it in lst:
                if it[0] == 'wait':
                    eng.wait_ge(it[1], it[2])
                else:
                    it[1]().then_inc(it[2], it[3])

        with nc.Block() as block:
            @block.tensor
            def _(e):
                run(nc.tensor, self.prog['pe'])

            @block.vector
            def _(e):
                run(nc.vector, self.prog['dve'])

            @block.scalar
            def _(e):
                run(nc.scalar, self.prog['act'])

            @block.gpsimd
            def _(e):
                run(nc.gpsimd, self.prog['pool'])

            @block.sync
            def _(e):
                run(nc.sync, self.prog['sp'])


def build_nc(n_ptiles=SEQ // TP, do_sample=True, n_layers=DEPTH):
    nc = bass.Bass("TRN2", target_bir_lowering=False)
    with ExitStack() as stack:
        c = Ctx(nc, stack)
        _build(c, n_ptiles, do_sample, n_layers)
    return nc


def _build(c, n_ptiles, do_sample, n_layers):
    nc = c.nc

    def din(name, shape, dt=F32):
        return nc.dram_tensor(name, list(shape), dt, kind="ExternalInput").ap()

    def dout(name, shape, dt=F32):
        return nc.dram_tensor(name, list(shape), dt, kind="ExternalOutput").ap()

    NTOK_P = n_ptiles * TP
    x_p = din("x_p", [NTOK_P, D])
    x_s = din("x_s", [128, D])
    c_all = din("c_all", [17, D])
    w_ada = din("w_ada", [DEPTH * D, 6 * D])
    b_ada = din("b_ada", [128, DEPTH * 48])
    nmix = din("nmix", [128, DEPTH * NCH])
    nffn = din("nffn", [128, DEPTH * NCH])
    nkv = din("nkv", [128, NCH])
    fnorm = din("fnorm", [128, NCH])
    w_pw1 = din("w_pw1", [NA * D, 2 * D])
    b_pw1 = din("b_pw1", [128, NA * 16])
    w_dw = din("w_dw", [128, NA * NCH * CW])
    b_dw = din("b_dw", [128, NA * NCH])
    ln_g = din("ln_g", [128, NA * NCH])
    ln_b = din("ln_b", [128, NA * NCH])
    w_pw2 = din("w_pw2", [NA * D, D])
    b_pw2 = din("b_pw2", [128, NA * NCH])
    w_k = din("w_k", [D, D])
    w_v = din("w_v", [D, D])
    w_q = din("w_q", [2 * D, D])
    w_o = din("w_o", [2 * D, D])
    bsb = din("bsb", [128, 2 * NH])
    w_pq = din("w_pq", [DEPTH * D, 2 * D])
    skT = din("skT", [DEPTH * 16 * 128, 128])
    exp_u = din("exp_u", [DEPTH * NEXP, D])
    exp_v = din("exp_v", [DEPTH * NEXP, D])
    npool = NPOOL if do_sample else 16
    cache_k = din("cache_k", [npool * PAGE, D])
    cache_v = din("cache_v", [npool * PAGE, D])
    ptab = din("ptab", [128, SPC * NPAGES], I32)
    st_conv = din("st_conv", [NA * SPC * HIST, D])

    y_p = dout("y_p", [NTOK_P, D])
    y_s = dout("y_s", [128, D])
    conv_p = dout("conv_p", [NA * HIST, D])
    conv_s = dout("conv_s", [NA * SPC * HIST, D])
    k_p = dout("k_p", [NTOK_P, D])
    v_p = dout("v_p", [NTOK_P, D])
    k_s = dout("k_s", [128, D])
    v_s = dout("v_s", [128, D])

    KT_d = c.dram("KT_d", [NH, DH, SEQ], BF16)
    V_d = c.dram("V_d", [SEQ, D], BF16)
    VS_d = c.dram("VS_d", [128, D], BF16)
    EU_d = [c.dram("EU_d%d" % l, [NEXP, D], BF16, dma=True) for l in range(n_layers)]
    EV_d = [c.dram("EV_d%d" % l, [NEXP, D], BF16, dma=True) for l in range(n_layers)]

    ident = c.sb([128, 128], F32, "ident")
    ones = c.sb([128, 128], F32, "ones")
    negtri = c.sb([128, 128], BF16, "negtri")
    negones = c.sb([128, 128], BF16, "negones")
    iota16 = c.sb([128, 16], F32, "iota16")
    iotap = c.sb([128, 1], F32, "iotap")
    c.op('pool', lambda: nc.gpsimd.memset(ones[:, :], 1.0), writes=[ones])
    c.op('pool', lambda: nc.gpsimd.memset(ident[:, :], 1.0), writes=[ident])
    c.op('pool', lambda: nc.gpsimd.affine_select(out=ident[:, :], in_=ident[:, :], pattern=[[-1, 128]],
                                                 compare_op=ALU.is_equal, fill=0.0, base=0, channel_multiplier=1),
         reads=[ident], writes=[ident])
    c.op('pool', lambda: nc.gpsimd.memset(negones[:, :], -1.0), writes=[negones])
    c.op('pool', lambda: nc.gpsimd.memset(negtri[:, :], -1.0), writes=[negtri])
    c.op('pool', lambda: nc.gpsimd.affine_select(out=negtri[:, :], in_=negtri[:, :], pattern=[[-1, 128]],
                                                 compare_op=ALU.is_ge, fill=0.0, base=0, channel_multiplier=1),
         reads=[negtri], writes=[negtri])
    c.op('pool', lambda: nc.gpsimd.iota(iota16[:, :], pattern=[[1, 16]], base=0, channel_multiplier=0,
                                        allow_small_or_imprecise_dtypes=True), writes=[iota16])
    c.op('pool', lambda: nc.gpsimd.iota(iotap[:, :], pattern=[[0, 1]], base=0, channel_multiplier=1,
                                        allow_small_or_imprecise_dtypes=True), writes=[iotap])

    def load_small(src, cols, name, dt=F32):
        t = c.sb([128, cols], dt, name, dma=True)
        c.dma('sp', lambda: nc.sync.dma_start(out=t[:, :], in_=src[:, :]), t, writes=[t])
        return t

    bada_sb = load_small(b_ada, DEPTH * 48, "bada_sb")
    nmix_sb = load_small(nmix, DEPTH * NCH, "nmix_sb")
    nffn_sb = load_small(nffn, DEPTH * NCH, "nffn_sb")
    nkv_sb = load_small(nkv, NCH, "nkv_sb")
    fn_sb = load_small(fnorm, NCH, "fn_sb")
    bpw1_sb = load_small(b_pw1, NA * 16, "bpw1_sb")
    wdw_sb = load_small(w_dw, NA * NCH * CW, "wdw_sb")
    bdw_sb = load_small(b_dw, NA * NCH, "bdw_sb")
    lng_sb = load_small(ln_g, NA * NCH, "lng_sb")
    lnb_sb = load_small(ln_b, NA * NCH, "lnb_sb")
    bpw2_sb = load_small(b_pw2, NA * NCH, "bpw2_sb")
    bsb_sb = load_small(bsb, 2 * NH, "bsb_sb")
    bias_s = c.sb([128, 128], F32, "bias_s")

    psum = [c.ps("ps%d" % i) for i in range(8)]
    prot = [0]

    def nps():
        prot[0] = (prot[0] + 1) % 4
        return psum[prot[0]]

    tm = c.sb([128, 2, D], F32, "tm", dma=True)
    xT = c.sb([128, NCH, TP], F32, "xT")
    scrA = c.sb([128, NCH, TP], F32, "scrA")
    hT = c.sb([128, NCH, TP], BF16, "hT")
    rstd = c.sb([128, TP], F32, "rstd")
    tmpN = c.sb([128, TP], F32, "tmpN")
    tmpM = c.sb([128, TP], F32, "tmpM")
    wsl = [c.sb([128, 8, 256], BF16, "wsl%d" % i, dma=True) for i in range(2)]
    wrot = [0]
    ada = c.sb([128, DEPTH * 48, 17], F32, "ada")
    fullbuf = c.sb([128, NCH, SPC * (HIST + DEC_T)], F32, "fullbuf")
    FOFF = [0, 304]

    class _V:
        def __init__(self, fn):
            self.fn = fn

        def __getitem__(self, idx):
            return self.fn(idx)
    yb = [c.sb([128, TP], F32, "yb%d" % j) for j in range(NCH)]
    qT = c.sb([128, 16, TP], BF16, "qT", dma=True)
    attT = c.sb([64, NH, TP], BF16, "attT")
    ktn = c.sb([64, NH, TP], BF16, "ktn", dma=True)
    vbf = c.sb([128, 2, D], BF16, "vbf", dma=True)
    KTb = c.sb([64, SEQ], BF16, "KTb", dma=True)
    Vb = c.sb([128, SEQ // 128, DH], BF16, "Vb", dma=True)
    u_h = [c.sb([128, 128], F32, "u_t%d" % i) for i in range(2)]
    lp_h = [c.sb([128, 128], BF16, "lp_t%d" % i) for i in range(2)]
    at_h = [c.sb([128, 128], BF16, "at_t%d" % i) for i in range(2)]
    C32_h = [c.sb([128, 128], F32, "C32_%d" % i) for i in range(2)]
    c16_h = [c.sb([128, 128], BF16, "c16_%d" % i) for i in range(2)]
    u_t, lp_t, at_t, C32, c16 = u_h[0], lp_h[0], at_h[0], C32_h[0], c16_h[0]
    oacc = c.sb([64, 128], F32, "oacc")

    def wslab(W, r0, col0, ncols=256, kmode=128):
        t = wsl[wrot[0]]
        wrot[0] ^= 1
        if kmode == 128:
            src = W[r0:r0 + D, col0:col0 + ncols].rearrange("(kc p) n -> p kc n", p=128)
            c.dma('pool', lambda: nc.gpsimd.dma_start(out=t[:, 0:8, 0:ncols], in_=src), t, writes=[t])
        else:
            src = W[r0:r0 + D, col0:col0 + 128].rearrange("(h p) n -> p h n", p=64)
            dstv = t[0:64, :, :].rearrange("p k (a n) -> p (k a) n", n=128)
            c.dma('pool', lambda: nc.gpsimd.dma_start(out=dstv, in_=src), t, writes=[t])
        return t

    def A(l, k, ch):
        return l * 48 + k * 8 + ch

    if DO_PEER:
        CR = 2048
        for l in range(n_layers):
            for (src, dst) in ((exp_u, EU_d[l]), (exp_v, EV_d[l])):
                for r0 in range(0, NEXP, CR):
                    c.dma('pool', lambda src=src, dst=dst, r0=r0, l=l: nc.gpsimd.dma_start(
                        out=dst[r0:r0 + CR, :], in_=src[l * NEXP + r0:l * NEXP + r0 + CR, :]), dst, writes=[dst])

    cin = tm
    scT = c.sb([128, NCH, 17], BF16, "scT")
    c.dma('sp', lambda: nc.sync.dma_start(out=tm[0:17, 0, :], in_=c_all[:, :]), tm, writes=[tm])
    for ch in range(NCH):
        pst = nps()
        c.op('pe', lambda pst=pst, ch=ch: nc.tensor.transpose(out=pst[:, 0:17], in_=tm[0:17, 0, ch * 128:(ch + 1) * 128],
                                                               identity=ident[0:17, 0:17]),
             reads=[cin, ident], writes=[pst])
        c.op('act', lambda pst=pst, ch=ch: nc.scalar.activation(out=scT[:, ch, :], in_=pst[:, 0:17], func=AF.Silu),
             reads=[pst], writes=[scT])
    for l in range(n_layers):
        for sl in range(24):
            t = wslab(w_ada, l * D, sl * 256)
            for jj in range(2):
                j = sl * 2 + jj
                pst = nps()
                for kc in range(NCH):
                    c.op('pe', lambda pst=pst, t=t, kc=kc, jj=jj: nc.tensor.matmul(
                        pst[:, 0:17], lhsT=t[:, kc, jj * 128:(jj + 1) * 128], rhs=scT[:, kc, :],
                        start=(kc == 0), stop=(kc == NCH - 1)), reads=[t, scT], writes=[pst])
                c.op('dve', lambda pst=pst, l=l, j=j: nc.vector.tensor_scalar(
                    out=ada[:, l * 48 + j, :], in0=pst[:, 0:17], scalar1=bada_sb[:, l * 48 + j:l * 48 + j + 1], scalar2=1.0,
                    op0=ALU.add, op1=ALU.mult), reads=[pst, bada_sb], writes=[ada])
        for ch in range(NCH):
            for (k, nsb) in ((1, nmix_sb), (4, nffn_sb)):
                c.op('dve', lambda l=l, k=k, ch=ch, nsb=nsb: nc.vector.tensor_scalar(
                    out=ada[:, A(l, k, ch), :], in0=ada[:, A(l, k, ch), :], scalar1=1.0,
                    scalar2=nsb[:, l * NCH + ch:l * NCH + ch + 1], op0=ALU.add, op1=ALU.mult),
                    reads=[ada, nsb], writes=[ada])

    def v3(ap):
        return ap.rearrange("p (s t) -> p s t", t=DEC_T)

    def abc(idx):
        return ada[:, idx, 1:17].unsqueeze(2).to_broadcast([128, SPC, DEC_T])

    def modulate(out_ap, in_ap, aidx, bidx, N, samp, out_tk, in_tk):
        if not samp:
            c.op('dve', lambda: nc.vector.tensor_scalar(out=out_ap, in0=in_ap, scalar1=ada[:, aidx, 0:1],
                                                        scalar2=ada[:, bidx, 0:1], op0=ALU.mult, op1=ALU.add),
                 reads=[ada, in_tk], writes=[out_tk])
        else:
            c.op('dve', lambda: nc.vector.tensor_tensor(out=v3(tmpN[:, 0:N]), in0=v3(in_ap), in1=abc(aidx), op=ALU.mult),
                 reads=[ada, in_tk], writes=[tmpN])
            c.op('dve', lambda: nc.vector.tensor_tensor(out=v3(out_ap), in0=v3(tmpN[:, 0:N]), in1=abc(bidx), op=ALU.add),
                 reads=[ada, tmpN], writes=[out_tk])

    def load_x_tile(src_ap, N):
        nt = N // 128
        c.dma('sp', lambda: nc.sync.dma_start(out=tm[:, 0:nt, :], in_=src_ap.rearrange("(n p) d -> p n d", p=128)),
              tm, writes=[tm])
        for tb in range(nt):
            for ch in range(NCH):
                pst = nps()
                c.op('pe', lambda pst=pst, tb=tb, ch=ch: nc.tensor.transpose(
                    out=pst[:, 0:128], in_=tm[:, tb, ch * 128:(ch + 1) * 128], identity=ident[:, :]),
                    reads=[tm, ident], writes=[pst])
                c.op('act', lambda pst=pst, tb=tb, ch=ch: nc.scalar.copy(
                    out=xT[:, ch, tb * 128:(tb + 1) * 128], in_=pst[:, 0:128]),
                    reads=[pst], writes=[xT])

    def finish_rstd(pst, N, dst):
        c.op('dve', lambda: nc.vector.tensor_scalar(out=dst[:, 0:N], in0=pst[:, 0:N], scalar1=1.0 / D, scalar2=EPS,
                                                    op0=ALU.mult, op1=ALU.add), reads=[pst], writes=[dst])
        c.op('act', lambda: nc.scalar.activation(out=dst[:, 0:N], in_=dst[:, 0:N], func=AF.Sqrt),
             reads=[dst], writes=[dst])
        c.op('dve', lambda: nc.vector.reciprocal(out=dst[:, 0:N], in_=dst[:, 0:N]), reads=[dst], writes=[dst])

    def rms_stats(N):
        for ch in range(NCH):
            c.op('act', lambda ch=ch: nc.scalar.activation(out=scrA[:, ch, 0:N], in_=xT[:, ch, 0:N], func=AF.Square),
                 reads=[xT], writes=[scrA])
        pst = psum[4]
        for ch in range(NCH):
            c.op('pe', lambda ch=ch: nc.tensor.matmul(pst[:, 0:N], lhsT=ones[:, :], rhs=scrA[:, ch, 0:N],
                                                        start=(ch == 0), stop=(ch == NCH - 1)),
                 reads=[ones, scrA], writes=[pst])
        finish_rstd(pst, N, rstd)

    def modnorm(l, which, N, samp, keep_f32=False):
        rms_stats(N)
        ka, kb_ = (1, 0) if which == 1 else (4, 3)
        for ch in range(NCH):
            c.op('dve', lambda ch=ch: nc.vector.tensor_tensor(out=scrA[:, ch, 0:N], in0=xT[:, ch, 0:N], in1=rstd[:, 0:N],
                                                              op=ALU.mult), reads=[xT, rstd], writes=[scrA])
            if keep_f32:
                modulate(scrA[:, ch, 0:N], scrA[:, ch, 0:N], A(l, ka, ch), A(l, kb_, ch), N, samp, scrA, scrA)
                c.op('act', lambda ch=ch: nc.scalar.copy(out=hT[:, ch, 0:N], in_=scrA[:, ch, 0:N]),
                     reads=[scrA], writes=[hT])
            else:
                modulate(hT[:, ch, 0:N], scrA[:, ch, 0:N], A(l, ka, ch), A(l, kb_, ch), N, samp, hT, scrA)

    def resid_add(pst, ps_ap, bias_ap, gidx, ch, N, samp):
        if not samp:
            if bias_ap is not None:
                c.op('dve', lambda: nc.vector.tensor_scalar(out=tmpN[:, 0:N], in0=ps_ap, scalar1=bias_ap,
                                                            scalar2=ada[:, gidx, 0:1], op0=ALU.add, op1=ALU.mult),
                     reads=[ada, pst, bpw2_sb], writes=[tmpN])
            else:
                c.op('dve', lambda: nc.vector.tensor_scalar(out=tmpN[:, 0:N], in0=ps_ap, scalar1=ada[:, gidx, 0:1],
                                                            scalar2=1.0, op0=ALU.mult, op1=ALU.mult),
                     reads=[ada, pst], writes=[tmpN])
        else:
            if bias_ap is not None:
                c.op('dve', lambda: nc.vector.tensor_scalar(out=tmpM[:, 0:N], in0=ps_ap, scalar1=bias_ap, scalar2=1.0,
                                                            op0=ALU.add, op1=ALU.mult), reads=[pst, bpw2_sb], writes=[tmpM])
                c.op('dve', lambda: nc.vector.tensor_tensor(out=v3(tmpN[:, 0:N]), in0=v3(tmpM[:, 0:N]), in1=abc(gidx),
                                                            op=ALU.mult), reads=[ada, tmpM], writes=[tmpN])
            else:
                c.op('dve', lambda: nc.vector.tensor_tensor(out=v3(tmpN[:, 0:N]), in0=v3(ps_ap), in1=abc(gidx),
                                                            op=ALU.mult), reads=[ada, pst], writes=[tmpN])
        c.op('dve', lambda: nc.vector.tensor_tensor(out=xT[:, ch, 0:N], in0=xT[:, ch, 0:N], in1=tmpN[:, 0:N], op=ALU.add),
             reads=[xT, tmpN], writes=[xT])

    def conv_layer(l, N, samp, ti, last):
        modnorm(l, 1, N, samp)
        fo = FOFF[l]
        fl = fullbuf
        full_s = fullbuf
        fs4 = fullbuf[:, :, :].rearrange("p c (s w) -> p c s w", w=HIST + DEC_T)
        if not samp and ti == 0:
            c.op('pool', lambda: nc.gpsimd.memset(fl[:, :, fo:fo + HIST], 0.0), writes=[fl])
        if samp:
            for half in range(2):
                r0 = l * SPC * HIST + half * 240
                c.dma('sp', lambda r0=r0: nc.sync.dma_start(
                    out=tm[0:120, 0:2, :], in_=st_conv[r0:r0 + 240, :].rearrange("(n p) d -> p n d", p=120)),
                    tm, writes=[tm])
                for n in range(2):
                    for ch in range(NCH):
                        pst = nps()
                        c.op('pe', lambda pst=pst, n=n, ch=ch: nc.tensor.transpose(
                            out=pst[:, 0:120], in_=tm[0:120, n, ch * 128:(ch + 1) * 128], identity=ident[0:120, 0:120]),
                            reads=[tm, ident], writes=[pst])
                        s0 = (half * 2 + n) * 4
                        c.op('act', lambda pst=pst, ch=ch, s0=s0: nc.scalar.copy(
                            out=fs4[:, ch, s0:s0 + 4, 0:HIST],
                            in_=pst[:, 0:120].rearrange("p (s r) -> p s r", r=HIST)),
                            reads=[pst], writes=[full_s])
        for jp in range(4):
            ta = wslab(w_pw1, l * D, jp * 256)
            tb_ = wslab(w_pw1, l * D, D + jp * 256)
            for jj in range(2):
                j = jp * 2 + jj
                ps1 = nps()
                ps2 = nps()
                for (pst, t) in ((ps1, ta), (ps2, tb_)):
                    for kc in range(NCH):
                        c.op('pe', lambda pst=pst, t=t, kc=kc, jj=jj: nc.tensor.matmul(
                            pst[:, 0:N], lhsT=t[:, kc, jj * 128:(jj + 1) * 128], rhs=hT[:, kc, 0:N],
                            start=(kc == 0), stop=(kc == NCH - 1)), reads=[t, hT], writes=[pst])
                c.op('act', lambda ps2=ps2, j=j: nc.scalar.activation(
                    out=tmpM[:, 0:N], in_=ps2[:, 0:N], func=AF.Sigmoid,
                    bias=bpw1_sb[:, l * 16 + 8 + j:l * 16 + 8 + j + 1], scale=1.0),
                    reads=[ps2, bpw1_sb], writes=[tmpM])
                if not samp:
                    c.op('dve', lambda ps1=ps1, j=j: nc.vector.scalar_tensor_tensor(
                        out=fl[:, j, fo + HIST:fo + HIST + N], in0=ps1[:, 0:N], scalar=bpw1_sb[:, l * 16 + j:l * 16 + j + 1],
                        in1=tmpM[:, 0:N], op0=ALU.add, op1=ALU.mult), reads=[ps1, bpw1_sb, tmpM], writes=[fl])
                else:
                    c.op('dve', lambda ps1=ps1, j=j: nc.vector.scalar_tensor_tensor(
                        out=scrA[:, j, 0:N], in0=ps1[:, 0:N], scalar=bpw1_sb[:, l * 16 + j:l * 16 + j + 1],
                        in1=tmpM[:, 0:N], op0=ALU.add, op1=ALU.mult), reads=[ps1, bpw1_sb, tmpM], writes=[scrA])
                    c.op('act', lambda j=j: nc.scalar.copy(out=fs4[:, j, :, HIST:HIST + DEC_T],
                                                           in_=v3(scrA[:, j, 0:N])), reads=[scrA], writes=[full_s])
        if samp:
            for ch in range(NCH):
                pst = nps()
                c.op('pe', lambda pst=pst, ch=ch: nc.tensor.transpose(out=pst[:, 0:128], in_=scrA[:, ch, 0:128],
                                                                       identity=ident[:, :]),
                     reads=[scrA, ident], writes=[pst])
                c.op('act', lambda pst=pst, ch=ch: nc.scalar.copy(out=tm[:, 0, ch * 128:(ch + 1) * 128], in_=pst[:, 0:128]),
                     reads=[pst], writes=[tm])
            r0 = l * SPC * HIST
            dst = conv_s[r0:r0 + SPC * HIST, :].rearrange("(s r) d -> s r d", r=HIST)
            src = st_conv[r0:r0 + SPC * HIST, :].rearrange("(s r) d -> s r d", r=HIST)
            for s in range(SPC):
                c.dma('sp', lambda s=s: nc.sync.dma_start(out=dst[s, HIST - DEC_T:HIST, :], in_=tm[s * DEC_T:(s + 1) * DEC_T, 0, :]),
                      tm, reads=[tm], out_dram=True)
            c.dma('sp', lambda: nc.sync.dma_start(out=dst[:, 0:HIST - DEC_T, :], in_=src[:, DEC_T:HIST, :]),
                  tm, reads=[], out_dram=True)
        elif last:
            for ch in range(NCH):
                pst = nps()
                c.op('pe', lambda pst=pst, ch=ch: nc.tensor.transpose(out=pst[0:HIST, 0:128], in_=fl[:, ch, fo + N:fo + N + HIST],
                                                                       identity=ident[:, :]),
                     reads=[fl, ident], writes=[pst])
                c.op('act', lambda pst=pst, ch=ch: nc.scalar.copy(out=tm[0:HIST, 0, ch * 128:(ch + 1) * 128],
                                                                  in_=pst[0:HIST, 0:128]), reads=[pst], writes=[tm])
            c.dma('sp', lambda: nc.sync.dma_start(out=conv_p[l * HIST:(l + 1) * HIST, :], in_=tm[0:HIST, 0, :]),
                  tm, reads=[tm], out_dram=True)
        src_t = full_s if samp else fl
        for w in range(CW):
            for j in range(NCH):
                widx = (l * NCH + j) * CW + w
                if samp:
                    in0 = fs4[:, j, :, w:w + DEC_T]
                    yv = v3(yb[j][:, 0:N])
                else:
                    in0 = fl[:, j, fo + w:fo + w + N]
                    yv = yb[j][:, 0:N]
                if w == 0:
                    c.op('dve', lambda in0=in0, yv=yv, widx=widx, j=j: nc.vector.tensor_scalar(
                        out=yv, in0=in0, scalar1=wdw_sb[:, widx:widx + 1], scalar2=bdw_sb[:, l * NCH + j:l * NCH + j + 1],
                        op0=ALU.mult, op1=ALU.add), reads=[src_t, wdw_sb, bdw_sb], writes=[yb[j]])
                else:
                    c.op('dve', lambda in0=in0, yv=yv, widx=widx: nc.vector.scalar_tensor_tensor(
                        out=yv, in0=in0, scalar=wdw_sb[:, widx:widx + 1], in1=yv, op0=ALU.mult, op1=ALU.add),
                        reads=[src_t, wdw_sb, yb[j]], writes=[yb[j]])
        if not samp and not last:
            c.op('pool', lambda: nc.gpsimd.tensor_copy(out=fl[:, :, fo:fo + HIST], in_=fl[:, :, fo + N:fo + N + HIST]),
                 reads=[fl], writes=[fl])
        for j in range(NCH):
            c.op('act', lambda j=j: nc.scalar.activation(out=scrA[:, j, 0:N], in_=yb[j][:, 0:N], func=AF.Square),
                 reads=[yb[j]], writes=[scrA])
        pm, pq = psum[4], psum[5]
        for j in range(NCH):
            c.op('pe', lambda j=j: nc.tensor.matmul(pm[:, 0:N], lhsT=ones[:, :], rhs=yb[j][:, 0:N],
                                                      start=(j == 0), stop=(j == NCH - 1)), reads=[ones, yb[j]], writes=[pm])
        for j in range(NCH):
            c.op('pe', lambda j=j: nc.tensor.matmul(pq[:, 0:N], lhsT=ones[:, :], rhs=scrA[:, j, 0:N],
                                                      start=(j == 0), stop=(j == NCH - 1)), reads=[ones, scrA], writes=[pq])
        c.op('dve', lambda: nc.vector.tensor_scalar(out=tmpM[:, 0:N], in0=pm[:, 0:N], scalar1=1.0 / D, scalar2=1.0,
                                                    op0=ALU.mult, op1=ALU.mult), reads=[pm], writes=[tmpM])
        c.op('dve', lambda: nc.vector.tensor_tensor(out=tmpN[:, 0:N], in0=tmpM[:, 0:N], in1=tmpM[:, 0:N], op=ALU.mult),
             reads=[tmpM], writes=[tmpN])
        c.op('dve', lambda: nc.vector.scalar_tensor_tensor(out=rstd[:, 0:N], in0=pq[:, 0:N], scalar=1.0 / D, in1=tmpN[:, 0:N],
                                                           op0=ALU.mult, op1=ALU.subtract), reads=[pq, tmpN], writes=[rstd])
        c.op('dve', lambda: nc.vector.tensor_scalar(out=rstd[:, 0:N], in0=rstd[:, 0:N], scalar1=EPS, scalar2=1.0,
                                                    op0=ALU.add, op1=ALU.mult), reads=[rstd], writes=[rstd])
        c.op('act', lambda: nc.scalar.activation(out=rstd[:, 0:N], in_=rstd[:, 0:N], func=AF.Sqrt), reads=[rstd], writes=[rstd])
        c.op('dve', lambda: nc.vector.reciprocal(out=rstd[:, 0:N], in_=rstd[:, 0:N]), reads=[rstd], writes=[rstd])
        for j in range(NCH):
            c.op('dve', lambda j=j: nc.vector.tensor_tensor(out=scrA[:, j, 0:N], in0=yb[j][:, 0:N], in1=tmpM[:, 0:N],
                                                            op=ALU.subtract), reads=[yb[j], tmpM], writes=[scrA])
            c.op('dve', lambda j=j: nc.vector.tensor_tensor(out=scrA[:, j, 0:N], in0=scrA[:, j, 0:N], in1=rstd[:, 0:N],
                                                            op=ALU.mult), reads=[scrA, rstd], writes=[scrA])
            c.op('act', lambda j=j: nc.scalar.activation(out=hT[:, j, 0:N], in_=scrA[:, j, 0:N], func=AF.Silu,
                                                         bias=lnb_sb[:, l * NCH + j:l * NCH + j + 1],
                                                         scale=lng_sb[:, l * NCH + j:l * NCH + j + 1]),
                 reads=[scrA, lnb_sb, lng_sb], writes=[hT])
        for sl in range(4):
            t = wslab(w_pw2, l * D, sl * 256)
            for jj in range(2):
                ch = sl * 2 + jj
                pst = nps()
                for kc in range(NCH):
                    c.op('pe', lambda pst=pst, t=t, kc=kc, jj=jj: nc.tensor.matmul(
                        pst[:, 0:N], lhsT=t[:, kc, jj * 128:(jj + 1) * 128], rhs=hT[:, kc, 0:N],
                        start=(kc == 0), stop=(kc == NCH - 1)), reads=[t, hT], writes=[pst])
                resid_add(pst, pst[:, 0:N], bpw2_sb[:, l * NCH + ch:l * NCH + ch + 1], A(l, 2, ch), ch, N, samp)


    skb = c.sb([128, 16, 128], BF16, "skb", dma=True)
    sc = c.sb([128, 2048], F32, "sc")
    sv = c.sb([128, 16, 16], F32, "sv")
    si = c.sb([128, 16, 16], U32, "si")
    sif = c.sb([128, 16, 16], F32, "sif")
    cand = c.sb([128, PH, 256], F32, "cand")
    cv = c.sb([128, PH, 16], F32, "cv")
    cp = c.sb([128, PH, 16], U32, "cp")
    iiu = c.sb([128, PH, 16], U32, "iiu")
    jju = c.sb([128, PH, 16], U32, "jju")
    iif = c.sb([128, PH, 16], F32, "iif")
    jjf = c.sb([128, PH, 16], F32, "jjf")
    selI = c.sb([128, PH, 16], F32, "selI")
    selJ = c.sb([128, PH, 16], F32, "selJ")
    ef = c.sb([128, 128], F32, "ef")
    eidx = c.sb([128, 128], I32, "eidx")
    negm = c.sb([128, PH], F32, "negm")
    ee = c.sb([128, PH, 16], F32, "ee")
    zz = c.sb([128, PH], F32, "zz")
    gg = c.sb([128, PH, 16], F32, "gg")
    NUG = 6
    ug = [c.sb([128, D], BF16, "ug%d" % i, dma=True) for i in range(NUG)]
    h2b = c.sb([128, D], BF16, "h2b")

    junk = [c.sb([128, D], BF16, "junk0")] * 2
    araw = c.sb([128, 128], F32, "araw")
    gt1 = c.sb([128, 128], F32, "gt1")
    wgt = c.sb([128, 128], F32, "wgt")
    acc = c.sb([128, D], F32, "acc")

    def bc4(ap3, axis):
        return ap3.unsqueeze(axis).to_broadcast([128, PH, 16, 16])

    def peer_layer(l, N, samp):
        nt = N // 128
        modnorm(l, 2, N, samp, keep_f32=True)
        for sl in range(8):
            t = wslab(w_pq, l * D, sl * 256)
            for jj in range(2):
                hp = sl * 2 + jj
                pst = nps()
                for kc in range(NCH):
                    c.op('pe', lambda pst=pst, t=t, kc=kc, jj=jj: nc.tensor.matmul(
                        pst[:, 0:N], lhsT=t[:, kc, jj * 128:(jj + 1) * 128], rhs=hT[:, kc, 0:N],
                        start=(kc == 0), stop=(kc == NCH - 1)), reads=[t, hT], writes=[pst])
                c.op('act', lambda pst=pst, hp=hp: nc.scalar.copy(out=qT[:, hp, 0:N], in_=pst[:, 0:N]),
                     reads=[pst], writes=[qT])
        r0 = l * 16 * 128
        c.dma('pool', lambda: nc.gpsimd.dma_start(out=skb[:, :, :],
                                                  in_=skT[r0:r0 + 2048, :].rearrange("(hp d) k -> d hp k", d=128)),
              skb, writes=[skb])
        for tb in range(nt):
            cols = slice(tb * 128, (tb + 1) * 128)
            for ch in range(NCH):
                pst = psum[5 + ch % 2]
                c.op('pe', lambda pst=pst, ch=ch, cols=cols: nc.tensor.transpose(out=pst[:, 0:128], in_=scrA[:, ch, cols],
                                                                       identity=ident[:, :]),
                     reads=[scrA, ident], writes=[pst])
                c.op('act', lambda pst=pst, ch=ch: nc.scalar.copy(out=h2b[:, ch * 128:(ch + 1) * 128], in_=pst[:, 0:128]),
                     reads=[pst], writes=[h2b])
            for bq in range(4):
                pst = psum[bq]
                for i in range(4):
                    hp = bq * 4 + i
                    c.op('pe', lambda pst=pst, i=i, hp=hp, cols=cols: nc.tensor.matmul(
                        pst[:, i * 128:(i + 1) * 128], lhsT=qT[:, hp, cols], rhs=skb[:, hp, :], start=True, stop=True),
                        reads=[qT, skb], writes=[pst])
                c.op('act', lambda pst=pst, bq=bq: nc.scalar.copy(out=sc[:, bq * 512:(bq + 1) * 512], in_=pst[:, :]),
                     reads=[pst], writes=[sc])
            for hp in range(16):
                scv = sc[:, hp * 128:(hp + 1) * 128]
                c.op('dve', lambda hp=hp, scv=scv: nc.vector.max(out=sv[:, hp, 0:8], in_=scv), reads=[sc], writes=[sv])
                c.op('dve', lambda hp=hp, scv=scv: nc.vector.max_index(out=si[:, hp, 0:8], in_max=sv[:, hp, 0:8], in_values=scv),
                     reads=[sc, sv], writes=[si])
                c.op('dve', lambda hp=hp, scv=scv: nc.vector.match_replace(out=scv, in_to_replace=sv[:, hp, 0:8],
                                                                           in_values=scv, imm_value=-1e30),
                     reads=[sc, sv], writes=[sc])
                c.op('dve', lambda hp=hp, scv=scv: nc.vector.max(out=sv[:, hp, 8:16], in_=scv), reads=[sc], writes=[sv])
                c.op('dve', lambda hp=hp, scv=scv: nc.vector.max_index(out=si[:, hp, 8:16], in_max=sv[:, hp, 8:16], in_values=scv),
                     reads=[sc, sv], writes=[si])
            c.op('dve', lambda: nc.vector.tensor_copy(out=sif[:, :, :], in_=si[:, :, :]), reads=[si], writes=[sif])
            sv4 = sv[:, :, :].rearrange("p (h t) k -> p h t k", t=2)
            sif4 = sif[:, :, :].rearrange("p (h t) k -> p h t k", t=2)
            cand4 = cand[:, :, :].rearrange("p h (i j) -> p h i j", j=16)
            c.op('dve', lambda: nc.vector.tensor_tensor(out=cand4, in0=bc4(sv4[:, :, 0, :], 3), in1=bc4(sv4[:, :, 1, :], 2),
                                                        op=ALU.add), reads=[sv], writes=[cand])
            for h in range(PH):
                cdv = cand[:, h, :]
                c.op('dve', lambda h=h, cdv=cdv: nc.vector.max(out=cv[:, h, 0:8], in_=cdv), reads=[cand], writes=[cv])
                c.op('dve', lambda h=h, cdv=cdv: nc.vector.max_index(out=cp[:, h, 0:8], in_max=cv[:, h, 0:8], in_values=cdv),
                     reads=[cand, cv], writes=[cp])
                c.op('dve', lambda h=h, cdv=cdv: nc.vector.match_replace(out=cdv, in_to_replace=cv[:, h, 0:8],
                                                                         in_values=cdv, imm_value=-1e30),
                     reads=[cand, cv], writes=[cand])
                c.op('dve', lambda h=h, cdv=cdv: nc.vector.max(out=cv[:, h, 8:16], in_=cdv), reads=[cand], writes=[cv])
                c.op('dve', lambda h=h, cdv=cdv: nc.vector.max_index(out=cp[:, h, 8:16], in_max=cv[:, h, 8:16], in_values=cdv),
                     reads=[cand, cv], writes=[cp])
            c.op('dve', lambda: nc.vector.tensor_single_scalar(out=iiu[:, :, :], in_=cp[:, :, :], scalar=4,
                                                               op=ALU.logical_shift_right), reads=[cp], writes=[iiu])
            c.op('dve', lambda: nc.vector.tensor_single_scalar(out=jju[:, :, :], in_=cp[:, :, :], scalar=15,
                                                               op=ALU.bitwise_and), reads=[cp], writes=[jju])
            c.op('dve', lambda: nc.vector.tensor_copy(out=iif[:, :, :], in_=iiu[:, :, :]), reads=[iiu], writes=[iif])
            c.op('dve', lambda: nc.vector.tensor_copy(out=jjf[:, :, :], in_=jju[:, :, :]), reads=[jju], writes=[jjf])
            eq4 = sc[:, :].rearrange("p (h k i) -> p h k i", k=16, i=16)
            io4 = iota16[:, :].unsqueeze(1).unsqueeze(1).to_broadcast([128, PH, 16, 16])
            for (xf, tsel, sel) in ((iif, 0, selI), (jjf, 1, selJ)):
                c.op('dve', lambda xf=xf: nc.vector.tensor_tensor(out=eq4, in0=bc4(xf[:, :, :], 3), in1=io4, op=ALU.is_equal),
                     reads=[xf, iota16], writes=[sc])
                c.op('dve', lambda tsel=tsel: nc.vector.tensor_tensor(out=eq4, in0=eq4, in1=bc4(sif4[:, :, tsel, :], 2),
                                                                      op=ALU.mult), reads=[sc, sif], writes=[sc])
                c.op('dve', lambda sel=sel: nc.vector.tensor_reduce(out=sel[:, :, :], in_=eq4, axis=AX.X, op=ALU.add),
                     reads=[sc], writes=[sel])
            efv = ef[:, :].rearrange("p (h k) -> p h k", k=16)
            c.op('dve', lambda: nc.vector.scalar_tensor_tensor(out=ef[:, :], in0=selI[:, :, :].rearrange("p h k -> p (h k)"),
                                                               scalar=128.0, in1=selJ[:, :, :].rearrange("p h k -> p (h k)"),
                                                               op0=ALU.mult, op1=ALU.add), reads=[selI, selJ], writes=[ef])
            c.op('dve', lambda: nc.vector.tensor_copy(out=eidx[:, :], in_=ef[:, :]), reads=[ef], writes=[eidx])
            c.op('dve', lambda: nc.vector.tensor_scalar(out=negm[:, :], in0=cv[:, :, 0], scalar1=-1.0, scalar2=1.0,
                                                        op0=ALU.mult, op1=ALU.mult), reads=[cv], writes=[negm])
            for h in range(PH):
                c.op('act', lambda h=h: nc.scalar.activation(out=ee[:, h, :], in_=cv[:, h, :], func=AF.Exp,
                                                             bias=negm[:, h:h + 1], scale=1.0),
                     reads=[cv, negm], writes=[ee])
            c.op('dve', lambda: nc.vector.tensor_reduce(out=zz[:, :], in_=ee[:, :, :], axis=AX.X, op=ALU.add),
                 reads=[ee], writes=[zz])
            c.op('dve', lambda: nc.vector.reciprocal(out=zz[:, :], in_=zz[:, :]), reads=[zz], writes=[zz])
            c.op('dve', lambda: nc.vector.tensor_tensor(out=gg[:, :, :], in0=ee[:, :, :],
                                                        in1=zz[:, :].unsqueeze(2).to_broadcast([128, PH, 16]), op=ALU.mult),
                 reads=[ee, zz], writes=[gg])
            for hk in range(128):
                b_ = ug[hk % NUG]
                c.dma('pool', lambda b_=b_, hk=hk: nc.gpsimd.indirect_dma_start(
                    out=b_[:, :], out_offset=None, in_=EU_d[l][:, :],
                    in_offset=bass.IndirectOffsetOnAxis(ap=eidx[:, hk:hk + 1], axis=0)), b_, reads=[eidx, EU_d[l]], writes=[b_])
                jk = junk[hk % 2]
                c.op('dve', lambda b_=b_, hk=hk, jk=jk: nc.vector.scalar_tensor_tensor(
                    out=jk[:, :], in0=b_[:, :], scalar=1.0, in1=h2b[:, :], op0=ALU.mult, op1=ALU.mult,
                    accum_out=araw[:, hk:hk + 1]), reads=[b_, h2b], writes=[jk, araw])
            c.op('dve', lambda: nc.vector.tensor_tensor(out=gt1[:, :], in0=araw[:, :], in1=araw[:, :], op=ALU.mult),
                 reads=[araw], writes=[gt1])
            c.op('dve', lambda: nc.vector.tensor_scalar(out=gt1[:, :], in0=gt1[:, :], scalar1=0.044715, scalar2=1.0,
                                                        op0=ALU.mult, op1=ALU.add), reads=[gt1], writes=[gt1])
            c.op('dve', lambda: nc.vector.tensor_tensor(out=gt1[:, :], in0=gt1[:, :], in1=araw[:, :], op=ALU.mult),
                 reads=[gt1, araw], writes=[gt1])
            c.op('act', lambda: nc.scalar.activation(out=gt1[:, :], in_=gt1[:, :], func=AF.Sigmoid, scale=1.5957691216),
                 reads=[gt1], writes=[gt1])
            c.op('dve', lambda: nc.vector.tensor_tensor(out=wgt[:, :], in0=gt1[:, :], in1=araw[:, :], op=ALU.mult),
                 reads=[gt1, araw], writes=[wgt])
            c.op('dve', lambda: nc.vector.tensor_tensor(out=wgt[:, :], in0=wgt[:, :],
                                                        in1=gg[:, :, :].rearrange("p h k -> p (h k)"), op=ALU.mult),
                 reads=[wgt, gg], writes=[wgt])
            for hk in range(128):
                b_ = ug[(hk + 3) % NUG]
                c.dma('pool', lambda b_=b_, hk=hk: nc.gpsimd.indirect_dma_start(
                    out=b_[:, :], out_offset=None, in_=EV_d[l][:, :],
                    in_offset=bass.IndirectOffsetOnAxis(ap=eidx[:, hk:hk + 1], axis=0)), b_, reads=[eidx, EV_d[l]], writes=[b_])
                if hk == 0:
                    c.op('dve', lambda b_=b_, hk=hk: nc.vector.tensor_scalar(out=acc[:, :], in0=b_[:, :], scalar1=wgt[:, hk:hk + 1],
                                                                             scalar2=1.0, op0=ALU.mult, op1=ALU.mult),
                         reads=[b_, wgt], writes=[acc])
                else:
                    c.op('dve', lambda b_=b_, hk=hk: nc.vector.scalar_tensor_tensor(
                        out=acc[:, :], in0=b_[:, :], scalar=wgt[:, hk:hk + 1], in1=acc[:, :], op0=ALU.mult, op1=ALU.add),
                        reads=[b_, wgt, acc], writes=[acc])
            for ch in range(NCH):
                pst = psum[5 + ch % 2]
                c.op('pe', lambda pst=pst, ch=ch: nc.tensor.transpose(out=pst[:, 0:128], in_=acc[:, ch * 128:(ch + 1) * 128],
                                                                       identity=ident[:, :]),
                     reads=[acc, ident], writes=[pst])
                gidx = A(l, 5, ch)
                if not samp:
                    c.op('dve', lambda pst=pst, ch=ch, gidx=gidx, cols=cols: nc.vector.scalar_tensor_tensor(
                        out=xT[:, ch, cols], in0=pst[:, 0:128], scalar=ada[:, gidx, 0:1], in1=xT[:, ch, cols],
                        op0=ALU.mult, op1=ALU.add), reads=[pst, ada, xT], writes=[xT])
                else:
                    c.op('dve', lambda pst=pst, gidx=gidx: nc.vector.tensor_tensor(
                        out=v3(tmpN[:, 0:128]), in0=v3(pst[:, 0:128]), in1=abc(gidx), op=ALU.mult),
                        reads=[pst, ada], writes=[tmpN])
                    c.op('dve', lambda ch=ch: nc.vector.tensor_tensor(out=xT[:, ch, 0:128], in0=xT[:, ch, 0:128],
                                                                      in1=tmpN[:, 0:128], op=ALU.add),
                         reads=[xT, tmpN], writes=[xT])


    ptab_sb = c.sb([128, SPC * NPAGES], I32, "ptab_sb", dma=True)
    ptf = c.sb([128, SPC * NPAGES], F32, "ptf")
    idxpg = c.sb([128, SPC * NPAGES], I32, "idxpg")
    if do_sample:
        c.dma('sp', lambda: nc.sync.dma_start(out=ptab_sb[:, :], in_=ptab[:, :]), ptab_sb, writes=[ptab_sb])
        c.op('dve', lambda: nc.vector.tensor_copy(out=ptf[:, :], in_=ptab_sb[:, :]), reads=[ptab_sb], writes=[ptf])
        c.op('dve', lambda: nc.vector.tensor_scalar(out=ptf[:, :], in0=ptf[:, :], scalar1=float(PAGE), scalar2=iotap[:, 0:1],
                                                    op0=ALU.mult, op1=ALU.add), reads=[ptf, iotap], writes=[ptf])
        c.op('dve', lambda: nc.vector.tensor_copy(out=idxpg[:, :], in_=ptf[:, :]), reads=[ptf], writes=[idxpg])

    def kv_project(N, samp, t0):
        nt = N // 128
        rms_stats(N)
        for ch in range(NCH):
            c.op('dve', lambda ch=ch: nc.vector.scalar_tensor_tensor(
                out=hT[:, ch, 0:N], in0=xT[:, ch, 0:N], scalar=nkv_sb[:, ch:ch + 1], in1=rstd[:, 0:N],
                op0=ALU.mult, op1=ALU.mult), reads=[xT, nkv_sb, rstd], writes=[hT])
        for (W, outd, is_v) in ((w_k, (k_s if samp else k_p), False), (w_v, (v_s if samp else v_p), True)):
            if DBG_KV < 2 or (is_v and DBG_KV < 3):
                continue
            for sl in range(4):
                t = wslab(W, 0, sl * 256)
                for tb in range(nt):
                    pst = nps()
                    for kc in range(NCH):
                        c.op('pe', lambda pst=pst, t=t, kc=kc, tb=tb: nc.tensor.matmul(
                            pst[:, 0:256], lhsT=hT[:, kc, tb * 128:(tb + 1) * 128], rhs=t[:, kc, 0:256],
                            start=(kc == 0), stop=(kc == NCH - 1)), reads=[t, hT], writes=[pst])
                    c.op('act', lambda pst=pst, tb=tb, sl=sl: nc.scalar.copy(out=tm[:, tb, sl * 256:(sl + 1) * 256], in_=pst[:, 0:256]),
                         reads=[pst], writes=[tm])
                    if is_v:
                        c.op('dve', lambda tb=tb, sl=sl: nc.vector.tensor_copy(
                            out=vbf[:, tb, sl * 256:(sl + 1) * 256], in_=tm[:, tb, sl * 256:(sl + 1) * 256]), reads=[tm], writes=[vbf])
            c.dma('sp', lambda outd=outd: nc.sync.dma_start(out=outd[t0:t0 + N, :].rearrange("(n p) d -> p n d", p=128),
                                                            in_=tm[:, 0:nt, :]), tm, reads=[tm], out_dram=True)
            if is_v:
                if samp:
                    c.dma('sp', lambda: nc.sync.dma_start(out=VS_d[0:128, :], in_=vbf[:, 0, :]), vbf, reads=[vbf], writes=[VS_d])
                else:
                    c.dma('sp', lambda: nc.sync.dma_start(out=V_d[t0:t0 + N, :].rearrange("(n p) d -> p n d", p=128),
                                                          in_=vbf[:, 0:nt, :]), vbf, reads=[vbf], writes=[V_d])
        if DBG_KV < 4:
            return
        for sl in range(4):
            t = wslab(w_k, 0, sl * 256)
            for hh in range(4):
                h = sl * 4 + hh
                pst = nps()
                for kc in range(NCH):
                    c.op('pe', lambda pst=pst, t=t, kc=kc, hh=hh: nc.tensor.matmul(
                        pst[0:64, 0:N], lhsT=t[:, kc, hh * 64:(hh + 1) * 64], rhs=hT[:, kc, 0:N],
                        start=(kc == 0), stop=(kc == NCH - 1)), reads=[t, hT], writes=[pst])
                c.op('act', lambda pst=pst, h=h: nc.scalar.copy(out=ktn[0:64, h, 0:N], in_=pst[0:64, 0:N]),
                     reads=[pst], writes=[ktn])
        if not samp:
            c.dma('sp', lambda: nc.sync.dma_start(out=KT_d[:, :, :].rearrange("h p t -> p h t")[:, :, t0:t0 + N],
                                                  in_=ktn[0:64, :, 0:N]), ktn, reads=[ktn], writes=[KT_d])

    def q_project(j, N, samp):
        l = NA + j
        modnorm(l, 1, N, samp)
        for sl in range(4):
            t = wslab(w_q, j * D, sl * 256)
            for hh in range(4):
                h = sl * 4 + hh
                pst = nps()
                for kc in range(NCH):
                    c.op('pe', lambda pst=pst, t=t, kc=kc, hh=hh: nc.tensor.matmul(
                        pst[0:64, 0:N], lhsT=t[:, kc, hh * 64:(hh + 1) * 64], rhs=hT[:, kc, 0:N],
                        start=(kc == 0), stop=(kc == NCH - 1)), reads=[t, hT], writes=[pst])
                c.op('act', lambda pst=pst, h=h: nc.scalar.activation(out=qT[0:64, h, 0:N], in_=pst[0:64, 0:N],
                                                                      func=AF.Copy, scale=0.125),
                     reads=[pst], writes=[qT])

    def o_project(j, N, samp):
        l = NA + j
        for ch in range(NCH):
            t = wslab(w_o, j * D, ch * 128, kmode=64)
            tv = t[0:64, :, :].rearrange("p k (a n) -> p (k a) n", n=128)
            pst = nps()
            for h in range(NH):
                c.op('pe', lambda pst=pst, tv=tv, h=h: nc.tensor.matmul(
                    pst[:, 0:N], lhsT=tv[:, h, :], rhs=attT[0:64, h, 0:N], start=(h == 0), stop=(h == NH - 1)),
                    reads=[t, attT], writes=[pst])
            resid_add(pst, pst[:, 0:N], None, A(l, 2, ch), ch, N, samp)

    def csum_update(first, nk, W):
        if first:
            c.op('dve', lambda: nc.vector.tensor_copy(out=C32[0:nk, 0:W], in_=lp_t[0:nk, 0:W]), reads=[lp_t], writes=[C32])
        else:
            c.op('dve', lambda: nc.vector.tensor_tensor(out=C32[0:nk, 0:W], in0=C32[0:nk, 0:W], in1=lp_t[0:nk, 0:W], op=ALU.add),
                 reads=[C32, lp_t], writes=[C32])
        c.op('dve', lambda: nc.vector.tensor_copy(out=c16[:, 0:W], in_=C32[:, 0:W]), reads=[C32], writes=[c16])

    def attention_prompt(j, N, t0):
        q_project(j, N, False)
        if DBG_ATT < 3:
            return
        nb = (t0 + N) // 128
        kb0 = t0 // 128
        nhalf = N // 128
        PSZ = [psum[4], psum[5]]
        PSA = [psum[6], psum[7]]
        PSO = [psum[2], psum[3]]
        for h in range(NH):
            c.dma('sp', lambda h=h: nc.sync.dma_start(out=KTb[0:64, 0:t0 + N], in_=KT_d[h, :, 0:t0 + N]),
                  KTb, reads=[KT_d], writes=[KTb])
            c.dma('sp', lambda h=h: nc.sync.dma_start(
                out=Vb[:, 0:nb, :], in_=V_d[0:t0 + N, h * DH:(h + 1) * DH].rearrange("(kb p) d -> p kb d", p=128)),
                Vb, reads=[V_d], writes=[Vb])
            bcol = j * NH + h
            started = [False] * nhalf
            for kb in reversed(range(nb)):
                ks = slice(kb * 128, (kb + 1) * 128)
                for hf in range(nhalf):
                    r = kb - kb0 - hf
                    if r > 0:
                        continue
                    first = not started[hf]
                    started[hf] = True
                    qs = slice(hf * 128, (hf + 1) * 128)
                    psz, psa, pso = PSZ[hf], PSA[hf], PSO[hf]
                    ut, lpt, att, cc32, cc16 = u_h[hf], lp_h[hf], at_h[hf], C32_h[hf], c16_h[hf]
                    c.op('pe', lambda psz=psz, ks=ks, h=h, qs=qs: nc.tensor.matmul(
                        psz[:, 0:128], lhsT=KTb[0:64, ks], rhs=qT[0:64, h, qs], start=True, stop=True),
                        reads=[KTb, qT], writes=[psz])
                    c.op('act', lambda psz=psz, ut=ut, bcol=bcol: nc.scalar.activation(
                        out=ut[:, :], in_=psz[:, 0:128], func=AF.Exp, bias=bsb_sb[:, bcol:bcol + 1], scale=1.0),
                        reads=[psz, bsb_sb], writes=[ut])
                    c.op('act', lambda ut=ut, lpt=lpt: nc.scalar.activation(out=lpt[:, :], in_=ut[:, :], func=AF.Ln,
                                                                            bias=1.0, scale=1.0), reads=[ut], writes=[lpt])
                    if r == 0:
                        c.op('pool', lambda lpt=lpt: nc.gpsimd.affine_select(
                            out=lpt[:, :], in_=lpt[:, :], pattern=[[1, 128]], compare_op=ALU.is_gt, fill=0.0, base=0,
                            channel_multiplier=-1), reads=[lpt], writes=[lpt])
                    c.op('pe', lambda psa=psa, ks=ks, h=h, qs=qs: nc.tensor.matmul(
                        psa[:, 0:128], lhsT=KTb[0:64, ks], rhs=qT[0:64, h, qs], start=True, stop=False),
                        reads=[KTb, qT], writes=[psa])
                    c.op('pe', lambda psa=psa, lpt=lpt, first=first: nc.tensor.matmul(
                        psa[:, 0:128], lhsT=negtri[:, :], rhs=lpt[:, :], start=False, stop=first),
                        reads=[negtri, lpt], writes=[psa])
                    if not first:
                        c.op('pe', lambda psa=psa, cc16=cc16: nc.tensor.matmul(
                            psa[:, 0:128], lhsT=negones[:, :], rhs=cc16[:, :], start=False, stop=True),
                            reads=[negones, cc16], writes=[psa])
                    c.op('act', lambda psa=psa, att=att, bcol=bcol: nc.scalar.activation(
                        out=att[:, :], in_=psa[:, 0:128], func=AF.Exp, bias=bsb_sb[:, bcol:bcol + 1], scale=1.0),
                        reads=[psa, bsb_sb], writes=[att])
                    if r == 0:
                        c.op('pool', lambda att=att: nc.gpsimd.affine_select(
                            out=att[:, :], in_=att[:, :], pattern=[[1, 128]], compare_op=ALU.is_gt, fill=0.0, base=0,
                            channel_multiplier=-1), reads=[att], writes=[att])
                    c.op('pe', lambda pso=pso, kb=kb, att=att, first=first: nc.tensor.matmul(
                        pso[0:64, 0:128], lhsT=Vb[:, kb, :], rhs=att[:, :], start=first, stop=(kb == 0)),
                        reads=[Vb, att], writes=[pso])
                    if kb > 0:
                        if first:
                            c.op('dve', lambda cc32=cc32, lpt=lpt: nc.vector.tensor_copy(out=cc32[:, :], in_=lpt[:, :]),
                                 reads=[lpt], writes=[cc32])
                        else:
                            c.op('dve', lambda cc32=cc32, lpt=lpt: nc.vector.tensor_tensor(
                                out=cc32[:, :], in0=cc32[:, :], in1=lpt[:, :], op=ALU.add), reads=[cc32, lpt], writes=[cc32])
                        c.op('dve', lambda cc32=cc32, cc16=cc16: nc.vector.tensor_copy(out=cc16[:, :], in_=cc32[:, :]),
                             reads=[cc32], writes=[cc16])
            for hf in range(nhalf):
                c.op('act', lambda h=h, hf=hf: nc.scalar.copy(out=attT[0:64, h, hf * 128:(hf + 1) * 128],
                                                              in_=PSO[hf][0:64, 0:128]), reads=[PSO[hf]], writes=[attT])
        if DBG_ATT >= 4:
            o_project(j, N, False)

    def attention_sample(j):
        N = 128
        q_project(j, N, True)
        ktp = KTb
        ktp3 = KTb[0:64, 0:NH * 128].rearrange("p (h n) -> p h n", n=128)
        c.op('dve', lambda: nc.vector.tensor_copy(
            out=bias_s[:, :].rearrange("p (h t) -> p h t", t=DEC_T),
            in_=bsb_sb[:, j * NH:(j + 1) * NH].unsqueeze(2).to_broadcast([128, NH, DEC_T])), reads=[bsb_sb], writes=[bias_s])
        for s in range(SPC):
            qs = slice(s * DEC_T, (s + 1) * DEC_T)
            c.dma('sp', lambda s=s: nc.sync.dma_start(out=vbf[0:DEC_T, 1, :], in_=VS_d[s * DEC_T:(s + 1) * DEC_T, :]),
                  vbf, reads=[VS_d], writes=[vbf])
            c.op('dve', lambda: nc.vector.memset(C32[:, 0:128], 0.0), writes=[C32])
            blocks = ['new'] + list(reversed(range(NPAGES)))
            for bi, blk in enumerate(blocks):
                first = bi == 0
                if blk == 'new':
                    nk = DEC_T
                    ksrc = lambda h, qs=qs: ktn[0:64, h, qs]
                    vsrc = lambda h: vbf[0:DEC_T, 1, h * DH:(h + 1) * DH]
                    ktk, vtk = ktn, vbf
                else:
                    nk = 128
                    col = s * NPAGES + blk
                    c.dma('pool', lambda col=col: nc.gpsimd.indirect_dma_start(
                        out=tm[:, 0, :], out_offset=None, in_=cache_k[:, :],
                        in_offset=bass.IndirectOffsetOnAxis(ap=idxpg[:, col:col + 1], axis=0)), tm, reads=[idxpg], writes=[tm])
                    c.dma('pool', lambda col=col: nc.gpsimd.indirect_dma_start(
                        out=tm[:, 1, :], out_offset=None, in_=cache_v[:, :],
                        in_offset=bass.IndirectOffsetOnAxis(ap=idxpg[:, col:col + 1], axis=0)), tm, reads=[idxpg], writes=[tm])
                    for bq in range(4):
                        pst = psum[bq]
                        for i in range(4):
                            h = bq * 4 + i
                            c.op('pe', lambda pst=pst, i=i, h=h: nc.tensor.transpose(
                                out=pst[0:64, i * 128:(i + 1) * 128], in_=tm[:, 0, h * DH:(h + 1) * DH], identity=ident[:, :]),
                                reads=[tm, ident], writes=[pst])
                        c.op('act', lambda pst=pst, bq=bq: nc.scalar.copy(
                            out=ktp3[:, bq * 4:(bq + 1) * 4, :], in_=pst[0:64, :].rearrange("p (a n) -> p a n", n=128)),
                            reads=[pst], writes=[ktp])
                    c.op('dve', lambda: nc.vector.tensor_copy(out=vbf[:, 0, :], in_=tm[:, 1, :]), reads=[tm], writes=[vbf])
                    ksrc = lambda h: ktp3[:, h, :]
                    vsrc = lambda h: vbf[:, 0, h * DH:(h + 1) * DH]
                    ktk, vtk = ktp, vbf
                psz, psa, pso = psum[4], psum[6], psum[7]
                for h in range(NH):
                    c.op('pe', lambda h=h, ksrc=ksrc, nk=nk, qs=qs: nc.tensor.matmul(
                        psz[0:nk, h * DEC_T:(h + 1) * DEC_T], lhsT=ksrc(h), rhs=qT[0:64, h, qs], start=True, stop=True),
                        reads=[ktk, qT], writes=[psz])
                c.op('dve', lambda nk=nk: nc.vector.tensor_tensor(out=tmpN[0:nk, 0:128], in0=psz[0:nk, 0:128], in1=bias_s[0:nk, :],
                                                                  op=ALU.add), reads=[psz, bias_s], writes=[tmpN])
                c.op('act', lambda nk=nk: nc.scalar.activation(out=u_t[0:nk, 0:128], in_=tmpN[0:nk, 0:128], func=AF.Exp),
                     reads=[tmpN], writes=[u_t])
                c.op('act', lambda nk=nk: nc.scalar.activation(out=lp_t[0:nk, 0:128], in_=u_t[0:nk, 0:128], func=AF.Ln,
                                                               bias=1.0, scale=1.0), reads=[u_t], writes=[lp_t])
                if blk == 'new':
                    c.op('pool', lambda nk=nk: nc.gpsimd.affine_select(
                        out=lp_t[0:nk, 0:128], in_=lp_t[0:nk, 0:128], pattern=[[0, NH], [1, DEC_T]], compare_op=ALU.is_gt,
                        fill=0.0, base=0, channel_multiplier=-1), reads=[lp_t], writes=[lp_t])
                c.op('pe', lambda nk=nk, first=first: nc.tensor.matmul(psa[0:nk, 0:128], lhsT=negtri[0:nk, 0:nk], rhs=lp_t[0:nk, 0:128],
                                                                        start=True, stop=first), reads=[negtri, lp_t], writes=[psa])
                if not first:
                    c.op('pe', lambda nk=nk: nc.tensor.matmul(psa[0:nk, 0:128], lhsT=negones[:, 0:nk], rhs=c16[:, 0:128],
                                                               start=False, stop=True), reads=[negones, c16], writes=[psa])
                c.op('act', lambda nk=nk: nc.scalar.activation(out=tmpM[0:nk, 0:128], in_=psa[0:nk, 0:128], func=AF.Exp),
                     reads=[psa], writes=[tmpM])
                c.op('dve', lambda nk=nk: nc.vector.tensor_tensor(out=at_t[0:nk, 0:128], in0=tmpM[0:nk, 0:128], in1=u_t[0:nk, 0:128],
                                                                  op=ALU.mult), reads=[tmpM, u_t], writes=[at_t])
                if blk == 'new':
                    c.op('pool', lambda nk=nk: nc.gpsimd.affine_select(
                        out=at_t[0:nk, 0:128], in_=at_t[0:nk, 0:128], pattern=[[0, NH], [1, DEC_T]], compare_op=ALU.is_gt,
                        fill=0.0, base=0, channel_multiplier=-1), reads=[at_t], writes=[at_t])
                for h in range(NH):
                    c.op('pe', lambda h=h, vsrc=vsrc, nk=nk: nc.tensor.matmul(
                        pso[0:64, h * DEC_T:(h + 1) * DEC_T], lhsT=vsrc(h), rhs=at_t[0:nk, h * DEC_T:(h + 1) * DEC_T],
                        start=True, stop=True), reads=[vtk, at_t], writes=[pso])
                if first:
                    c.op('dve', lambda: nc.vector.tensor_copy(out=oacc[:, :], in_=pso[0:64, 0:128]), reads=[pso], writes=[oacc])
                else:
                    c.op('dve', lambda: nc.vector.tensor_tensor(out=oacc[:, :], in0=oacc[:, :], in1=pso[0:64, 0:128], op=ALU.add),
                         reads=[pso, oacc], writes=[oacc])
                if bi < len(blocks) - 1:
                    csum_update(False, nk, 128)
            c.op('act', lambda qs=qs: nc.scalar.copy(out=attT[0:64, :, qs], in_=oacc[:, :].rearrange("p (h t) -> p h t", t=DEC_T)),
                 reads=[oacc], writes=[attT])
        o_project(j, N, True)

    def final_out(dst_ap, N):
        nt = N // 128
        rms_stats(N)
        for ch in range(NCH):
            c.op('dve', lambda ch=ch: nc.vector.scalar_tensor_tensor(
                out=scrA[:, ch, 0:N], in0=xT[:, ch, 0:N], scalar=fn_sb[:, ch:ch + 1], in1=rstd[:, 0:N],
                op0=ALU.mult, op1=ALU.mult), reads=[xT, fn_sb, rstd], writes=[scrA])
        for tb in range(nt):
            for ch in range(NCH):
                pst = nps()
                c.op('pe', lambda pst=pst, tb=tb, ch=ch: nc.tensor.transpose(
                    out=pst[:, 0:128], in_=scrA[:, ch, tb * 128:(tb + 1) * 128], identity=ident[:, :]),
                    reads=[scrA, ident], writes=[pst])
                c.op('act', lambda pst=pst, tb=tb, ch=ch: nc.scalar.copy(
                    out=tm[:, tb, ch * 128:(ch + 1) * 128], in_=pst[:, 0:128]), reads=[pst], writes=[tm])
        c.dma('sp', lambda: nc.sync.dma_start(out=dst_ap.rearrange("(n p) d -> p n d", p=128), in_=tm[:, 0:nt, :]),
              tm, reads=[tm], out_dram=True)

    def run_tile(src, dst, N, samp, ti, last):
        load_x_tile(src, N)
        for l in range(n_layers):
            if l < NA:
                conv_layer(l, N, samp, ti, last)
            else:
                if l == NA:
                    kv_project(N, samp, 0 if samp else ti * TP)
                if DBG_ATT >= 2:
                    if samp:
                        attention_sample(l - NA)
                    else:
                        attention_prompt(l - NA, N, ti * TP)
            if DO_PEER:
                peer_layer(l, N, samp)
        final_out(dst, N)

    for ti in range(n_ptiles):
        run_tile(x_p[ti * TP:(ti + 1) * TP, :], y_p[ti * TP:(ti + 1) * TP, :], TP, False, ti, ti == n_ptiles - 1)
    if do_sample:
        run_tile(x_s[:, :], y_s[:, :], 128, True, 0, True)

    c.emit()


def fm(v):
    v = np.asarray(v, np.float32)
    lead = int(np.prod(v.shape[:-1])) if v.ndim > 1 else 1
    n = v.shape[-1] // 128
    return np.ascontiguousarray(v.reshape(lead, n, 128).transpose(2, 0, 1).reshape(128, lead * n))


def _shared_inputs(inp):
    f32 = lambda a: np.ascontiguousarray(np.asarray(a, np.float32))
    sh = {}
    sh["w_ada"] = f32(inp["w_ada"]).reshape(DEPTH * D, 6 * D)
    sh["b_ada"] = fm(inp["b_ada"].reshape(DEPTH * 48, 128).reshape(-1))
    sh["nmix"] = fm(inp["norm_mix"].reshape(-1))
    sh["nffn"] = fm(inp["norm_ffn"].reshape(-1))
    sh["nkv"] = fm(inp["norm_kv"])
    sh["fnorm"] = fm(inp["final_norm"])
    sh["w_pw1"] = f32(inp["w_pw1"]).reshape(NA * D, 2 * D)
    sh["b_pw1"] = fm(inp["b_pw1"].reshape(-1))
    wd = np.asarray(inp["w_dw"], np.float32).reshape(NA, CW, NCH, 128)
    sh["w_dw"] = np.ascontiguousarray(wd.transpose(3, 0, 2, 1).reshape(128, NA * NCH * CW))
    sh["b_dw"] = fm(inp["b_dw"].reshape(-1))
    sh["ln_g"] = fm(inp["ln_g"].reshape(-1))
    sh["ln_b"] = fm(inp["ln_b"].reshape(-1))
    sh["w_pw2"] = f32(inp["w_pw2"]).reshape(NA * D, D)
    sh["b_pw2"] = fm(inp["b_pw2"].reshape(-1))
    sh["w_k"] = f32(inp["w_k"])
    sh["w_v"] = f32(inp["w_v"])
    sh["w_q"] = f32(inp["w_q"]).reshape(2 * D, D)
    sh["w_o"] = f32(inp["w_o"]).reshape(2 * D, D)
    sh["bsb"] = np.ascontiguousarray(np.broadcast_to(np.asarray(inp["b_sb"], np.float32).reshape(1, 2 * NH), (128, 2 * NH)))
    sh["w_pq"] = f32(inp["w_pq"]).reshape(DEPTH * D, 2 * D)
    sk = np.asarray(inp["sub_keys"], np.float32).transpose(0, 1, 2, 4, 3)
    sh["skT"] = np.ascontiguousarray(sk.reshape(DEPTH * 16 * 128, 128))
    sh["exp_u"] = f32(inp["expert_u"]).reshape(DEPTH * NEXP, D)
    sh["exp_v"] = f32(inp["expert_v"]).reshape(DEPTH * NEXP, D)
    sh["cache_k"] = f32(inp["cache_k"]).reshape(NPOOL * PAGE, D)
    sh["cache_v"] = f32(inp["cache_v"]).reshape(NPOOL * PAGE, D)
    return sh


def _core_inputs(inp, sh, core, n_ptiles=SEQ // TP):
    b = core % NB
    m = dict(sh)
    m["x_p"] = np.ascontiguousarray(np.asarray(inp["x_prompt"][b, :n_ptiles * TP], np.float32))
    s0 = core * SPC
    m["x_s"] = np.ascontiguousarray(np.asarray(inp["x_sample"][s0:s0 + SPC], np.float32).reshape(128, D))
    m["c_all"] = np.ascontiguousarray(np.concatenate([np.asarray(inp["c_prompt"][b:b + 1], np.float32),
                                                      np.asarray(inp["c_sample"][s0:s0 + SPC], np.float32)], axis=0))
    pt = np.asarray(inp["page_table"][s0:s0 + SPC], np.int32).reshape(1, SPC * NPAGES)
    m["ptab"] = np.ascontiguousarray(np.broadcast_to(pt, (128, SPC * NPAGES)))
    m["st_conv"] = np.ascontiguousarray(np.asarray(inp["state_conv"][:, s0:s0 + SPC], np.float32).reshape(NA * SPC * HIST, D))
    return m


_NC_CACHE = {}


def kernel(**inputs):
    if "nc" not in _NC_CACHE:
        _NC_CACHE["nc"] = build_nc()
    nc = _NC_CACHE["nc"]
    sh = _shared_inputs(inputs)
    in_maps = [_core_inputs(inputs, sh, cix) for cix in range(NCORES)]
    res = run_bass_kernel_spmd(nc, in_maps, core_ids=list(range(NCORES))).results
    y_prompt = np.stack([res[b]["y_p"] for b in range(NB)], axis=0)
    y_sample = np.concatenate([res[cix]["y_s"] for cix in range(NCORES)], axis=0).reshape(DEC_B, DEC_T, D)
    conv_prompt = np.stack([res[b]["conv_p"].reshape(NA, HIST, D) for b in range(NB)], axis=1)
    conv_sample = np.concatenate([res[cix]["conv_s"].reshape(NA, SPC, HIST, D) for cix in range(NCORES)], axis=1)
    k_prompt = np.stack([res[b]["k_p"] for b in range(NB)], axis=0).reshape(NB, SEQ, NH, DH)
    v_prompt = np.stack([res[b]["v_p"] for b in range(NB)], axis=0).reshape(NB, SEQ, NH, DH)
    k_sample = np.concatenate([res[cix]["k_s"] for cix in range(NCORES)], axis=0).reshape(DEC_B, DEC_T, NH, DH)
    v_sample = np.concatenate([res[cix]["v_s"] for cix in range(NCORES)], axis=0).reshape(DEC_B, DEC_T, NH, DH)
    return (y_prompt, y_sample, conv_prompt, conv_sample, k_prompt, v_prompt, k_sample, v_sample)
```

```python
from contextlib import ExitStack
import numpy as np
import concourse.bass as bass
import concourse.mybir as mybir
from concourse.bass_utils import run_bass_kernel_spmd

F32 = mybir.dt.float32
BF16 = mybir.dt.bfloat16
I32 = mybir.dt.int32
U32 = mybir.dt.uint32
ALU = mybir.AluOpType
AF = mybir.ActivationFunctionType
AX = mybir.AxisListType

D = 1024
NCH = 8
SEQ = 4096
NB = 4
DEC_B = 128
DEC_T = 8
PAST = 2048
PAGE = 128
NPAGES = 16
NPOOL = 2560
NH = 16
DH = 64
CW = 31
HIST = 30
DEPTH = 4
NA = 2
PH = 8
PK = 16
NKEYS = 128
NEXP = 16384
EPS = 1e-6
TP = 256
DO_PEER = True
DBG_ATT = 9
DBG_KV = 9
NCORES = 8
SPC = DEC_B // NCORES


class Tk:
    def __init__(self, t, name, sem=None):
        self.t = t
        self.name = name
        self.w = None
        self.r = []
        self.dsem = sem
        self.dcnt = 0

    def __getitem__(self, idx):
        return self.t[idx]


class Ctx:
    def __init__(self, nc, stack):
        self.nc = nc
        self.stack = stack
        self.engs = {'pe': nc.tensor, 'dve': nc.vector, 'act': nc.scalar, 'pool': nc.gpsimd, 'sp': nc.sync}
        self.sem = {k: stack.enter_context(nc.semaphore("s_" + k)) for k in self.engs}
        self.cnt = {k: 0 for k in self.engs}
        self.waited = {k: {} for k in self.engs}
        self.prog = {k: [] for k in self.engs}
        self.out_events = []
        self.ntile = 0

    def sb(self, shape, dt, name, dma=False):
        t = self.stack.enter_context(self.nc.sbuf_tensor(name, list(shape), dt))
        self.sbytes = getattr(self, "sbytes", 0) + int(np.prod(shape[1:])) * (2 if dt == BF16 else 4)
        sem = self.stack.enter_context(self.nc.semaphore("d_" + name)) if dma else None
        return Tk(t, name, sem)

    def ps(self, name):
        t = self.stack.enter_context(self.nc.psum_tensor(name, [128, 512], F32))
        tk = Tk(t, name)
        tk.excl = True
        return tk

    def dram(self, name, shape, dt, dma=False):
        t = self.nc.dram_tensor(name, list(shape), dt, kind="ExternalOutput").ap()
        sem = self.stack.enter_context(self.nc.semaphore("d_" + name)) if dma else None
        return Tk(t, name, sem)

    def _deps(self, e, reads, writes):
        evs = []
        for tk in reads:
            if tk.w is not None:
                evs.append(tk.w)
            if getattr(tk, "excl", False):
                evs.extend(tk.r)
        for tk in writes:
            if tk.w is not None:
                evs.append(tk.w)
            evs.extend(tk.r)
        for (sem, val, src) in evs:
            if e == 'pe' and src == 'pe':
                continue
            key = id(sem)
            if self.waited[e].get(key, 0) >= val:
                continue
            self.waited[e][key] = val
            self.prog[e].append(('wait', sem, val))

    def _mark(self, ev, reads, writes):
        for tk in reads:
            tk.r.append(ev)
        for tk in writes:
            tk.w = ev
            tk.r = []

    def op(self, e, fn, reads=(), writes=()):
        self._deps(e, reads, writes)
        self.cnt[e] += 1
        ev = (self.sem[e], self.cnt[e], e)
        self.prog[e].append(('op', fn, self.sem[e], 1))
        self._mark(ev, reads, writes)
        return ev

    def dma(self, q, fn, sbt, reads=(), writes=(), out_dram=False):
        self._deps(q, reads, writes)
        sbt.dcnt += 1
        ev = (sbt.dsem, 16 * sbt.dcnt, 'dma')
        self.prog[q].append(('op', fn, sbt.dsem, 16))
        self._mark(ev, reads, writes)
        if out_dram:
            self.out_events.append(ev)
        return ev

    def emit(self):
        nc = self.nc
        last = {}
        for (sem, val, _) in self.out_events:
            k = id(sem)
            if k not in last or last[k][1] < val:
                last[k] = (sem, val)
        for (sem, val) in last.values():
            self.prog['sp'].append(('wait', sem, val))

        def run(eng, lst):
            for it in lst:
                if it[0] == 'wait':
                    eng.wait_ge(it[1], it[2])
                else:
                    it[1]().then_inc(it[2], it[3])

        with nc.Block() as block:
            @block.tensor
            def _(e):
                run(nc.tensor, self.prog['pe'])

            @block.vector
            def _(e):
                run(nc.vector, self.prog['dve'])

            @block.scalar
            def _(e):
                run(nc.scalar, self.prog['act'])

            @block.gpsimd
            def _(e):
                run(nc.gpsimd, self.prog['pool'])

            @block.sync
            def _(e):
                run(nc.sync, self.prog['sp'])


def build_nc(n_ptiles=SEQ // TP, do_sample=True, n_layers=DEPTH):
    nc = bass.Bass("TRN2", target_bir_lowering=False)
    with ExitStack() as stack:
        c = Ctx(nc, stack)
        _build(c, n_ptiles, do_sample, n_layers)
    return nc


def _build(c, n_ptiles, do_sample, n_layers):
    nc = c.nc

    def din(name, shape, dt=F32):
        return nc.dram_tensor(name, list(shape), dt, kind="ExternalInput").ap()

    def dout(name, shape, dt=F32):
        return nc.dram_tensor(name, list(shape), dt, kind="ExternalOutput").ap()

    NTOK_P = n_ptiles * TP
    x_p = din("x_p", [NTOK_P, D])
    x_s = din("x_s", [128, D])
    c_all = din("c_all", [17, D])
    w_ada = din("w_ada", [DEPTH * D, 6 * D])
    b_ada = din("b_ada", [128, DEPTH * 48])
    nmix = din("nmix", [128, DEPTH * NCH])
    nffn = din("nffn", [128, DEPTH * NCH])
    nkv = din("nkv", [128, NCH])
    fnorm = din("fnorm", [128, NCH])
    w_pw1 = din("w_pw1", [NA * D, 2 * D])
    b_pw1 = din("b_pw1", [128, NA * 16])
    w_dw = din("w_dw", [128, NA * NCH * CW])
    b_dw = din("b_dw", [128, NA * NCH])
    ln_g = din("ln_g", [128, NA * NCH])
    ln_b = din("ln_b", [128, NA * NCH])
    w_pw2 = din("w_pw2", [NA * D, D])
    b_pw2 = din("b_pw2", [128, NA * NCH])
    w_k = din("w_k", [D, D])
    w_v = din("w_v", [D, D])
    w_q = din("w_q", [2 * D, D])
    w_o = din("w_o", [2 * D, D])
    bsb = din("bsb", [128, 2 * NH])
    w_pq = din("w_pq", [DEPTH * D, 2 * D])
    skT = din("skT", [DEPTH * 16 * 128, 128])
    exp_u = din("exp_u", [DEPTH * NEXP, D])
    exp_v = din("exp_v", [DEPTH * NEXP, D])
    npool = NPOOL if do_sample else 16
    cache_k = din("cache_k", [npool * PAGE, D])
    cache_v = din("cache_v", [npool * PAGE, D])
    ptab = din("ptab", [128, SPC * NPAGES], I32)
    st_conv = din("st_conv", [NA * SPC * HIST, D])

    y_p = dout("y_p", [NTOK_P, D])
    y_s = dout("y_s", [128, D])
    conv_p = dout("conv_p", [NA * HIST, D])
    conv_s = dout("conv_s", [NA * SPC * HIST, D])
    k_p = dout("k_p", [NTOK_P, D])
    v_p = dout("v_p", [NTOK_P, D])
    k_s = dout("k_s", [128, D])
    v_s = dout("v_s", [128, D])

    KT_d = c.dram("KT_d", [NH, DH, SEQ], BF16)
    V_d = c.dram("V_d", [SEQ, D], BF16)
    VS_d = c.dram("VS_d", [128, D], BF16)
    EU_d = [c.dram("EU_d%d" % l, [NEXP, D], BF16, dma=True) for l in range(n_layers)]
    EV_d = [c.dram("EV_d%d" % l, [NEXP, D], BF16, dma=True) for l in range(n_layers)]

    ident = c.sb([128, 128], F32, "ident")
    ones = c.sb([128, 128], F32, "ones")
    negtri = c.sb([128, 128], BF16, "negtri")
    negones = c.sb([128, 128], BF16, "negones")
    iota16 = c.sb([128, 16], F32, "iota16")
    iotap = c.sb([128, 1], F32, "iotap")
    c.op('pool', lambda: nc.gpsimd.memset(ones[:, :], 1.0), writes=[ones])
    c.op('pool', lambda: nc.gpsimd.memset(ident[:, :], 1.0), writes=[ident])
    c.op('pool', lambda: nc.gpsimd.affine_select(out=ident[:, :], in_=ident[:, :], pattern=[[-1, 128]],
                                                 compare_op=ALU.is_equal, fill=0.0, base=0, channel_multiplier=1),
         reads=[ident], writes=[ident])
    c.op('pool', lambda: nc.gpsimd.memset(negones[:, :], -1.0), writes=[negones])
    c.op('pool', lambda: nc.gpsimd.memset(negtri[:, :], -1.0), writes=[negtri])
    c.op('pool', lambda: nc.gpsimd.affine_select(out=negtri[:, :], in_=negtri[:, :], pattern=[[-1, 128]],
                                                 compare_op=ALU.is_ge, fill=0.0, base=0, channel_multiplier=1),
         reads=[negtri], writes=[negtri])
    c.op('pool', lambda: nc.gpsimd.iota(iota16[:, :], pattern=[[1, 16]], base=0, channel_multiplier=0,
                                        allow_small_or_imprecise_dtypes=True), writes=[iota16])
    c.op('pool', lambda: nc.gpsimd.iota(iotap[:, :], pattern=[[0, 1]], base=0, channel_multiplier=1,
                                        allow_small_or_imprecise_dtypes=True), writes=[iotap])

    def load_small(src, cols, name, dt=F32):
        t = c.sb([128, cols], dt, name, dma=True)
        c.dma('sp', lambda: nc.sync.dma_start(out=t[:, :], in_=src[:, :]), t, writes=[t])
        return t

    bada_sb = load_small(b_ada, DEPTH * 48, "bada_sb")
    nmix_sb = load_small(nmix, DEPTH * NCH, "nmix_sb")
    nffn_sb = load_small(nffn, DEPTH * NCH, "nffn_sb")
    nkv_sb = load_small(nkv, NCH, "nkv_sb")
    fn_sb = load_small(fnorm, NCH, "fn_sb")
    bpw1_sb = load_small(b_pw1, NA * 16, "bpw1_sb")
    wdw_sb = load_small(w_dw, NA * NCH * CW, "wdw_sb")
    bdw_sb = load_small(b_dw, NA * NCH, "bdw_sb")
    lng_sb = load_small(ln_g, NA * NCH, "lng_sb")
    lnb_sb = load_small(ln_b, NA * NCH, "lnb_sb")
    bpw2_sb = load_small(b_pw2, NA * NCH, "bpw2_sb")
    bsb_sb = load_small(bsb, 2 * NH, "bsb_sb")
    bias_s = c.sb([128, 128], F32, "bias_s")

    psum = [c.ps("ps%d" % i) for i in range(8)]
    prot = [0]

    def nps():
        prot[0] = (prot[0] + 1) % 4
        return psum[prot[0]]

    tm = c.sb([128, 2, D], F32, "tm", dma=True)
    xT = c.sb([128, NCH, TP], F32, "xT")
    scrA = c.sb([128, NCH, TP], F32, "scrA")
    hT = c.sb([128, NCH, TP], BF16, "hT")
    rstd = c.sb([128, TP], F32, "rstd")
    tmpN = c.sb([128, TP], F32, "tmpN")
    tmpM = c.sb([128, TP], F32, "tmpM")
    wsl = [c.sb([128, 8, 256], BF16, "wsl%d" % i, dma=True) for i in range(2)]
    wrot = [0]
    ada = c.sb([128, DEPTH * 48, 17], F32, "ada")
    fullbuf = c.sb([128, NCH, SPC * (HIST + DEC_T)], F32, "fullbuf")
    FOFF = [0, 304]

    class _V:
        def __init__(self, fn):
            self.fn = fn

        def __getitem__(self, idx):
            return self.fn(idx)
    yb = [c.sb([128, TP], F32, "yb%d" % j) for j in range(NCH)]
    qT = c.sb([128, 16, TP], BF16, "qT", dma=True)
    attT = c.sb([64, NH, TP], BF16, "attT")
    ktn = c.sb([64, NH, TP], BF16, "ktn", dma=True)
    vbf = c.sb([128, 2, D], BF16, "vbf", dma=True)
    KTb = c.sb([64, SEQ], BF16, "KTb", dma=True)
    Vb = c.sb([128, SEQ // 128, DH], BF16, "Vb", dma=True)
    u_t = c.sb([128, TP], F32, "u_t")
    lp_t = c.sb([128, TP], BF16, "lp_t")
    at_t = c.sb([128, TP], BF16, "at_t")
    C32 = c.sb([128, TP], F32, "C32")
    c16 = c.sb([128, TP], BF16, "c16")
    u_tB = c.sb([128, TP], F32, "u_tB")
    lp_tB = c.sb([128, TP], BF16, "lp_tB")
    oacc = c.sb([64, 128], F32, "oacc")

    def wslab(W, r0, col0, ncols=256, kmode=128):
        t = wsl[wrot[0]]
        wrot[0] ^= 1
        if kmode == 128:
            src = W[r0:r0 + D, col0:col0 + ncols].rearrange("(kc p) n -> p kc n", p=128)
            c.dma('pool', lambda: nc.gpsimd.dma_start(out=t[:, 0:8, 0:ncols], in_=src), t, writes=[t])
        else:
            src = W[r0:r0 + D, col0:col0 + 128].rearrange("(h p) n -> p h n", p=64)
            dstv = t[0:64, :, :].rearrange("p k (a n) -> p (k a) n", n=128)
            c.dma('pool', lambda: nc.gpsimd.dma_start(out=dstv, in_=src), t, writes=[t])
        return t

    def A(l, k, ch):
        return l * 48 + k * 8 + ch

    if DO_PEER:
        CR = 2048
        for l in range(n_layers):
            for (src, dst) in ((exp_u, EU_d[l]), (exp_v, EV_d[l])):
                for r0 in range(0, NEXP, CR):
                    c.dma('pool', lambda src=src, dst=dst, r0=r0, l=l: nc.gpsimd.dma_start(
                        out=dst[r0:r0 + CR, :], in_=src[l * NEXP + r0:l * NEXP + r0 + CR, :]), dst, writes=[dst])

    cin = tm
    scT = c.sb([128, NCH, 17], BF16, "scT")
    c.dma('sp', lambda: nc.sync.dma_start(out=tm[0:17, 0, :], in_=c_all[:, :]), tm, writes=[tm])
    for ch in range(NCH):
        pst = nps()
        c.op('pe', lambda pst=pst, ch=ch: nc.tensor.transpose(out=pst[:, 0:17], in_=tm[0:17, 0, ch * 128:(ch + 1) * 128],
                                                               identity=ident[0:17, 0:17]),
             reads=[cin, ident], writes=[pst])
        c.op('act', lambda pst=pst, ch=ch: nc.scalar.activation(out=scT[:, ch, :], in_=pst[:, 0:17], func=AF.Silu),
             reads=[pst], writes=[scT])
    for l in range(n_layers):
        for sl in range(24):
            t = wslab(w_ada, l * D, sl * 256)
            for jj in range(2):
                j = sl * 2 + jj
                pst = nps()
                for kc in range(NCH):
                    c.op('pe', lambda pst=pst, t=t, kc=kc, jj=jj: nc.tensor.matmul(
                        pst[:, 0:17], lhsT=t[:, kc, jj * 128:(jj + 1) * 128], rhs=scT[:, kc, :],
                        start=(kc == 0), stop=(kc == NCH - 1)), reads=[t, scT], writes=[pst])
                c.op('dve', lambda pst=pst, l=l, j=j: nc.vector.tensor_scalar(
                    out=ada[:, l * 48 + j, :], in0=pst[:, 0:17], scalar1=bada_sb[:, l * 48 + j:l * 48 + j + 1], scalar2=1.0,
                    op0=ALU.add, op1=ALU.mult), reads=[pst, bada_sb], writes=[ada])
        for ch in range(NCH):
            for (k, nsb) in ((1, nmix_sb), (4, nffn_sb)):
                c.op('dve', lambda l=l, k=k, ch=ch, nsb=nsb: nc.vector.tensor_scalar(
                    out=ada[:, A(l, k, ch), :], in0=ada[:, A(l, k, ch), :], scalar1=1.0,
                    scalar2=nsb[:, l * NCH + ch:l * NCH + ch + 1], op0=ALU.add, op1=ALU.mult),
                    reads=[ada, nsb], writes=[ada])

    def v3(ap):
        return ap.rearrange("p (s t) -> p s t", t=DEC_T)

    def abc(idx):
        return ada[:, idx, 1:17].unsqueeze(2).to_broadcast([128, SPC, DEC_T])

    def modulate(out_ap, in_ap, aidx, bidx, N, samp, out_tk, in_tk):
        if not samp:
            c.op('dve', lambda: nc.vector.tensor_scalar(out=out_ap, in0=in_ap, scalar1=ada[:, aidx, 0:1],
                                                        scalar2=ada[:, bidx, 0:1], op0=ALU.mult, op1=ALU.add),
                 reads=[ada, in_tk], writes=[out_tk])
        else:
            c.op('dve', lambda: nc.vector.tensor_tensor(out=v3(tmpN[:, 0:N]), in0=v3(in_ap), in1=abc(aidx), op=ALU.mult),
                 reads=[ada, in_tk], writes=[tmpN])
            c.op('dve', lambda: nc.vector.tensor_tensor(out=v3(out_ap), in0=v3(tmpN[:, 0:N]), in1=abc(bidx), op=ALU.add),
                 reads=[ada, tmpN], writes=[out_tk])

    def load_x_tile(src_ap, N):
        nt = N // 128
        c.dma('sp', lambda: nc.sync.dma_start(out=tm[:, 0:nt, :], in_=src_ap.rearrange("(n p) d -> p n d", p=128)),
              tm, writes=[tm])
        for tb in range(nt):
            for ch in range(NCH):
                pst = nps()
                c.op('pe', lambda pst=pst, tb=tb, ch=ch: nc.tensor.transpose(
                    out=pst[:, 0:128], in_=tm[:, tb, ch * 128:(ch + 1) * 128], identity=ident[:, :]),
                    reads=[tm, ident], writes=[pst])
                c.op('act', lambda pst=pst, tb=tb, ch=ch: nc.scalar.copy(
                    out=xT[:, ch, tb * 128:(tb + 1) * 128], in_=pst[:, 0:128]),
                    reads=[pst], writes=[xT])

    def finish_rstd(pst, N, dst):
        c.op('dve', lambda: nc.vector.tensor_scalar(out=dst[:, 0:N], in0=pst[:, 0:N], scalar1=1.0 / D, scalar2=EPS,
                                                    op0=ALU.mult, op1=ALU.add), reads=[pst], writes=[dst])
        c.op('act', lambda: nc.scalar.activation(out=dst[:, 0:N], in_=dst[:, 0:N], func=AF.Sqrt),
             reads=[dst], writes=[dst])
        c.op('dve', lambda: nc.vector.reciprocal(out=dst[:, 0:N], in_=dst[:, 0:N]), reads=[dst], writes=[dst])

    def rms_stats(N):
        for ch in range(NCH):
            c.op('act', lambda ch=ch: nc.scalar.activation(out=scrA[:, ch, 0:N], in_=xT[:, ch, 0:N], func=AF.Square),
                 reads=[xT], writes=[scrA])
        pst = psum[4]
        for ch in range(NCH):
            c.op('pe', lambda ch=ch: nc.tensor.matmul(pst[:, 0:N], lhsT=ones[:, :], rhs=scrA[:, ch, 0:N],
                                                        start=(ch == 0), stop=(ch == NCH - 1)),
                 reads=[ones, scrA], writes=[pst])
        finish_rstd(pst, N, rstd)

    def modnorm(l, which, N, samp, keep_f32=False):
        rms_stats(N)
        ka, kb_ = (1, 0) if which == 1 else (4, 3)
        for ch in range(NCH):
            c.op('dve', lambda ch=ch: nc.vector.tensor_tensor(out=scrA[:, ch, 0:N], in0=xT[:, ch, 0:N], in1=rstd[:, 0:N],
                                                              op=ALU.mult), reads=[xT, rstd], writes=[scrA])
            if keep_f32:
                modulate(scrA[:, ch, 0:N], scrA[:, ch, 0:N], A(l, ka, ch), A(l, kb_, ch), N, samp, scrA, scrA)
                c.op('act', lambda ch=ch: nc.scalar.copy(out=hT[:, ch, 0:N], in_=scrA[:, ch, 0:N]),
                     reads=[scrA], writes=[hT])
            else:
                modulate(hT[:, ch, 0:N], scrA[:, ch, 0:N], A(l, ka, ch), A(l, kb_, ch), N, samp, hT, scrA)

    def resid_add(pst, ps_ap, bias_ap, gidx, ch, N, samp):
        if not samp:
            if bias_ap is not None:
                c.op('dve', lambda: nc.vector.tensor_scalar(out=tmpN[:, 0:N], in0=ps_ap, scalar1=bias_ap,
                                                            scalar2=ada[:, gidx, 0:1], op0=ALU.add, op1=ALU.mult),
                     reads=[ada, pst, bpw2_sb], writes=[tmpN])
            else:
                c.op('dve', lambda: nc.vector.tensor_scalar(out=tmpN[:, 0:N], in0=ps_ap, scalar1=ada[:, gidx, 0:1],
                                                            scalar2=1.0, op0=ALU.mult, op1=ALU.mult),
                     reads=[ada, pst], writes=[tmpN])
        else:
            if bias_ap is not None:
                c.op('dve', lambda: nc.vector.tensor_scalar(out=tmpM[:, 0:N], in0=ps_ap, scalar1=bias_ap, scalar2=1.0,
                                                            op0=ALU.add, op1=ALU.mult), reads=[pst, bpw2_sb], writes=[tmpM])
                c.op('dve', lambda: nc.vector.tensor_tensor(out=v3(tmpN[:, 0:N]), in0=v3(tmpM[:, 0:N]), in1=abc(gidx),
                                                            op=ALU.mult), reads=[ada, tmpM], writes=[tmpN])
            else:
                c.op('dve', lambda: nc.vector.tensor_tensor(out=v3(tmpN[:, 0:N]), in0=v3(ps_ap), in1=abc(gidx),
                                                            op=ALU.mult), reads=[ada, pst], writes=[tmpN])
        c.op('dve', lambda: nc.vector.tensor_tensor(out=xT[:, ch, 0:N], in0=xT[:, ch, 0:N], in1=tmpN[:, 0:N], op=ALU.add),
             reads=[xT, tmpN], writes=[xT])

    def conv_layer(l, N, samp, ti, last):
        modnorm(l, 1, N, samp)
        fo = FOFF[l]
        fl = fullbuf
        full_s = fullbuf
        fs4 = fullbuf[:, :, :].rearrange("p c (s w) -> p c s w", w=HIST + DEC_T)
        if not samp and ti == 0:
            c.op('pool', lambda: nc.gpsimd.memset(fl[:, :, fo:fo + HIST], 0.0), writes=[fl])
        if samp:
            for half in range(2):
                r0 = l * SPC * HIST + half * 240
                c.dma('sp', lambda r0=r0: nc.sync.dma_start(
                    out=tm[0:120, 0:2, :], in_=st_conv[r0:r0 + 240, :].rearrange("(n p) d -> p n d", p=120)),
                    tm, writes=[tm])
                for n in range(2):
                    for ch in range(NCH):
                        pst = nps()
                        c.op('pe', lambda pst=pst, n=n, ch=ch: nc.tensor.transpose(
                            out=pst[:, 0:120], in_=tm[0:120, n, ch * 128:(ch + 1) * 128], identity=ident[0:120, 0:120]),
                            reads=[tm, ident], writes=[pst])
                        s0 = (half * 2 + n) * 4
                        c.op('act', lambda pst=pst, ch=ch, s0=s0: nc.scalar.copy(
                            out=fs4[:, ch, s0:s0 + 4, 0:HIST],
                            in_=pst[:, 0:120].rearrange("p (s r) -> p s r", r=HIST)),
                            reads=[pst], writes=[full_s])
        for jp in range(4):
            ta = wslab(w_pw1, l * D, jp * 256)
            tb_ = wslab(w_pw1, l * D, D + jp * 256)
            for jj in range(2):
                j = jp * 2 + jj
                ps1 = nps()
                ps2 = nps()
                for (pst, t) in ((ps1, ta), (ps2, tb_)):
                    for kc in range(NCH):
                        c.op('pe', lambda pst=pst, t=t, kc=kc, jj=jj: nc.tensor.matmul(
                            pst[:, 0:N], lhsT=t[:, kc, jj * 128:(jj + 1) * 128], rhs=hT[:, kc, 0:N],
                            start=(kc == 0), stop=(kc == NCH - 1)), reads=[t, hT], writes=[pst])
                c.op('act', lambda ps2=ps2, j=j: nc.scalar.activation(
                    out=tmpM[:, 0:N], in_=ps2[:, 0:N], func=AF.Sigmoid,
                    bias=bpw1_sb[:, l * 16 + 8 + j:l * 16 + 8 + j + 1], scale=1.0),
                    reads=[ps2, bpw1_sb], writes=[tmpM])
                if not samp:
                    c.op('dve', lambda ps1=ps1, j=j: nc.vector.scalar_tensor_tensor(
                        out=fl[:, j, fo + HIST:fo + HIST + N], in0=ps1[:, 0:N], scalar=bpw1_sb[:, l * 16 + j:l * 16 + j + 1],
                        in1=tmpM[:, 0:N], op0=ALU.add, op1=ALU.mult), reads=[ps1, bpw1_sb, tmpM], writes=[fl])
                else:
                    c.op('dve', lambda ps1=ps1, j=j: nc.vector.scalar_tensor_tensor(
                        out=scrA[:, j, 0:N], in0=ps1[:, 0:N], scalar=bpw1_sb[:, l * 16 + j:l * 16 + j + 1],
                        in1=tmpM[:, 0:N], op0=ALU.add, op1=ALU.mult), reads=[ps1, bpw1_sb, tmpM], writes=[scrA])
                    c.op('act', lambda j=j: nc.scalar.copy(out=fs4[:, j, :, HIST:HIST + DEC_T],
                                                           in_=v3(scrA[:, j, 0:N])), reads=[scrA], writes=[full_s])
        if samp:
            for ch in range(NCH):
                pst = nps()
                c.op('pe', lambda pst=pst, ch=ch: nc.tensor.transpose(out=pst[:, 0:128], in_=scrA[:, ch, 0:128],
                                                                       identity=ident[:, :]),
                     reads=[scrA, ident], writes=[pst])
                c.op('act', lambda pst=pst, ch=ch: nc.scalar.copy(out=tm[:, 0, ch * 128:(ch + 1) * 128], in_=pst[:, 0:128]),
                     reads=[pst], writes=[tm])
            r0 = l * SPC * HIST
            dst = conv_s[r0:r0 + SPC * HIST, :].rearrange("(s r) d -> s r d", r=HIST)
            src = st_conv[r0:r0 + SPC * HIST, :].rearrange("(s r) d -> s r d", r=HIST)
            for s in range(SPC):
                c.dma('sp', lambda s=s: nc.sync.dma_start(out=dst[s, HIST - DEC_T:HIST, :], in_=tm[s * DEC_T:(s + 1) * DEC_T, 0, :]),
                      tm, reads=[tm], out_dram=True)
            c.dma('sp', lambda: nc.sync.dma_start(out=dst[:, 0:HIST - DEC_T, :], in_=src[:, DEC_T:HIST, :]),
                  tm, reads=[], out_dram=True)
        elif last:
            for ch in range(NCH):
                pst = nps()
                c.op('pe', lambda pst=pst, ch=ch: nc.tensor.transpose(out=pst[0:HIST, 0:128], in_=fl[:, ch, fo + N:fo + N + HIST],
                                                                       identity=ident[:, :]),
                     reads=[fl, ident], writes=[pst])
                c.op('act', lambda pst=pst, ch=ch: nc.scalar.copy(out=tm[0:HIST, 0, ch * 128:(ch + 1) * 128],
                                                                  in_=pst[0:HIST, 0:128]), reads=[pst], writes=[tm])
            c.dma('sp', lambda: nc.sync.dma_start(out=conv_p[l * HIST:(l + 1) * HIST, :], in_=tm[0:HIST, 0, :]),
                  tm, reads=[tm], out_dram=True)
        src_t = full_s if samp else fl
        for w in range(CW):
            for j in range(NCH):
                widx = (l * NCH + j) * CW + w
                if samp:
                    in0 = fs4[:, j, :, w:w + DEC_T]
                    yv = v3(yb[j][:, 0:N])
                else:
                    in0 = fl[:, j, fo + w:fo + w + N]
                    yv = yb[j][:, 0:N]
                if w == 0:
                    c.op('dve', lambda in0=in0, yv=yv, widx=widx, j=j: nc.vector.tensor_scalar(
                        out=yv, in0=in0, scalar1=wdw_sb[:, widx:widx + 1], scalar2=bdw_sb[:, l * NCH + j:l * NCH + j + 1],
                        op0=ALU.mult, op1=ALU.add), reads=[src_t, wdw_sb, bdw_sb], writes=[yb[j]])
                else:
                    c.op('dve', lambda in0=in0, yv=yv, widx=widx: nc.vector.scalar_tensor_tensor(
                        out=yv, in0=in0, scalar=wdw_sb[:, widx:widx + 1], in1=yv, op0=ALU.mult, op1=ALU.add),
                        reads=[src_t, wdw_sb, yb[j]], writes=[yb[j]])
        if not samp and not last:
            c.op('pool', lambda: nc.gpsimd.tensor_copy(out=fl[:, :, fo:fo + HIST], in_=fl[:, :, fo + N:fo + N + HIST]),
                 reads=[fl], writes=[fl])
        for j in range(NCH):
            c.op('act', lambda j=j: nc.scalar.activation(out=scrA[:, j, 0:N], in_=yb[j][:, 0:N], func=AF.Square),
                 reads=[yb[j]], writes=[scrA])
        pm, pq = psum[4], psum[5]
        for j in range(NCH):
            c.op('pe', lambda j=j: nc.tensor.matmul(pm[:, 0:N], lhsT=ones[:, :], rhs=yb[j][:, 0:N],
                                                      start=(j == 0), stop=(j == NCH - 1)), reads=[ones, yb[j]], writes=[pm])
        for j in range(NCH):
            c.op('pe', lambda j=j: nc.tensor.matmul(pq[:, 0:N], lhsT=ones[:, :], rhs=scrA[:, j, 0:N],
                                                      start=(j == 0), stop=(j == NCH - 1)), reads=[ones, scrA], writes=[pq])
        c.op('dve', lambda: nc.vector.tensor_scalar(out=tmpM[:, 0:N], in0=pm[:, 0:N], scalar1=1.0 / D, scalar2=1.0,
                                                    op0=ALU.mult, op1=ALU.mult), reads=[pm], writes=[tmpM])
        c.op('dve', lambda: nc.vector.tensor_tensor(out=tmpN[:, 0:N], in0=tmpM[:, 0:N], in1=tmpM[:, 0:N], op=ALU.mult),
             reads=[tmpM], writes=[tmpN])
        c.op('dve', lambda: nc.vector.scalar_tensor_tensor(out=rstd[:, 0:N], in0=pq[:, 0:N], scalar=1.0 / D, in1=tmpN[:, 0:N],
                                                           op0=ALU.mult, op1=ALU.subtract), reads=[pq, tmpN], writes=[rstd])
        c.op('dve', lambda: nc.vector.tensor_scalar(out=rstd[:, 0:N], in0=rstd[:, 0:N], scalar1=EPS, scalar2=1.0,
                                                    op0=ALU.add, op1=ALU.mult), reads=[rstd], writes=[rstd])
        c.op('act', lambda: nc.scalar.activation(out=rstd[:, 0:N], in_=rstd[:, 0:N], func=AF.Sqrt), reads=[rstd], writes=[rstd])
        c.op('dve', lambda: nc.vector.reciprocal(out=rstd[:, 0:N], in_=rstd[:, 0:N]), reads=[rstd], writes=[rstd])
        for j in range(NCH):
            c.op('dve', lambda j=j: nc.vector.tensor_tensor(out=scrA[:, j, 0:N], in0=yb[j][:, 0:N], in1=tmpM[:, 0:N],
                                                            op=ALU.subtract), reads=[yb[j], tmpM], writes=[scrA])
            c.op('dve', lambda j=j: nc.vector.tensor_tensor(out=scrA[:, j, 0:N], in0=scrA[:, j, 0:N], in1=rstd[:, 0:N],
                                                            op=ALU.mult), reads=[scrA, rstd], writes=[scrA])
            c.op('act', lambda j=j: nc.scalar.activation(out=hT[:, j, 0:N], in_=scrA[:, j, 0:N], func=AF.Silu,
                                                         bias=lnb_sb[:, l * NCH + j:l * NCH + j + 1],
                                                         scale=lng_sb[:, l * NCH + j:l * NCH + j + 1]),
                 reads=[scrA, lnb_sb, lng_sb], writes=[hT])
        for sl in range(4):
            t = wslab(w_pw2, l * D, sl * 256)
            for jj in range(2):
                ch = sl * 2 + jj
                pst = nps()
                for kc in range(NCH):
                    c.op('pe', lambda pst=pst, t=t, kc=kc, jj=jj: nc.tensor.matmul(
                        pst[:, 0:N], lhsT=t[:, kc, jj * 128:(jj + 1) * 128], rhs=hT[:, kc, 0:N],
                        start=(kc == 0), stop=(kc == NCH - 1)), reads=[t, hT], writes=[pst])
                resid_add(pst, pst[:, 0:N], bpw2_sb[:, l * NCH + ch:l * NCH + ch + 1], A(l, 2, ch), ch, N, samp)


    skb = c.sb([128, 16, 128], BF16, "skb", dma=True)
    sc = c.sb([128, 2048], F32, "sc")
    sv = c.sb([128, 16, 16], F32, "sv")
    si = c.sb([128, 16, 16], U32, "si")
    sif = c.sb([128, 16, 16], F32, "sif")
    cand = c.sb([128, PH, 256], F32, "cand")
    cv = c.sb([128, PH, 16], F32, "cv")
    cp = c.sb([128, PH, 16], U32, "cp")
    iiu = c.sb([128, PH, 16], U32, "iiu")
    jju = c.sb([128, PH, 16], U32, "jju")
    iif = c.sb([128, PH, 16], F32, "iif")
    jjf = c.sb([128, PH, 16], F32, "jjf")
    selI = c.sb([128, PH, 16], F32, "selI")
    selJ = c.sb([128, PH, 16], F32, "selJ")
    ef = c.sb([128, 128], F32, "ef")
    eidx = c.sb([128, 128], I32, "eidx")
    negm = c.sb([128, PH], F32, "negm")
    ee = c.sb([128, PH, 16], F32, "ee")
    zz = c.sb([128, PH], F32, "zz")
    gg = c.sb([128, PH, 16], F32, "gg")
    NUG = 6
    ug = [c.sb([128, D], BF16, "ug%d" % i, dma=True) for i in range(NUG)]
    h2b = c.sb([128, D], BF16, "h2b")

    junk = [c.sb([128, D], BF16, "junk0")] * 2
    araw = c.sb([128, 128], F32, "araw")
    gt1 = c.sb([128, 128], F32, "gt1")
    wgt = c.sb([128, 128], F32, "wgt")
    acc = c.sb([128, D], F32, "acc")

    def bc4(ap3, axis):
        return ap3.unsqueeze(axis).to_broadcast([128, PH, 16, 16])

    def peer_layer(l, N, samp):
        nt = N // 128
        modnorm(l, 2, N, samp, keep_f32=True)
        for sl in range(8):
            t = wslab(w_pq, l * D, sl * 256)
            for jj in range(2):
                hp = sl * 2 + jj
                pst = nps()
                for kc in range(NCH):
                    c.op('pe', lambda pst=pst, t=t, kc=kc, jj=jj: nc.tensor.matmul(
                        pst[:, 0:N], lhsT=t[:, kc, jj * 128:(jj + 1) * 128], rhs=hT[:, kc, 0:N],
                        start=(kc == 0), stop=(kc == NCH - 1)), reads=[t, hT], writes=[pst])
                c.op('act', lambda pst=pst, hp=hp: nc.scalar.copy(out=qT[:, hp, 0:N], in_=pst[:, 0:N]),
                     reads=[pst], writes=[qT])
        r0 = l * 16 * 128
        c.dma('pool', lambda: nc.gpsimd.dma_start(out=skb[:, :, :],
                                                  in_=skT[r0:r0 + 2048, :].rearrange("(hp d) k -> d hp k", d=128)),
              skb, writes=[skb])
        for tb in range(nt):
            cols = slice(tb * 128, (tb + 1) * 128)
            for ch in range(NCH):
                pst = psum[5 + ch % 2]
                c.op('pe', lambda pst=pst, ch=ch, cols=cols: nc.tensor.transpose(out=pst[:, 0:128], in_=scrA[:, ch, cols],
                                                                       identity=ident[:, :]),
                     reads=[scrA, ident], writes=[pst])
                c.op('act', lambda pst=pst, ch=ch: nc.scalar.copy(out=h2b[:, ch * 128:(ch + 1) * 128], in_=pst[:, 0:128]),
                     reads=[pst], writes=[h2b])
            for bq in range(4):
                pst = psum[bq]
                for i in range(4):
                    hp = bq * 4 + i
                    c.op('pe', lambda pst=pst, i=i, hp=hp, cols=cols: nc.tensor.matmul(
                        pst[:, i * 128:(i + 1) * 128], lhsT=qT[:, hp, cols], rhs=skb[:, hp, :], start=True, stop=True),
                        reads=[qT, skb], writes=[pst])
                c.op('act', lambda pst=pst, bq=bq: nc.scalar.copy(out=sc[:, bq * 512:(bq + 1) * 512], in_=pst[:, :]),
                     reads=[pst], writes=[sc])
            for hp in range(16):
                scv = sc[:, hp * 128:(hp + 1) * 128]
                c.op('dve', lambda hp=hp, scv=scv: nc.vector.max(out=sv[:, hp, 0:8], in_=scv), reads=[sc], writes=[sv])
                c.op('dve', lambda hp=hp, scv=scv: nc.vector.max_index(out=si[:, hp, 0:8], in_max=sv[:, hp, 0:8], in_values=scv),
                     reads=[sc, sv], writes=[si])
                c.op('dve', lambda hp=hp, scv=scv: nc.vector.match_replace(out=scv, in_to_replace=sv[:, hp, 0:8],
                                                                           in_values=scv, imm_value=-1e30),
                     reads=[sc, sv], writes=[sc])
                c.op('dve', lambda hp=hp, scv=scv: nc.vector.max(out=sv[:, hp, 8:16], in_=scv), reads=[sc], writes=[sv])
                c.op('dve', lambda hp=hp, scv=scv: nc.vector.max_index(out=si[:, hp, 8:16], in_max=sv[:, hp, 8:16], in_values=scv),
                     reads=[sc, sv], writes=[si])
            c.op('dve', lambda: nc.vector.tensor_copy(out=sif[:, :, :], in_=si[:, :, :]), reads=[si], writes=[sif])
            sv4 = sv[:, :, :].rearrange("p (h t) k -> p h t k", t=2)
            sif4 = sif[:, :, :].rearrange("p (h t) k -> p h t k", t=2)
            cand4 = cand[:, :, :].rearrange("p h (i j) -> p h i j", j=16)
            c.op('dve', lambda: nc.vector.tensor_tensor(out=cand4, in0=bc4(sv4[:, :, 0, :], 3), in1=bc4(sv4[:, :, 1, :], 2),
                                                        op=ALU.add), reads=[sv], writes=[cand])
            for h in range(PH):
                cdv = cand[:, h, :]
                c.op('dve', lambda h=h, cdv=cdv: nc.vector.max(out=cv[:, h, 0:8], in_=cdv), reads=[cand], writes=[cv])
                c.op('dve', lambda h=h, cdv=cdv: nc.vector.max_index(out=cp[:, h, 0:8], in_max=cv[:, h, 0:8], in_values=cdv),
                     reads=[cand, cv], writes=[cp])
                c.op('dve', lambda h=h, cdv=cdv: nc.vector.match_replace(out=cdv, in_to_replace=cv[:, h, 0:8],
                                                                         in_values=cdv, imm_value=-1e30),
                     reads=[cand, cv], writes=[cand])
                c.op('dve', lambda h=h, cdv=cdv: nc.vector.max(out=cv[:, h, 8:16], in_=cdv), reads=[cand], writes=[cv])
                c.op('dve', lambda h=h, cdv=cdv: nc.vector.max_index(out=cp[:, h, 8:16], in_max=cv[:, h, 8:16], in_values=cdv),
                     reads=[cand, cv], writes=[cp])
            c.op('dve', lambda: nc.vector.tensor_single_scalar(out=iiu[:, :, :], in_=cp[:, :, :], scalar=4,
                                                               op=ALU.logical_shift_right), reads=[cp], writes=[iiu])
            c.op('dve', lambda: nc.vector.tensor_single_scalar(out=jju[:, :, :], in_=cp[:, :, :], scalar=15,
                                                               op=ALU.bitwise_and), reads=[cp], writes=[jju])
            c.op('dve', lambda: nc.vector.tensor_copy(out=iif[:, :, :], in_=iiu[:, :, :]), reads=[iiu], writes=[iif])
            c.op('dve', lambda: nc.vector.tensor_copy(out=jjf[:, :, :], in_=jju[:, :, :]), reads=[jju], writes=[jjf])
            eq4 = sc[:, :].rearrange("p (h k i) -> p h k i", k=16, i=16)
            io4 = iota16[:, :].unsqueeze(1).unsqueeze(1).to_broadcast([128, PH, 16, 16])
            for (xf, tsel, sel) in ((iif, 0, selI), (jjf, 1, selJ)):
                c.op('dve', lambda xf=xf: nc.vector.tensor_tensor(out=eq4, in0=bc4(xf[:, :, :], 3), in1=io4, op=ALU.is_equal),
                     reads=[xf, iota16], writes=[sc])
                c.op('dve', lambda tsel=tsel: nc.vector.tensor_tensor(out=eq4, in0=eq4, in1=bc4(sif4[:, :, tsel, :], 2),
                                                                      op=ALU.mult), reads=[sc, sif], writes=[sc])
                c.op('dve', lambda sel=sel: nc.vector.tensor_reduce(out=sel[:, :, :], in_=eq4, axis=AX.X, op=ALU.add),
                     reads=[sc], writes=[sel])
            efv = ef[:, :].rearrange("p (h k) -> p h k", k=16)
            c.op('dve', lambda: nc.vector.scalar_tensor_tensor(out=ef[:, :], in0=selI[:, :, :].rearrange("p h k -> p (h k)"),
                                                               scalar=128.0, in1=selJ[:, :, :].rearrange("p h k -> p (h k)"),
                                                               op0=ALU.mult, op1=ALU.add), reads=[selI, selJ], writes=[ef])
            c.op('dve', lambda: nc.vector.tensor_copy(out=eidx[:, :], in_=ef[:, :]), reads=[ef], writes=[eidx])
            c.op('dve', lambda: nc.vector.tensor_scalar(out=negm[:, :], in0=cv[:, :, 0], scalar1=-1.0, scalar2=1.0,
                                                        op0=ALU.mult, op1=ALU.mult), reads=[cv], writes=[negm])
            for h in range(PH):
                c.op('act', lambda h=h: nc.scalar.activation(out=ee[:, h, :], in_=cv[:, h, :], func=AF.Exp,
                                                             bias=negm[:, h:h + 1], scale=1.0),
                     reads=[cv, negm], writes=[ee])
            c.op('dve', lambda: nc.vector.tensor_reduce(out=zz[:, :], in_=ee[:, :, :], axis=AX.X, op=ALU.add),
                 reads=[ee], writes=[zz])
            c.op('dve', lambda: nc.vector.reciprocal(out=zz[:, :], in_=zz[:, :]), reads=[zz], writes=[zz])
            c.op('dve', lambda: nc.vector.tensor_tensor(out=gg[:, :, :], in0=ee[:, :, :],
                                                        in1=zz[:, :].unsqueeze(2).to_broadcast([128, PH, 16]), op=ALU.mult),
                 reads=[ee, zz], writes=[gg])
            for hk in range(128):
                b_ = ug[hk % NUG]
                c.dma('pool', lambda b_=b_, hk=hk: nc.gpsimd.indirect_dma_start(
                    out=b_[:, :], out_offset=None, in_=EU_d[l][:, :],
                    in_offset=bass.IndirectOffsetOnAxis(ap=eidx[:, hk:hk + 1], axis=0)), b_, reads=[eidx, EU_d[l]], writes=[b_])
                jk = junk[hk % 2]
                c.op('dve', lambda b_=b_, hk=hk, jk=jk: nc.vector.scalar_tensor_tensor(
                    out=jk[:, :], in0=b_[:, :], scalar=1.0, in1=h2b[:, :], op0=ALU.mult, op1=ALU.mult,
                    accum_out=araw[:, hk:hk + 1]), reads=[b_, h2b], writes=[jk, araw])
            c.op('dve', lambda: nc.vector.tensor_tensor(out=gt1[:, :], in0=araw[:, :], in1=araw[:, :], op=ALU.mult),
                 reads=[araw], writes=[gt1])
            c.op('dve', lambda: nc.vector.tensor_scalar(out=gt1[:, :], in0=gt1[:, :], scalar1=0.044715, scalar2=1.0,
                                                        op0=ALU.mult, op1=ALU.add), reads=[gt1], writes=[gt1])
            c.op('dve', lambda: nc.vector.tensor_tensor(out=gt1[:, :], in0=gt1[:, :], in1=araw[:, :], op=ALU.mult),
                 reads=[gt1, araw], writes=[gt1])
            c.op('act', lambda: nc.scalar.activation(out=gt1[:, :], in_=gt1[:, :], func=AF.Sigmoid, scale=1.5957691216),
                 reads=[gt1], writes=[gt1])
            c.op('dve', lambda: nc.vector.tensor_tensor(out=wgt[:, :], in0=gt1[:, :], in1=araw[:, :], op=ALU.mult),
                 reads=[gt1, araw], writes=[wgt])
            c.op('dve', lambda: nc.vector.tensor_tensor(out=wgt[:, :], in0=wgt[:, :],
                                                        in1=gg[:, :, :].rearrange("p h k -> p (h k)"), op=ALU.mult),
                 reads=[wgt, gg], writes=[wgt])
            for hk in range(128):
                b_ = ug[(hk + 3) % NUG]
                c.dma('pool', lambda b_=b_, hk=hk: nc.gpsimd.indirect_dma_start(
                    out=b_[:, :], out_offset=None, in_=EV_d[l][:, :],
                    in_offset=bass.IndirectOffsetOnAxis(ap=eidx[:, hk:hk + 1], axis=0)), b_, reads=[eidx, EV_d[l]], writes=[b_])
                if hk == 0:
                    c.op('dve', lambda b_=b_, hk=hk: nc.vector.tensor_scalar(out=acc[:, :], in0=b_[:, :], scalar1=wgt[:, hk:hk + 1],
                                                                             scalar2=1.0, op0=ALU.mult, op1=ALU.mult),
                         reads=[b_, wgt], writes=[acc])
                else:
                    c.op('dve', lambda b_=b_, hk=hk: nc.vector.scalar_tensor_tensor(
                        out=acc[:, :], in0=b_[:, :], scalar=wgt[:, hk:hk + 1], in1=acc[:, :], op0=ALU.mult, op1=ALU.add),
                        reads=[b_, wgt, acc], writes=[acc])
            for ch in range(NCH):
                pst = psum[5 + ch % 2]
                c.op('pe', lambda pst=pst, ch=ch: nc.tensor.transpose(out=pst[:, 0:128], in_=acc[:, ch * 128:(ch + 1) * 128],
                                                                       identity=ident[:, :]),
                     reads=[acc, ident], writes=[pst])
                gidx = A(l, 5, ch)
                if not samp:
                    c.op('dve', lambda pst=pst, ch=ch, gidx=gidx, cols=cols: nc.vector.scalar_tensor_tensor(
                        out=xT[:, ch, cols], in0=pst[:, 0:128], scalar=ada[:, gidx, 0:1], in1=xT[:, ch, cols],
                        op0=ALU.mult, op1=ALU.add), reads=[pst, ada, xT], writes=[xT])
                else:
                    c.op('dve', lambda pst=pst, gidx=gidx: nc.vector.tensor_tensor(
                        out=v3(tmpN[:, 0:128]), in0=v3(pst[:, 0:128]), in1=abc(gidx), op=ALU.mult),
                        reads=[pst, ada], writes=[tmpN])
                    c.op('dve', lambda ch=ch: nc.vector.tensor_tensor(out=xT[:, ch, 0:128], in0=xT[:, ch, 0:128],
                                                                      in1=tmpN[:, 0:128], op=ALU.add),
                         reads=[xT, tmpN], writes=[xT])


    ptab_sb = c.sb([128, SPC * NPAGES], I32, "ptab_sb", dma=True)
    ptf = c.sb([128, SPC * NPAGES], F32, "ptf")
    idxpg = c.sb([128, SPC * NPAGES], I32, "idxpg")
    if do_sample:
        c.dma('sp', lambda: nc.sync.dma_start(out=ptab_sb[:, :], in_=ptab[:, :]), ptab_sb, writes=[ptab_sb])
        c.op('dve', lambda: nc.vector.tensor_copy(out=ptf[:, :], in_=ptab_sb[:, :]), reads=[ptab_sb], writes=[ptf])
        c.op('dve', lambda: nc.vector.tensor_scalar(out=ptf[:, :], in0=ptf[:, :], scalar1=float(PAGE), scalar2=iotap[:, 0:1],
                                                    op0=ALU.mult, op1=ALU.add), reads=[ptf, iotap], writes=[ptf])
        c.op('dve', lambda: nc.vector.tensor_copy(out=idxpg[:, :], in_=ptf[:, :]), reads=[ptf], writes=[idxpg])

    def kv_project(N, samp, t0):
        nt = N // 128
        rms_stats(N)
        for ch in range(NCH):
            c.op('dve', lambda ch=ch: nc.vector.scalar_tensor_tensor(
                out=hT[:, ch, 0:N], in0=xT[:, ch, 0:N], scalar=nkv_sb[:, ch:ch + 1], in1=rstd[:, 0:N],
                op0=ALU.mult, op1=ALU.mult), reads=[xT, nkv_sb, rstd], writes=[hT])
        for (W, outd, is_v) in ((w_k, (k_s if samp else k_p), False), (w_v, (v_s if samp else v_p), True)):
            if DBG_KV < 2 or (is_v and DBG_KV < 3):
                continue
            for sl in range(4):
                t = wslab(W, 0, sl * 256)
                for tb in range(nt):
                    pst = nps()
                    for kc in range(NCH):
                        c.op('pe', lambda pst=pst, t=t, kc=kc, tb=tb: nc.tensor.matmul(
                            pst[:, 0:256], lhsT=hT[:, kc, tb * 128:(tb + 1) * 128], rhs=t[:, kc, 0:256],
                            start=(kc == 0), stop=(kc == NCH - 1)), reads=[t, hT], writes=[pst])
                    c.op('act', lambda pst=pst, tb=tb, sl=sl: nc.scalar.copy(out=tm[:, tb, sl * 256:(sl + 1) * 256], in_=pst[:, 0:256]),
                         reads=[pst], writes=[tm])
                    if is_v:
                        c.op('dve', lambda tb=tb, sl=sl: nc.vector.tensor_copy(
                            out=vbf[:, tb, sl * 256:(sl + 1) * 256], in_=tm[:, tb, sl * 256:(sl + 1) * 256]), reads=[tm], writes=[vbf])
            c.dma('sp', lambda outd=outd: nc.sync.dma_start(out=outd[t0:t0 + N, :].rearrange("(n p) d -> p n d", p=128),
                                                            in_=tm[:, 0:nt, :]), tm, reads=[tm], out_dram=True)
            if is_v:
                if samp:
                    c.dma('sp', lambda: nc.sync.dma_start(out=VS_d[0:128, :], in_=vbf[:, 0, :]), vbf, reads=[vbf], writes=[VS_d])
                else:
                    c.dma('sp', lambda: nc.sync.dma_start(out=V_d[t0:t0 + N, :].rearrange("(n p) d -> p n d", p=128),
                                                          in_=vbf[:, 0:nt, :]), vbf, reads=[vbf], writes=[V_d])
        if DBG_KV < 4:
            return
        for sl in range(4):
            t = wslab(w_k, 0, sl * 256)
            for hh in range(4):
                h = sl * 4 + hh
                pst = nps()
                for kc in range(NCH):
                    c.op('pe', lambda pst=pst, t=t, kc=kc, hh=hh: nc.tensor.matmul(
                        pst[0:64, 0:N], lhsT=t[:, kc, hh * 64:(hh + 1) * 64], rhs=hT[:, kc, 0:N],
                        start=(kc == 0), stop=(kc == NCH - 1)), reads=[t, hT], writes=[pst])
                c.op('act', lambda pst=pst, h=h: nc.scalar.copy(out=ktn[0:64, h, 0:N], in_=pst[0:64, 0:N]),
                     reads=[pst], writes=[ktn])
        if not samp:
            c.dma('sp', lambda: nc.sync.dma_start(out=KT_d[:, :, :].rearrange("h p t -> p h t")[:, :, t0:t0 + N],
                                                  in_=ktn[0:64, :, 0:N]), ktn, reads=[ktn], writes=[KT_d])

    def q_project(j, N, samp):
        l = NA + j
        modnorm(l, 1, N, samp)
        for sl in range(4):
            t = wslab(w_q, j * D, sl * 256)
            for hh in range(4):
                h = sl * 4 + hh
                pst = nps()
                for kc in range(NCH):
                    c.op('pe', lambda pst=pst, t=t, kc=kc, hh=hh: nc.tensor.matmul(
                        pst[0:64, 0:N], lhsT=t[:, kc, hh * 64:(hh + 1) * 64], rhs=hT[:, kc, 0:N],
                        start=(kc == 0), stop=(kc == NCH - 1)), reads=[t, hT], writes=[pst])
                c.op('act', lambda pst=pst, h=h: nc.scalar.activation(out=qT[0:64, h, 0:N], in_=pst[0:64, 0:N],
                                                                      func=AF.Copy, scale=0.125),
                     reads=[pst], writes=[qT])

    def o_project(j, N, samp):
        l = NA + j
        for ch in range(NCH):
            t = wslab(w_o, j * D, ch * 128, kmode=64)
            tv = t[0:64, :, :].rearrange("p k (a n) -> p (k a) n", n=128)
            pst = nps()
            for h in range(NH):
                c.op('pe', lambda pst=pst, tv=tv, h=h: nc.tensor.matmul(
                    pst[:, 0:N], lhsT=tv[:, h, :], rhs=attT[0:64, h, 0:N], start=(h == 0), stop=(h == NH - 1)),
                    reads=[t, attT], writes=[pst])
            resid_add(pst, pst[:, 0:N], None, A(l, 2, ch), ch, N, samp)

    def csum_update(first, nk, W):
        if first:
            c.op('dve', lambda: nc.vector.tensor_copy(out=C32[0:nk, 0:W], in_=lp_t[0:nk, 0:W]), reads=[lp_t], writes=[C32])
        else:
            c.op('dve', lambda: nc.vector.tensor_tensor(out=C32[0:nk, 0:W], in0=C32[0:nk, 0:W], in1=lp_t[0:nk, 0:W], op=ALU.add),
                 reads=[C32, lp_t], writes=[C32])
        c.op('dve', lambda: nc.vector.tensor_copy(out=c16[:, 0:W], in_=C32[:, 0:W]), reads=[C32], writes=[c16])

    def attention_prompt(j, N, t0):
        q_project(j, N, False)
        if DBG_ATT < 3:
            return
        nb = (t0 + N) // 128
        kb0 = t0 // 128
        U2 = [u_t, u_tB]
        LP2 = [lp_t, lp_tB]
        PSZ = [psum[4], psum[5]]
        PSA = [psum[6], psum[0]]
        pso = psum[7]
        for h in range(NH):
            c.dma('sp', lambda h=h: nc.sync.dma_start(out=KTb[0:64, 0:t0 + N], in_=KT_d[h, :, 0:t0 + N]),
                  KTb, reads=[KT_d], writes=[KTb])
            c.dma('sp', lambda h=h: nc.sync.dma_start(
                out=Vb[:, 0:nb, :], in_=V_d[0:t0 + N, h * DH:(h + 1) * DH].rearrange("(kb p) d -> p kb d", p=128)),
                Vb, reads=[V_d], writes=[Vb])
            bcol = j * NH + h
            kbs = list(reversed(range(nb)))

            def stage1(idx, h=h, bcol=bcol):
                kb = kbs[idx]
                r = kb - kb0
                psz, ut, lpt = PSZ[idx % 2], U2[idx % 2], LP2[idx % 2]
                ks = slice(kb * 128, (kb + 1) * 128)
                c.op('pe', lambda: nc.tensor.matmul(psz[:, 0:N], lhsT=KTb[0:64, ks], rhs=qT[0:64, h, 0:N],
                                                    start=True, stop=True), reads=[KTb, qT], writes=[psz])
                c.op('act', lambda: nc.scalar.activation(out=ut[:, 0:N], in_=psz[:, 0:N], func=AF.Exp,
                                                         bias=bsb_sb[:, bcol:bcol + 1], scale=1.0),
                     reads=[psz, bsb_sb], writes=[ut])
                c.op('act', lambda: nc.scalar.activation(out=lpt[:, 0:N], in_=ut[:, 0:N], func=AF.Ln, bias=1.0, scale=1.0),
                     reads=[ut], writes=[lpt])
                if r >= 0:
                    c.op('pool', lambda: nc.gpsimd.affine_select(out=lpt[:, 0:N], in_=lpt[:, 0:N], pattern=[[1, N]],
                                                                 compare_op=ALU.is_gt, fill=0.0, base=-r * 128,
                                                                 channel_multiplier=-1), reads=[lpt], writes=[lpt])

            def stage2(idx, h=h, bcol=bcol):
                kb = kbs[idx]
                r = kb - kb0
                first = idx == 0
                psa, lpt = PSA[idx % 2], LP2[idx % 2]
                ks = slice(kb * 128, (kb + 1) * 128)
                c.op('pe', lambda: nc.tensor.matmul(psa[:, 0:N], lhsT=KTb[0:64, ks], rhs=qT[0:64, h, 0:N],
                                                    start=True, stop=False), reads=[KTb, qT], writes=[psa])
                c.op('pe', lambda: nc.tensor.matmul(psa[:, 0:N], lhsT=negtri[:, :], rhs=lpt[:, 0:N],
                                                    start=False, stop=first), reads=[negtri, lpt], writes=[psa])
                if not first:
                    c.op('pe', lambda: nc.tensor.matmul(psa[:, 0:N], lhsT=negones[:, :], rhs=c16[:, 0:N],
                                                        start=False, stop=True), reads=[negones, c16], writes=[psa])
                c.op('act', lambda: nc.scalar.activation(out=at_t[:, 0:N], in_=psa[:, 0:N], func=AF.Exp,
                                                         bias=bsb_sb[:, bcol:bcol + 1], scale=1.0),
                     reads=[psa, bsb_sb], writes=[at_t])
                if r >= 0:
                    c.op('pool', lambda: nc.gpsimd.affine_select(out=at_t[:, 0:N], in_=at_t[:, 0:N], pattern=[[1, N]],
                                                                 compare_op=ALU.is_gt, fill=0.0, base=-r * 128,
                                                                 channel_multiplier=-1), reads=[at_t], writes=[at_t])
                c.op('pe', lambda: nc.tensor.matmul(pso[0:64, 0:N], lhsT=Vb[:, kb, :], rhs=at_t[:, 0:N],
                                                    start=first, stop=(kb == 0)), reads=[Vb, at_t], writes=[pso])
                if kb > 0:
                    if first:
                        c.op('dve', lambda: nc.vector.tensor_copy(out=C32[:, 0:N], in_=lpt[:, 0:N]), reads=[lpt], writes=[C32])
                    else:
                        c.op('dve', lambda: nc.vector.tensor_tensor(out=C32[:, 0:N], in0=C32[:, 0:N], in1=lpt[:, 0:N], op=ALU.add),
                             reads=[C32, lpt], writes=[C32])
                    c.op('dve', lambda: nc.vector.tensor_copy(out=c16[:, 0:N], in_=C32[:, 0:N]), reads=[C32], writes=[c16])

            stage1(0)
            for idx in range(nb):
                if idx + 1 < nb:
                    stage1(idx + 1)
                stage2(idx)
            c.op('act', lambda h=h: nc.scalar.copy(out=attT[0:64, h, 0:N], in_=pso[0:64, 0:N]), reads=[pso], writes=[attT])
        if DBG_ATT >= 4:
            o_project(j, N, False)

    def attention_sample(j):
        N = 128
        q_project(j, N, True)
        ktp = KTb
        ktp3 = KTb[0:64, 0:NH * 128].rearrange("p (h n) -> p h n", n=128)
        c.op('dve', lambda: nc.vector.tensor_copy(
            out=bias_s[:, :].rearrange("p (h t) -> p h t", t=DEC_T),
            in_=bsb_sb[:, j * NH:(j + 1) * NH].unsqueeze(2).to_broadcast([128, NH, DEC_T])), reads=[bsb_sb], writes=[bias_s])
        for s in range(SPC):
            qs = slice(s * DEC_T, (s + 1) * DEC_T)
            c.dma('sp', lambda s=s: nc.sync.dma_start(out=vbf[0:DEC_T, 1, :], in_=VS_d[s * DEC_T:(s + 1) * DEC_T, :]),
                  vbf, reads=[VS_d], writes=[vbf])
            c.op('dve', lambda: nc.vector.memset(C32[:, 0:128], 0.0), writes=[C32])
            blocks = ['new'] + list(reversed(range(NPAGES)))
            for bi, blk in enumerate(blocks):
                first = bi == 0
                if blk == 'new':
                    nk = DEC_T
                    ksrc = lambda h, qs=qs: ktn[0:64, h, qs]
                    vsrc = lambda h: vbf[0:DEC_T, 1, h * DH:(h + 1) * DH]
                    ktk, vtk = ktn, vbf
                else:
                    nk = 128
                    col = s * NPAGES + blk
                    c.dma('pool', lambda col=col: nc.gpsimd.indirect_dma_start(
                        out=tm[:, 0, :], out_offset=None, in_=cache_k[:, :],
                        in_offset=bass.IndirectOffsetOnAxis(ap=idxpg[:, col:col + 1], axis=0)), tm, reads=[idxpg], writes=[tm])
                    c.dma('pool', lambda col=col: nc.gpsimd.indirect_dma_start(
                        out=tm[:, 1, :], out_offset=None, in_=cache_v[:, :],
                        in_offset=bass.IndirectOffsetOnAxis(ap=idxpg[:, col:col + 1], axis=0)), tm, reads=[idxpg], writes=[tm])
                    for bq in range(4):
                        pst = psum[bq]
                        for i in range(4):
                            h = bq * 4 + i
                            c.op('pe', lambda pst=pst, i=i, h=h: nc.tensor.transpose(
                                out=pst[0:64, i * 128:(i + 1) * 128], in_=tm[:, 0, h * DH:(h + 1) * DH], identity=ident[:, :]),
                                reads=[tm, ident], writes=[pst])
                        c.op('act', lambda pst=pst, bq=bq: nc.scalar.copy(
                            out=ktp3[:, bq * 4:(bq + 1) * 4, :], in_=pst[0:64, :].rearrange("p (a n) -> p a n", n=128)),
                            reads=[pst], writes=[ktp])
                    c.op('dve', lambda: nc.vector.tensor_copy(out=vbf[:, 0, :], in_=tm[:, 1, :]), reads=[tm], writes=[vbf])
                    ksrc = lambda h: ktp3[:, h, :]
                    vsrc = lambda h: vbf[:, 0, h * DH:(h + 1) * DH]
                    ktk, vtk = ktp, vbf
                psz, psa, pso = psum[4], psum[6], psum[7]
                for h in range(NH):
                    c.op('pe', lambda h=h, ksrc=ksrc, nk=nk, qs=qs: nc.tensor.matmul(
                        psz[0:nk, h * DEC_T:(h + 1) * DEC_T], lhsT=ksrc(h), rhs=qT[0:64, h, qs], start=True, stop=True),
                        reads=[ktk, qT], writes=[psz])
                c.op('dve', lambda nk=nk: nc.vector.tensor_tensor(out=tmpN[0:nk, 0:128], in0=psz[0:nk, 0:128], in1=bias_s[0:nk, :],
                                                                  op=ALU.add), reads=[psz, bias_s], writes=[tmpN])
                c.op('act', lambda nk=nk: nc.scalar.activation(out=u_t[0:nk, 0:128], in_=tmpN[0:nk, 0:128], func=AF.Exp),
                     reads=[tmpN], writes=[u_t])
                c.op('act', lambda nk=nk: nc.scalar.activation(out=lp_t[0:nk, 0:128], in_=u_t[0:nk, 0:128], func=AF.Ln,
                                                               bias=1.0, scale=1.0), reads=[u_t], writes=[lp_t])
                if blk == 'new':
                    c.op('pool', lambda nk=nk: nc.gpsimd.affine_select(
                        out=lp_t[0:nk, 0:128], in_=lp_t[0:nk, 0:128], pattern=[[0, NH], [1, DEC_T]], compare_op=ALU.is_gt,
                        fill=0.0, base=0, channel_multiplier=-1), reads=[lp_t], writes=[lp_t])
                c.op('pe', lambda nk=nk, first=first: nc.tensor.matmul(psa[0:nk, 0:128], lhsT=negtri[0:nk, 0:nk], rhs=lp_t[0:nk, 0:128],
                                                                        start=True, stop=first), reads=[negtri, lp_t], writes=[psa])
                if not first:
                    c.op('pe', lambda nk=nk: nc.tensor.matmul(psa[0:nk, 0:128], lhsT=negones[:, 0:nk], rhs=c16[:, 0:128],
                                                               start=False, stop=True), reads=[negones, c16], writes=[psa])
                c.op('act', lambda nk=nk: nc.scalar.activation(out=tmpM[0:nk, 0:128], in_=psa[0:nk, 0:128], func=AF.Exp),
                     reads=[psa], writes=[tmpM])
                c.op('dve', lambda nk=nk: nc.vector.tensor_tensor(out=at_t[0:nk, 0:128], in0=tmpM[0:nk, 0:128], in1=u_t[0:nk, 0:128],
                                                                  op=ALU.mult), reads=[tmpM, u_t], writes=[at_t])
                if blk == 'new':
                    c.op('pool', lambda nk=nk: nc.gpsimd.affine_select(
                        out=at_t[0:nk, 0:128], in_=at_t[0:nk, 0:128], pattern=[[0, NH], [1, DEC_T]], compare_op=ALU.is_gt,
                        fill=0.0, base=0, channel_multiplier=-1), reads=[at_t], writes=[at_t])
                for h in range(NH):
                    c.op('pe', lambda h=h, vsrc=vsrc, nk=nk: nc.tensor.matmul(
                        pso[0:64, h * DEC_T:(h + 1) * DEC_T], lhsT=vsrc(h), rhs=at_t[0:nk, h * DEC_T:(h + 1) * DEC_T],
                        start=True, stop=True), reads=[vtk, at_t], writes=[pso])
                if first:
                    c.op('dve', lambda: nc.vector.tensor_copy(out=oacc[:, :], in_=pso[0:64, 0:128]), reads=[pso], writes=[oacc])
                else:
                    c.op('dve', lambda: nc.vector.tensor_tensor(out=oacc[:, :], in0=oacc[:, :], in1=pso[0:64, 0:128], op=ALU.add),
                         reads=[pso, oacc], writes=[oacc])
                if bi < len(blocks) - 1:
                    csum_update(False, nk, 128)
            c.op('act', lambda qs=qs: nc.scalar.copy(out=attT[0:64, :, qs], in_=oacc[:, :].rearrange("p (h t) -> p h t", t=DEC_T)),
                 reads=[oacc], writes=[attT])
        o_project(j, N, True)

    def final_out(dst_ap, N):
        nt = N // 128
        rms_stats(N)
        for ch in range(NCH):
            c.op('dve', lambda ch=ch: nc.vector.scalar_tensor_tensor(
                out=scrA[:, ch, 0:N], in0=xT[:, ch, 0:N], scalar=fn_sb[:, ch:ch + 1], in1=rstd[:, 0:N],
                op0=ALU.mult, op1=ALU.mult), reads=[xT, fn_sb, rstd], writes=[scrA])
        for tb in range(nt):
            for ch in range(NCH):
                pst = nps()
                c.op('pe', lambda pst=pst, tb=tb, ch=ch: nc.tensor.transpose(
                    out=pst[:, 0:128], in_=scrA[:, ch, tb * 128:(tb + 1) * 128], identity=ident[:, :]),
                    reads=[scrA, ident], writes=[pst])
                c.op('act', lambda pst=pst, tb=tb, ch=ch: nc.scalar.copy(
                    out=tm[:, tb, ch * 128:(ch + 1) * 128], in_=pst[:, 0:128]), reads=[pst], writes=[tm])
        c.dma('sp', lambda: nc.sync.dma_start(out=dst_ap.rearrange("(n p) d -> p n d", p=128), in_=tm[:, 0:nt, :]),
              tm, reads=[tm], out_dram=True)

    def run_tile(src, dst, N, samp, ti, last):
        load_x_tile(src, N)
        for l in range(n_layers):
            if l < NA:
                conv_layer(l, N, samp, ti, last)
            else:
                if l == NA:
                    kv_project(N, samp, 0 if samp else ti * TP)
                if DBG_ATT >= 2:
                    if samp:
                        attention_sample(l - NA)
                    else:
                        attention_prompt(l - NA, N, ti * TP)
            if DO_PEER:
                peer_layer(l, N, samp)
        final_out(dst, N)

    for ti in range(n_ptiles):
        run_tile(x_p[ti * TP:(ti + 1) * TP, :], y_p[ti * TP:(ti + 1) * TP, :], TP, False, ti, ti == n_ptiles - 1)
    if do_sample:
        run_tile(x_s[:, :], y_s[:, :], 128, True, 0, True)

    c.emit()


def fm(v):
    v = np.asarray(v, np.float32)
    lead = int(np.prod(v.shape[:-1])) if v.ndim > 1 else 1
    n = v.shape[-1] // 128
    return np.ascontiguousarray(v.reshape(lead, n, 128).transpose(2, 0, 1).reshape(128, lead * n))


def _shared_inputs(inp):
    f32 = lambda a: np.ascontiguousarray(np.asarray(a, np.float32))
    sh = {}
    sh["w_ada"] = f32(inp["w_ada"]).reshape(DEPTH * D, 6 * D)
    sh["b_ada"] = fm(inp["b_ada"].reshape(DEPTH * 48, 128).reshape(-1))
    sh["nmix"] = fm(inp["norm_mix"].reshape(-1))
    sh["nffn"] = fm(inp["norm_ffn"].reshape(-1))
    sh["nkv"] = fm(inp["norm_kv"])
    sh["fnorm"] = fm(inp["final_norm"])
    sh["w_pw1"] = f32(inp["w_pw1"]).reshape(NA * D, 2 * D)
    sh["b_pw1"] = fm(inp["b_pw1"].reshape(-1))
    wd = np.asarray(inp["w_dw"], np.float32).reshape(NA, CW, NCH, 128)
    sh["w_dw"] = np.ascontiguousarray(wd.transpose(3, 0, 2, 1).reshape(128, NA * NCH * CW))
    sh["b_dw"] = fm(inp["b_dw"].reshape(-1))
    sh["ln_g"] = fm(inp["ln_g"].reshape(-1))
    sh["ln_b"] = fm(inp["ln_b"].reshape(-1))
    sh["w_pw2"] = f32(inp["w_pw2"]).reshape(NA * D, D)
    sh["b_pw2"] = fm(inp["b_pw2"].reshape(-1))
    sh["w_k"] = f32(inp["w_k"])
    sh["w_v"] = f32(inp["w_v"])
    sh["w_q"] = f32(inp["w_q"]).reshape(2 * D, D)
    sh["w_o"] = f32(inp["w_o"]).reshape(2 * D, D)
    sh["bsb"] = np.ascontiguousarray(np.broadcast_to(np.asarray(inp["b_sb"], np.float32).reshape(1, 2 * NH), (128, 2 * NH)))
    sh["w_pq"] = f32(inp["w_pq"]).reshape(DEPTH * D, 2 * D)
    sk = np.asarray(inp["sub_keys"], np.float32).transpose(0, 1, 2, 4, 3)
    sh["skT"] = np.ascontiguousarray(sk.reshape(DEPTH * 16 * 128, 128))
    sh["exp_u"] = f32(inp["expert_u"]).reshape(DEPTH * NEXP, D)
    sh["exp_v"] = f32(inp["expert_v"]).reshape(DEPTH * NEXP, D)
    sh["cache_k"] = f32(inp["cache_k"]).reshape(NPOOL * PAGE, D)
    sh["cache_v"] = f32(inp["cache_v"]).reshape(NPOOL * PAGE, D)
    return sh


def _core_inputs(inp, sh, core, n_ptiles=SEQ // TP):
    b = core % NB
    m = dict(sh)
    m["x_p"] = np.ascontiguousarray(np.asarray(inp["x_prompt"][b, :n_ptiles * TP], np.float32))
    s0 = core * SPC
    m["x_s"] = np.ascontiguousarray(np.asarray(inp["x_sample"][s0:s0 + SPC], np.float32).reshape(128, D))
    m["c_all"] = np.ascontiguousarray(np.concatenate([np.asarray(inp["c_prompt"][b:b + 1], np.float32),
                                                      np.asarray(inp["c_sample"][s0:s0 + SPC], np.float32)], axis=0))
    pt = np.asarray(inp["page_table"][s0:s0 + SPC], np.int32).reshape(1, SPC * NPAGES)
    m["ptab"] = np.ascontiguousarray(np.broadcast_to(pt, (128, SPC * NPAGES)))
    m["st_conv"] = np.ascontiguousarray(np.asarray(inp["state_conv"][:, s0:s0 + SPC], np.float32).reshape(NA * SPC * HIST, D))
    return m


_NC_CACHE = {}


def kernel(**inputs):
    if "nc" not in _NC_CACHE:
        _NC_CACHE["nc"] = build_nc()
    nc = _NC_CACHE["nc"]
    sh = _shared_inputs(inputs)
    in_maps = [_core_inputs(inputs, sh, cix) for cix in range(NCORES)]
    res = run_bass_kernel_spmd(nc, in_maps, core_ids=list(range(NCORES))).results
    y_prompt = np.stack([res[b]["y_p"] for b in range(NB)], axis=0)
    y_sample = np.concatenate([res[cix]["y_s"] for cix in range(NCORES)], axis=0).reshape(DEC_B, DEC_T, D)
    conv_prompt = np.stack([res[b]["conv_p"].reshape(NA, HIST, D) for b in range(NB)], axis=1)
    conv_sample = np.concatenate([res[cix]["conv_s"].reshape(NA, SPC, HIST, D) for cix in range(NCORES)], axis=1)
    k_prompt = np.stack([res[b]["k_p"] for b in range(NB)], axis=0).reshape(NB, SEQ, NH, DH)
    v_prompt = np.stack([res[b]["v_p"] for b in range(NB)], axis=0).reshape(NB, SEQ, NH, DH)
    k_sample = np.concatenate([res[cix]["k_s"] for cix in range(NCORES)], axis=0).reshape(DEC_B, DEC_T, NH, DH)
    v_sample = np.concatenate([res[cix]["v_s"] for cix in range(NCORES)], axis=0).reshape(DEC_B, DEC_T, NH, DH)
    return (y_prompt, y_sample, conv_prompt, conv_sample, k_prompt, v_prompt, k_sample, v_sample)
```

```python
from contextlib import ExitStack
import numpy as np
import concourse.bass as bass
import concourse.mybir as mybir
from concourse.bass_utils import run_bass_kernel_spmd

F32 = mybir.dt.float32
BF16 = mybir.dt.bfloat16
I32 = mybir.dt.int32
U32 = mybir.dt.uint32
ALU = mybir.AluOpType
AF = mybir.ActivationFunctionType
AX = mybir.AxisListType

D = 1024
NCH = 8
SEQ = 4096
NB = 4
DEC_B = 128
DEC_T = 8
PAST = 2048
PAGE = 128
NPAGES = 16
NPOOL = 2560
NH = 16
DH = 64
CW = 31
HIST = 30
DEPTH = 4
NA = 2
PH = 8
PK = 16
NKEYS = 128
NEXP = 16384
EPS = 1e-6
TP = 256
DO_PEER = True
DBG_ATT = 9
DBG_KV = 9
NCORES = 8
SPC = DEC_B // NCORES


class Tk:
    def __init__(self, t, name, sem=None):
        self.t = t
        self.name = name
        self.w = None
        self.r = []
        self.dsem = sem
        self.dcnt = 0

    def __getitem__(self, idx):
        return self.t[idx]


class Ctx:
    def __init__(self, nc, stack):
        self.nc = nc
        self.stack = stack
        self.engs = {'pe': nc.tensor, 'dve': nc.vector, 'act': nc.scalar, 'pool': nc.gpsimd, 'sp': nc.sync}
        self.sem = {k: stack.enter_context(nc.semaphore("s_" + k)) for k in self.engs}
        self.cnt = {k: 0 for k in self.engs}
        self.waited = {k: {} for k in self.engs}
        self.prog = {k: [] for k in self.engs}
        self.out_events = []
        self.ntile = 0

    def sb(self, shape, dt, name, dma=False):
        t = self.stack.enter_context(self.nc.sbuf_tensor(name, list(shape), dt))
        self.sbytes = getattr(self, "sbytes", 0) + int(np.prod(shape[1:])) * (2 if dt == BF16 else 4)
        sem = self.stack.enter_context(self.nc.semaphore("d_" + name)) if dma else None
        return Tk(t, name, sem)

    def ps(self, name):
        t = self.stack.enter_context(self.nc.psum_tensor(name, [128, 512], F32))
        tk = Tk(t, name)
        tk.excl = True
        return tk

    def dram(self, name, shape, dt, dma=False):
        t = self.nc.dram_tensor(name, list(shape), dt, kind="ExternalOutput").ap()
        sem = self.stack.enter_context(self.nc.semaphore("d_" + name)) if dma else None
        return Tk(t, name, sem)

    def _deps(self, e, reads, writes):
        evs = []
        for tk in reads:
            if tk.w is not None:
                evs.append(tk.w)
            if getattr(tk, "excl", False):
                evs.extend(tk.r)
        for tk in writes:
            if tk.w is not None:
                evs.append(tk.w)
            evs.extend(tk.r)
        for (sem, val, src) in evs:
            if e == 'pe' and src == 'pe':
                continue
            key = id(sem)
            if self.waited[e].get(key, 0) >= val:
                continue
            self.waited[e][key] = val
            self.prog[e].append(('wait', sem, val))

    def _mark(self, ev, reads, writes):
        for tk in reads:
            tk.r.append(ev)
        for tk in writes:
            tk.w = ev
            tk.r = []

    def op(self, e, fn, reads=(), writes=()):
        self._deps(e, reads, writes)
        self.cnt[e] += 1
        ev = (self.sem[e], self.cnt[e], e)
        self.prog[e].append(('op', fn, self.sem[e], 1))
        self._mark(ev, reads, writes)
        return ev

    def dma(self, q, fn, sbt, reads=(), writes=(), out_dram=False):
        self._deps(q, reads, writes)
        sbt.dcnt += 1
        ev = (sbt.dsem, 16 * sbt.dcnt, 'dma')
        self.prog[q].append(('op', fn, sbt.dsem, 16))
        self._mark(ev, reads, writes)
        if out_dram:
            self.out_events.append(ev)
        return ev

    def emit(self):
        nc = self.nc
        last = {}
        for (sem, val, _) in self.out_events:
            k = id(sem)
            if k not in last or last[k][1] < val:
                last[k] = (sem, val)
        for (sem, val) in last.values():
            self.prog['sp'].append(('wait', sem, val))

        def run(eng, lst):
            for it in lst:
                if it[0] == 'wait':
                    eng.wait_ge(it[1], it[2])
                else:
                    it[1]().then_inc(it[2], it[3])

        with nc.Block() as block:
            @block.tensor
            def _(e):
                run(nc.tensor, self.prog['pe'])

            @block.vector
            def _(e):
                run(nc.vector, self.prog['dve'])

            @block.scalar
            def _(e):
                run(nc.scalar, self.prog['act'])

            @block.gpsimd
            def _(e):
                run(nc.gpsimd, self.prog['pool'])

            @block.sync
            def _(e):
                run(nc.sync, self.prog['sp'])


def build_nc(n_ptiles=SEQ // TP, do_sample=True, n_layers=DEPTH):
    nc = bass.Bass("TRN2", target_bir_lowering=False)
    with ExitStack() as stack:
        c = Ctx(nc, stack)
        _build(c, n_ptiles, do_sample, n_layers)
    return nc


def _build(c, n_ptiles, do_sample, n_layers):
    nc = c.nc

    def din(name, shape, dt=F32):
        return nc.dram_tensor(name, list(shape), dt, kind="ExternalInput").ap()

    def dout(name, shape, dt=F32):
        return nc.dram_tensor(name, list(shape), dt, kind="ExternalOutput").ap()

    NTOK_P = n_ptiles * TP
    x_p = din("x_p", [NTOK_P, D])
    x_s = din("x_s", [128, D])
    c_all = din("c_all", [17, D])
    w_ada = din("w_ada", [DEPTH * D, 6 * D])
    b_ada = din("b_ada", [128, DEPTH * 48])
    nmix = din("nmix", [128, DEPTH * NCH])
    nffn = din("nffn", [128, DEPTH * NCH])
    nkv = din("nkv", [128, NCH])
    fnorm = din("fnorm", [128, NCH])
    w_pw1 = din("w_pw1", [NA * D, 2 * D])
    b_pw1 = din("b_pw1", [128, NA * 16])
    w_dw = din("w_dw", [128, NA * NCH * CW])
    b_dw = din("b_dw", [128, NA * NCH])
    ln_g = din("ln_g", [128, NA * NCH])
    ln_b = din("ln_b", [128, NA * NCH])
    w_pw2 = din("w_pw2", [NA * D, D])
    b_pw2 = din("b_pw2", [128, NA * NCH])
    w_k = din("w_k", [D, D])
    w_v = din("w_v", [D, D])
    w_q = din("w_q", [2 * D, D])
    w_o = din("w_o", [2 * D, D])
    bsb = din("bsb", [128, 2 * NH])
    w_pq = din("w_pq", [DEPTH * D, 2 * D])
    skT = din("skT", [DEPTH * 16 * 128, 128])
    exp_u = din("exp_u", [DEPTH * NEXP, D])
    exp_v = din("exp_v", [DEPTH * NEXP, D])
    npool = NPOOL if do_sample else 16
    cache_k = din("cache_k", [npool * PAGE, D])
    cache_v = din("cache_v", [npool * PAGE, D])
    ptab = din("ptab", [128, SPC * NPAGES], I32)
    st_conv = din("st_conv", [NA * SPC * HIST, D])

    y_p = dout("y_p", [NTOK_P, D])
    y_s = dout("y_s", [128, D])
    conv_p = dout("conv_p", [NA * HIST, D])
    conv_s = dout("conv_s", [NA * SPC * HIST, D])
    k_p = dout("k_p", [NTOK_P, D])
    v_p = dout("v_p", [NTOK_P, D])
    k_s = dout("k_s", [128, D])
    v_s = dout("v_s", [128, D])

    KT_d = c.dram("KT_d", [NH, DH, SEQ], BF16)
    V_d = c.dram("V_d", [SEQ, D], BF16)
    VS_d = c.dram("VS_d", [128, D], BF16)
    EU_d = [c.dram("EU_d%d" % l, [NEXP, D], BF16, dma=True) for l in range(n_layers)]
    EV_d = [c.dram("EV_d%d" % l, [NEXP, D], BF16, dma=True) for l in range(n_layers)]

    ident = c.sb([128, 128], F32, "ident")
    ones = c.sb([128, 128], F32, "ones")
    negtri = c.sb([128, 128], BF16, "negtri")
    negones = c.sb([128, 128], BF16, "negones")
    iota16 = c.sb([128, 16], F32, "iota16")
    iotap = c.sb([128, 1], F32, "iotap")
    c.op('pool', lambda: nc.gpsimd.memset(ones[:, :], 1.0), writes=[ones])
    c.op('pool', lambda: nc.gpsimd.memset(ident[:, :], 1.0), writes=[ident])
    c.op('pool', lambda: nc.gpsimd.affine_select(out=ident[:, :], in_=ident[:, :], pattern=[[-1, 128]],
                                                 compare_op=ALU.is_equal, fill=0.0, base=0, channel_multiplier=1),
         reads=[ident], writes=[ident])
    c.op('pool', lambda: nc.gpsimd.memset(negones[:, :], -1.0), writes=[negones])
    c.op('pool', lambda: nc.gpsimd.memset(negtri[:, :], -1.0), writes=[negtri])
    c.op('pool', lambda: nc.gpsimd.affine_select(out=negtri[:, :], in_=negtri[:, :], pattern=[[-1, 128]],
                                                 compare_op=ALU.is_ge, fill=0.0, base=0, channel_multiplier=1),
         reads=[negtri], writes=[negtri])
    c.op('pool', lambda: nc.gpsimd.iota(iota16[:, :], pattern=[[1, 16]], base=0, channel_multiplier=0,
                                        allow_small_or_imprecise_dtypes=True), writes=[iota16])
    c.op('pool', lambda: nc.gpsimd.iota(iotap[:, :], pattern=[[0, 1]], base=0, channel_multiplier=1,
                                        allow_small_or_imprecise_dtypes=True), writes=[iotap])

    def load_small(src, cols, name, dt=F32):
        t = c.sb([128, cols], dt, name, dma=True)
        c.dma('sp', lambda: nc.sync.dma_start(out=t[:, :], in_=src[:, :]), t, writes=[t])
        return t

    bada_sb = load_small(b_ada, DEPTH * 48, "bada_sb")
    nmix_sb = load_small(nmix, DEPTH * NCH, "nmix_sb")
    nffn_sb = load_small(nffn, DEPTH * NCH, "nffn_sb")
    nkv_sb = load_small(nkv, NCH, "nkv_sb")
    fn_sb = load_small(fnorm, NCH, "fn_sb")
    bpw1_sb = load_small(b_pw1, NA * 16, "bpw1_sb")
    wdw_sb = load_small(w_dw, NA * NCH * CW, "wdw_sb")
    bdw_sb = load_small(b_dw, NA * NCH, "bdw_sb")
    lng_sb = load_small(ln_g, NA * NCH, "lng_sb")
    lnb_sb = load_small(ln_b, NA * NCH, "lnb_sb")
    bpw2_sb = load_small(b_pw2, NA * NCH, "bpw2_sb")
    bsb_sb = load_small(bsb, 2 * NH, "bsb_sb")
    bias_s = c.sb([128, 128], F32, "bias_s")

    psum = [c.ps("ps%d" % i) for i in range(8)]
    prot = [0]

    def nps():
        prot[0] = (prot[0] + 1) % 4
        return psum[prot[0]]

    tm = c.sb([128, 2, D], F32, "tm", dma=True)
    xT = c.sb([128, NCH, TP], F32, "xT")
    scrA = c.sb([128, NCH, TP], F32, "scrA")
    hT = c.sb([128, NCH, TP], BF16, "hT")
    rstd = c.sb([128, TP], F32, "rstd")
    tmpN = c.sb([128, TP], F32, "tmpN")
    tmpM = c.sb([128, TP], F32, "tmpM")
    wsl = [c.sb([128, 8, 256], BF16, "wsl%d" % i, dma=True) for i in range(3)]
    wrot = [0]
    ada = c.sb([128, DEPTH * 48, 17], F32, "ada")
    fullbuf = c.sb([128, NCH, SPC * (HIST + DEC_T)], F32, "fullbuf")
    FOFF = [0, 304]

    class _V:
        def __init__(self, fn):
            self.fn = fn

        def __getitem__(self, idx):
            return self.fn(idx)
    yb = [c.sb([128, TP], F32, "yb%d" % j) for j in range(NCH)]
    qT = c.sb([128, 16, TP], BF16, "qT", dma=True)
    attT = c.sb([64, NH, TP], BF16, "attT")
    ktn = c.sb([64, NH, TP], BF16, "ktn", dma=True)
    vbf = c.sb([128, 2, D], BF16, "vbf", dma=True)
    KTb = c.sb([64, SEQ], BF16, "KTb", dma=True)
    Vb = c.sb([128, SEQ // 128, DH], BF16, "Vb", dma=True)
    u_t = c.sb([128, TP], F32, "u_t")
    lp_t = c.sb([128, TP], BF16, "lp_t")
    at_t = c.sb([128, TP], BF16, "at_t")
    C32 = c.sb([128, TP], F32, "C32")
    c16 = c.sb([128, TP], BF16, "c16")
    u_tB = c.sb([128, TP], F32, "u_tB")
    lp_tB = c.sb([128, TP], BF16, "lp_tB")
    oacc = c.sb([64, 128], F32, "oacc")

    def wslab(W, r0, col0, ncols=256, kmode=128):
        t = wsl[wrot[0]]
        wrot[0] = (wrot[0] + 1) % 3
        if kmode == 128:
            src = W[r0:r0 + D, col0:col0 + ncols].rearrange("(kc p) n -> p kc n", p=128)
            c.dma('pool', lambda: nc.gpsimd.dma_start(out=t[:, 0:8, 0:ncols], in_=src), t, writes=[t])
        else:
            src = W[r0:r0 + D, col0:col0 + 128].rearrange("(h p) n -> p h n", p=64)
            dstv = t[0:64, :, :].rearrange("p k (a n) -> p (k a) n", n=128)
            c.dma('pool', lambda: nc.gpsimd.dma_start(out=dstv, in_=src), t, writes=[t])
        return t

    def A(l, k, ch):
        return l * 48 + k * 8 + ch

    if DO_PEER:
        CR = 2048
        for l in range(n_layers):
            for (src, dst) in ((exp_u, EU_d[l]), (exp_v, EV_d[l])):
                for r0 in range(0, NEXP, CR):
                    c.dma('pool', lambda src=src, dst=dst, r0=r0, l=l: nc.gpsimd.dma_start(
                        out=dst[r0:r0 + CR, :], in_=src[l * NEXP + r0:l * NEXP + r0 + CR, :]), dst, writes=[dst])

    cin = tm
    scT = c.sb([128, NCH, 17], BF16, "scT")
    c.dma('sp', lambda: nc.sync.dma_start(out=tm[0:17, 0, :], in_=c_all[:, :]), tm, writes=[tm])
    for ch in range(NCH):
        pst = nps()
        c.op('pe', lambda pst=pst, ch=ch: nc.tensor.transpose(out=pst[:, 0:17], in_=tm[0:17, 0, ch * 128:(ch + 1) * 128],
                                                               identity=ident[0:17, 0:17]),
             reads=[cin, ident], writes=[pst])
        c.op('act', lambda pst=pst, ch=ch: nc.scalar.activation(out=scT[:, ch, :], in_=pst[:, 0:17], func=AF.Silu),
             reads=[pst], writes=[scT])
    for l in range(n_layers):
        for sl in range(24):
            t = wslab(w_ada, l * D, sl * 256)
            for jj in range(2):
                j = sl * 2 + jj
                pst = nps()
                for kc in range(NCH):
                    c.op('pe', lambda pst=pst, t=t, kc=kc, jj=jj: nc.tensor.matmul(
                        pst[:, 0:17], lhsT=t[:, kc, jj * 128:(jj + 1) * 128], rhs=scT[:, kc, :],
                        start=(kc == 0), stop=(kc == NCH - 1)), reads=[t, scT], writes=[pst])
                c.op('dve', lambda pst=pst, l=l, j=j: nc.vector.tensor_scalar(
                    out=ada[:, l * 48 + j, :], in0=pst[:, 0:17], scalar1=bada_sb[:, l * 48 + j:l * 48 + j + 1], scalar2=1.0,
                    op0=ALU.add, op1=ALU.mult), reads=[pst, bada_sb], writes=[ada])
        for ch in range(NCH):
            for (k, nsb) in ((1, nmix_sb), (4, nffn_sb)):
                c.op('dve', lambda l=l, k=k, ch=ch, nsb=nsb: nc.vector.tensor_scalar(
                    out=ada[:, A(l, k, ch), :], in0=ada[:, A(l, k, ch), :], scalar1=1.0,
                    scalar2=nsb[:, l * NCH + ch:l * NCH + ch + 1], op0=ALU.add, op1=ALU.mult),
                    reads=[ada, nsb], writes=[ada])

    def v3(ap):
        return ap.rearrange("p (s t) -> p s t", t=DEC_T)

    def abc(idx):
        return ada[:, idx, 1:17].unsqueeze(2).to_broadcast([128, SPC, DEC_T])

    def modulate(out_ap, in_ap, aidx, bidx, N, samp, out_tk, in_tk):
        if not samp:
            c.op('dve', lambda: nc.vector.tensor_scalar(out=out_ap, in0=in_ap, scalar1=ada[:, aidx, 0:1],
                                                        scalar2=ada[:, bidx, 0:1], op0=ALU.mult, op1=ALU.add),
                 reads=[ada, in_tk], writes=[out_tk])
        else:
            c.op('dve', lambda: nc.vector.tensor_tensor(out=v3(tmpN[:, 0:N]), in0=v3(in_ap), in1=abc(aidx), op=ALU.mult),
                 reads=[ada, in_tk], writes=[tmpN])
            c.op('dve', lambda: nc.vector.tensor_tensor(out=v3(out_ap), in0=v3(tmpN[:, 0:N]), in1=abc(bidx), op=ALU.add),
                 reads=[ada, tmpN], writes=[out_tk])

    def load_x_tile(src_ap, N):
        nt = N // 128
        c.dma('sp', lambda: nc.sync.dma_start(out=tm[:, 0:nt, :], in_=src_ap.rearrange("(n p) d -> p n d", p=128)),
              tm, writes=[tm])
        for tb in range(nt):
            for ch in range(NCH):
                pst = nps()
                c.op('pe', lambda pst=pst, tb=tb, ch=ch: nc.tensor.transpose(
                    out=pst[:, 0:128], in_=tm[:, tb, ch * 128:(ch + 1) * 128], identity=ident[:, :]),
                    reads=[tm, ident], writes=[pst])
                c.op('act', lambda pst=pst, tb=tb, ch=ch: nc.scalar.copy(
                    out=xT[:, ch, tb * 128:(tb + 1) * 128], in_=pst[:, 0:128]),
                    reads=[pst], writes=[xT])

    def finish_rstd(pst, N, dst):
        c.op('dve', lambda: nc.vector.tensor_scalar(out=dst[:, 0:N], in0=pst[:, 0:N], scalar1=1.0 / D, scalar2=EPS,
                                                    op0=ALU.mult, op1=ALU.add), reads=[pst], writes=[dst])
        c.op('act', lambda: nc.scalar.activation(out=dst[:, 0:N], in_=dst[:, 0:N], func=AF.Sqrt),
             reads=[dst], writes=[dst])
        c.op('dve', lambda: nc.vector.reciprocal(out=dst[:, 0:N], in_=dst[:, 0:N]), reads=[dst], writes=[dst])

    def rms_stats(N):
        for ch in range(NCH):
            c.op('act', lambda ch=ch: nc.scalar.activation(out=scrA[:, ch, 0:N], in_=xT[:, ch, 0:N], func=AF.Square),
                 reads=[xT], writes=[scrA])
        pst = psum[4]
        for ch in range(NCH):
            c.op('pe', lambda ch=ch: nc.tensor.matmul(pst[:, 0:N], lhsT=ones[:, :], rhs=scrA[:, ch, 0:N],
                                                        start=(ch == 0), stop=(ch == NCH - 1)),
                 reads=[ones, scrA], writes=[pst])
        finish_rstd(pst, N, rstd)

    def modnorm(l, which, N, samp, keep_f32=False):
        rms_stats(N)
        ka, kb_ = (1, 0) if which == 1 else (4, 3)
        for ch in range(NCH):
            c.op('dve', lambda ch=ch: nc.vector.tensor_tensor(out=scrA[:, ch, 0:N], in0=xT[:, ch, 0:N], in1=rstd[:, 0:N],
                                                              op=ALU.mult), reads=[xT, rstd], writes=[scrA])
            if keep_f32:
                modulate(scrA[:, ch, 0:N], scrA[:, ch, 0:N], A(l, ka, ch), A(l, kb_, ch), N, samp, scrA, scrA)
                c.op('act', lambda ch=ch: nc.scalar.copy(out=hT[:, ch, 0:N], in_=scrA[:, ch, 0:N]),
                     reads=[scrA], writes=[hT])
            else:
                modulate(hT[:, ch, 0:N], scrA[:, ch, 0:N], A(l, ka, ch), A(l, kb_, ch), N, samp, hT, scrA)

    def resid_add(pst, ps_ap, bias_ap, gidx, ch, N, samp):
        if not samp:
            if bias_ap is not None:
                c.op('dve', lambda: nc.vector.tensor_scalar(out=tmpN[:, 0:N], in0=ps_ap, scalar1=bias_ap,
                                                            scalar2=ada[:, gidx, 0:1], op0=ALU.add, op1=ALU.mult),
                     reads=[ada, pst, bpw2_sb], writes=[tmpN])
            else:
                c.op('dve', lambda: nc.vector.tensor_scalar(out=tmpN[:, 0:N], in0=ps_ap, scalar1=ada[:, gidx, 0:1],
                                                            scalar2=1.0, op0=ALU.mult, op1=ALU.mult),
                     reads=[ada, pst], writes=[tmpN])
        else:
            if bias_ap is not None:
                c.op('dve', lambda: nc.vector.tensor_scalar(out=tmpM[:, 0:N], in0=ps_ap, scalar1=bias_ap, scalar2=1.0,
                                                            op0=ALU.add, op1=ALU.mult), reads=[pst, bpw2_sb], writes=[tmpM])
                c.op('dve', lambda: nc.vector.tensor_tensor(out=v3(tmpN[:, 0:N]), in0=v3(tmpM[:, 0:N]), in1=abc(gidx),
                                                            op=ALU.mult), reads=[ada, tmpM], writes=[tmpN])
            else:
                c.op('dve', lambda: nc.vector.tensor_tensor(out=v3(tmpN[:, 0:N]), in0=v3(ps_ap), in1=abc(gidx),
                                                            op=ALU.mult), reads=[ada, pst], writes=[tmpN])
        c.op('dve', lambda: nc.vector.tensor_tensor(out=xT[:, ch, 0:N], in0=xT[:, ch, 0:N], in1=tmpN[:, 0:N], op=ALU.add),
             reads=[xT, tmpN], writes=[xT])

    def conv_layer(l, N, samp, ti, last):
        modnorm(l, 1, N, samp)
        fo = FOFF[l]
        fl = fullbuf
        full_s = fullbuf
        fs4 = fullbuf[:, :, :].rearrange("p c (s w) -> p c s w", w=HIST + DEC_T)
        if not samp and ti == 0:
            c.op('pool', lambda: nc.gpsimd.memset(fl[:, :, fo:fo + HIST], 0.0), writes=[fl])
        if samp:
            for half in range(2):
                r0 = l * SPC * HIST + half * 240
                c.dma('sp', lambda r0=r0: nc.sync.dma_start(
                    out=tm[0:120, 0:2, :], in_=st_conv[r0:r0 + 240, :].rearrange("(n p) d -> p n d", p=120)),
                    tm, writes=[tm])
                for n in range(2):
                    for ch in range(NCH):
                        pst = nps()
                        c.op('pe', lambda pst=pst, n=n, ch=ch: nc.tensor.transpose(
                            out=pst[:, 0:120], in_=tm[0:120, n, ch * 128:(ch + 1) * 128], identity=ident[0:120, 0:120]),
                            reads=[tm, ident], writes=[pst])
                        s0 = (half * 2 + n) * 4
                        c.op('act', lambda pst=pst, ch=ch, s0=s0: nc.scalar.copy(
                            out=fs4[:, ch, s0:s0 + 4, 0:HIST],
                            in_=pst[:, 0:120].rearrange("p (s r) -> p s r", r=HIST)),
                            reads=[pst], writes=[full_s])
        for jp in range(4):
            ta = wslab(w_pw1, l * D, jp * 256)
            tb_ = wslab(w_pw1, l * D, D + jp * 256)
            for jj in range(2):
                j = jp * 2 + jj
                ps1 = nps()
                ps2 = nps()
                for (pst, t) in ((ps1, ta), (ps2, tb_)):
                    for kc in range(NCH):
                        c.op('pe', lambda pst=pst, t=t, kc=kc, jj=jj: nc.tensor.matmul(
                            pst[:, 0:N], lhsT=t[:, kc, jj * 128:(jj + 1) * 128], rhs=hT[:, kc, 0:N],
                            start=(kc == 0), stop=(kc == NCH - 1)), reads=[t, hT], writes=[pst])
                c.op('act', lambda ps2=ps2, j=j: nc.scalar.activation(
                    out=tmpM[:, 0:N], in_=ps2[:, 0:N], func=AF.Sigmoid,
                    bias=bpw1_sb[:, l * 16 + 8 + j:l * 16 + 8 + j + 1], scale=1.0),
                    reads=[ps2, bpw1_sb], writes=[tmpM])
                if not samp:
                    c.op('dve', lambda ps1=ps1, j=j: nc.vector.scalar_tensor_tensor(
                        out=fl[:, j, fo + HIST:fo + HIST + N], in0=ps1[:, 0:N], scalar=bpw1_sb[:, l * 16 + j:l * 16 + j + 1],
                        in1=tmpM[:, 0:N], op0=ALU.add, op1=ALU.mult), reads=[ps1, bpw1_sb, tmpM], writes=[fl])
                else:
                    c.op('dve', lambda ps1=ps1, j=j: nc.vector.scalar_tensor_tensor(
                        out=scrA[:, j, 0:N], in0=ps1[:, 0:N], scalar=bpw1_sb[:, l * 16 + j:l * 16 + j + 1],
                        in1=tmpM[:, 0:N], op0=ALU.add, op1=ALU.mult), reads=[ps1, bpw1_sb, tmpM], writes=[scrA])
                    c.op('act', lambda j=j: nc.scalar.copy(out=fs4[:, j, :, HIST:HIST + DEC_T],
                                                           in_=v3(scrA[:, j, 0:N])), reads=[scrA], writes=[full_s])
        if samp:
            for ch in range(NCH):
                pst = nps()
                c.op('pe', lambda pst=pst, ch=ch: nc.tensor.transpose(out=pst[:, 0:128], in_=scrA[:, ch, 0:128],
                                                                       identity=ident[:, :]),
                     reads=[scrA, ident], writes=[pst])
                c.op('act', lambda pst=pst, ch=ch: nc.scalar.copy(out=tm[:, 0, ch * 128:(ch + 1) * 128], in_=pst[:, 0:128]),
                     reads=[pst], writes=[tm])
            r0 = l * SPC * HIST
            dst = conv_s[r0:r0 + SPC * HIST, :].rearrange("(s r) d -> s r d", r=HIST)
            src = st_conv[r0:r0 + SPC * HIST, :].rearrange("(s r) d -> s r d", r=HIST)
            for s in range(SPC):
                c.dma('sp', lambda s=s: nc.sync.dma_start(out=dst[s, HIST - DEC_T:HIST, :], in_=tm[s * DEC_T:(s + 1) * DEC_T, 0, :]),
                      tm, reads=[tm], out_dram=True)
            c.dma('sp', lambda: nc.sync.dma_start(out=dst[:, 0:HIST - DEC_T, :], in_=src[:, DEC_T:HIST, :]),
                  tm, reads=[], out_dram=True)
        elif last:
            for ch in range(NCH):
                pst = nps()
                c.op('pe', lambda pst=pst, ch=ch: nc.tensor.transpose(out=pst[0:HIST, 0:128], in_=fl[:, ch, fo + N:fo + N + HIST],
                                                                       identity=ident[:, :]),
                     reads=[fl, ident], writes=[pst])
                c.op('act', lambda pst=pst, ch=ch: nc.scalar.copy(out=tm[0:HIST, 0, ch * 128:(ch + 1) * 128],
                                                                  in_=pst[0:HIST, 0:128]), reads=[pst], writes=[tm])
            c.dma('sp', lambda: nc.sync.dma_start(out=conv_p[l * HIST:(l + 1) * HIST, :], in_=tm[0:HIST, 0, :]),
                  tm, reads=[tm], out_dram=True)
        src_t = full_s if samp else fl
        for w in range(CW):
            for j in range(NCH):
                widx = (l * NCH + j) * CW + w
                if samp:
                    in0 = fs4[:, j, :, w:w + DEC_T]
                    yv = v3(yb[j][:, 0:N])
                else:
                    in0 = fl[:, j, fo + w:fo + w + N]
                    yv = yb[j][:, 0:N]
                if w == 0:
                    c.op('dve', lambda in0=in0, yv=yv, widx=widx, j=j: nc.vector.tensor_scalar(
                        out=yv, in0=in0, scalar1=wdw_sb[:, widx:widx + 1], scalar2=bdw_sb[:, l * NCH + j:l * NCH + j + 1],
                        op0=ALU.mult, op1=ALU.add), reads=[src_t, wdw_sb, bdw_sb], writes=[yb[j]])
                else:
                    c.op('dve', lambda in0=in0, yv=yv, widx=widx: nc.vector.scalar_tensor_tensor(
                        out=yv, in0=in0, scalar=wdw_sb[:, widx:widx + 1], in1=yv, op0=ALU.mult, op1=ALU.add),
                        reads=[src_t, wdw_sb, yb[j]], writes=[yb[j]])
        if not samp and not last:
            c.op('pool', lambda: nc.gpsimd.tensor_copy(out=fl[:, :, fo:fo + HIST], in_=fl[:, :, fo + N:fo + N + HIST]),
                 reads=[fl], writes=[fl])
        for j in range(NCH):
            c.op('act', lambda j=j: nc.scalar.activation(out=scrA[:, j, 0:N], in_=yb[j][:, 0:N], func=AF.Square),
                 reads=[yb[j]], writes=[scrA])
        pm, pq = psum[4], psum[5]
        for j in range(NCH):
            c.op('pe', lambda j=j: nc.tensor.matmul(pm[:, 0:N], lhsT=ones[:, :], rhs=yb[j][:, 0:N],
                                                      start=(j == 0), stop=(j == NCH - 1)), reads=[ones, yb[j]], writes=[pm])
        for j in range(NCH):
            c.op('pe', lambda j=j: nc.tensor.matmul(pq[:, 0:N], lhsT=ones[:, :], rhs=scrA[:, j, 0:N],
                                                      start=(j == 0), stop=(j == NCH - 1)), reads=[ones, scrA], writes=[pq])
        c.op('dve', lambda: nc.vector.tensor_scalar(out=tmpM[:, 0:N], in0=pm[:, 0:N], scalar1=1.0 / D, scalar2=1.0,
                                                    op0=ALU.mult, op1=ALU.mult), reads=[pm], writes=[tmpM])
        c.op('dve', lambda: nc.vector.tensor_tensor(out=tmpN[:, 0:N], in0=tmpM[:, 0:N], in1=tmpM[:, 0:N], op=ALU.mult),
             reads=[tmpM], writes=[tmpN])
        c.op('dve', lambda: nc.vector.scalar_tensor_tensor(out=rstd[:, 0:N], in0=pq[:, 0:N], scalar=1.0 / D, in1=tmpN[:, 0:N],
                                                           op0=ALU.mult, op1=ALU.subtract), reads=[pq, tmpN], writes=[rstd])
        c.op('dve', lambda: nc.vector.tensor_scalar(out=rstd[:, 0:N], in0=rstd[:, 0:N], scalar1=EPS, scalar2=1.0,
                                                    op0=ALU.add, op1=ALU.mult), reads=[rstd], writes=[rstd])
        c.op('act', lambda: nc.scalar.activation(out=rstd[:, 0:N], in_=rstd[:, 0:N], func=AF.Sqrt), reads=[rstd], writes=[rstd])
        c.op('dve', lambda: nc.vector.reciprocal(out=rstd[:, 0:N], in_=rstd[:, 0:N]), reads=[rstd], writes=[rstd])
        for j in range(NCH):
            c.op('dve', lambda j=j: nc.vector.tensor_tensor(out=scrA[:, j, 0:N], in0=yb[j][:, 0:N], in1=tmpM[:, 0:N],
                                                            op=ALU.subtract), reads=[yb[j], tmpM], writes=[scrA])
            c.op('dve', lambda j=j: nc.vector.tensor_tensor(out=scrA[:, j, 0:N], in0=scrA[:, j, 0:N], in1=rstd[:, 0:N],
                                                            op=ALU.mult), reads=[scrA, rstd], writes=[scrA])
            c.op('act', lambda j=j: nc.scalar.activation(out=hT[:, j, 0:N], in_=scrA[:, j, 0:N], func=AF.Silu,
                                                         bias=lnb_sb[:, l * NCH + j:l * NCH + j + 1],
                                                         scale=lng_sb[:, l * NCH + j:l * NCH + j + 1]),
                 reads=[scrA, lnb_sb, lng_sb], writes=[hT])
        for sl in range(4):
            t = wslab(w_pw2, l * D, sl * 256)
            for jj in range(2):
                ch = sl * 2 + jj
                pst = nps()
                for kc in range(NCH):
                    c.op('pe', lambda pst=pst, t=t, kc=kc, jj=jj: nc.tensor.matmul(
                        pst[:, 0:N], lhsT=t[:, kc, jj * 128:(jj + 1) * 128], rhs=hT[:, kc, 0:N],
                        start=(kc == 0), stop=(kc == NCH - 1)), reads=[t, hT], writes=[pst])
                resid_add(pst, pst[:, 0:N], bpw2_sb[:, l * NCH + ch:l * NCH + ch + 1], A(l, 2, ch), ch, N, samp)


    skb = c.sb([128, 16, 128], BF16, "skb", dma=True)
    sc = c.sb([128, 2048], F32, "sc")
    sv = c.sb([128, 16, 16], F32, "sv")
    si = c.sb([128, 16, 16], U32, "si")
    sif = c.sb([128, 16, 16], F32, "sif")
    cand = c.sb([128, PH, 256], F32, "cand")
    cv = c.sb([128, PH, 16], F32, "cv")
    cp = c.sb([128, PH, 16], U32, "cp")
    iiu = c.sb([128, PH, 16], U32, "iiu")
    jju = c.sb([128, PH, 16], U32, "jju")
    iif = c.sb([128, PH, 16], F32, "iif")
    jjf = c.sb([128, PH, 16], F32, "jjf")
    selI = c.sb([128, PH, 16], F32, "selI")
    selJ = c.sb([128, PH, 16], F32, "selJ")
    ef = c.sb([128, 128], F32, "ef")
    eidx = c.sb([128, 128], I32, "eidx")
    negm = c.sb([128, PH], F32, "negm")
    ee = c.sb([128, PH, 16], F32, "ee")
    zz = c.sb([128, PH], F32, "zz")
    gg = c.sb([128, PH, 16], F32, "gg")
    NUG = 6
    ug = [c.sb([128, D], BF16, "ug%d" % i, dma=True) for i in range(NUG)]
    h2b = c.sb([128, D], BF16, "h2b")

    junk = [c.sb([128, D], BF16, "junk0")] * 2
    araw = c.sb([128, 128], F32, "araw")
    gt1 = c.sb([128, 128], F32, "gt1")
    wgt = c.sb([128, 128], F32, "wgt")
    acc = c.sb([128, D], F32, "acc")

    def bc4(ap3, axis):
        return ap3.unsqueeze(axis).to_broadcast([128, PH, 16, 16])

    def peer_layer(l, N, samp):
        nt = N // 128
        modnorm(l, 2, N, samp, keep_f32=True)
        for sl in range(8):
            t = wslab(w_pq, l * D, sl * 256)
            for jj in range(2):
                hp = sl * 2 + jj
                pst = nps()
                for kc in range(NCH):
                    c.op('pe', lambda pst=pst, t=t, kc=kc, jj=jj: nc.tensor.matmul(
                        pst[:, 0:N], lhsT=t[:, kc, jj * 128:(jj + 1) * 128], rhs=hT[:, kc, 0:N],
                        start=(kc == 0), stop=(kc == NCH - 1)), reads=[t, hT], writes=[pst])
                c.op('act', lambda pst=pst, hp=hp: nc.scalar.copy(out=qT[:, hp, 0:N], in_=pst[:, 0:N]),
                     reads=[pst], writes=[qT])
        r0 = l * 16 * 128
        c.dma('pool', lambda: nc.gpsimd.dma_start(out=skb[:, :, :],
                                                  in_=skT[r0:r0 + 2048, :].rearrange("(hp d) k -> d hp k", d=128)),
              skb, writes=[skb])
        for tb in range(nt):
            cols = slice(tb * 128, (tb + 1) * 128)
            for ch in range(NCH):
                pst = psum[5 + ch % 2]
                c.op('pe', lambda pst=pst, ch=ch, cols=cols: nc.tensor.transpose(out=pst[:, 0:128], in_=scrA[:, ch, cols],
                                                                       identity=ident[:, :]),
                     reads=[scrA, ident], writes=[pst])
                c.op('act', lambda pst=pst, ch=ch: nc.scalar.copy(out=h2b[:, ch * 128:(ch + 1) * 128], in_=pst[:, 0:128]),
                     reads=[pst], writes=[h2b])
            for bq in range(4):
                pst = psum[bq]
                for i in range(4):
                    hp = bq * 4 + i
                    c.op('pe', lambda pst=pst, i=i, hp=hp, cols=cols: nc.tensor.matmul(
                        pst[:, i * 128:(i + 1) * 128], lhsT=qT[:, hp, cols], rhs=skb[:, hp, :], start=True, stop=True),
                        reads=[qT, skb], writes=[pst])
                c.op('act', lambda pst=pst, bq=bq: nc.scalar.copy(out=sc[:, bq * 512:(bq + 1) * 512], in_=pst[:, :]),
                     reads=[pst], writes=[sc])
            for hp in range(16):
                scv = sc[:, hp * 128:(hp + 1) * 128]
                c.op('dve', lambda hp=hp, scv=scv: nc.vector.max(out=sv[:, hp, 0:8], in_=scv), reads=[sc], writes=[sv])
                c.op('dve', lambda hp=hp, scv=scv: nc.vector.max_index(out=si[:, hp, 0:8], in_max=sv[:, hp, 0:8], in_values=scv),
                     reads=[sc, sv], writes=[si])
                c.op('dve', lambda hp=hp, scv=scv: nc.vector.match_replace(out=scv, in_to_replace=sv[:, hp, 0:8],
                                                                           in_values=scv, imm_value=-1e30),
                     reads=[sc, sv], writes=[sc])
                c.op('dve', lambda hp=hp, scv=scv: nc.vector.max(out=sv[:, hp, 8:16], in_=scv), reads=[sc], writes=[sv])
                c.op('dve', lambda hp=hp, scv=scv: nc.vector.max_index(out=si[:, hp, 8:16], in_max=sv[:, hp, 8:16], in_values=scv),
                     reads=[sc, sv], writes=[si])
            c.op('dve', lambda: nc.vector.tensor_copy(out=sif[:, :, :], in_=si[:, :, :]), reads=[si], writes=[sif])
            sv4 = sv[:, :, :].rearrange("p (h t) k -> p h t k", t=2)
            sif4 = sif[:, :, :].rearrange("p (h t) k -> p h t k", t=2)
            cand4 = cand[:, :, :].rearrange("p h (i j) -> p h i j", j=16)
            c.op('dve', lambda: nc.vector.tensor_tensor(out=cand4, in0=bc4(sv4[:, :, 0, :], 3), in1=bc4(sv4[:, :, 1, :], 2),
                                                        op=ALU.add), reads=[sv], writes=[cand])
            for h in range(PH):
                cdv = cand[:, h, :]
                c.op('dve', lambda h=h, cdv=cdv: nc.vector.max(out=cv[:, h, 0:8], in_=cdv), reads=[cand], writes=[cv])
                c.op('dve', lambda h=h, cdv=cdv: nc.vector.max_index(out=cp[:, h, 0:8], in_max=cv[:, h, 0:8], in_values=cdv),
                     reads=[cand, cv], writes=[cp])
                c.op('dve', lambda h=h, cdv=cdv: nc.vector.match_replace(out=cdv, in_to_replace=cv[:, h, 0:8],
                                                                         in_values=cdv, imm_value=-1e30),
                     reads=[cand, cv], writes=[cand])
                c.op('dve', lambda h=h, cdv=cdv: nc.vector.max(out=cv[:, h, 8:16], in_=cdv), reads=[cand], writes=[cv])
                c.op('dve', lambda h=h, cdv=cdv: nc.vector.max_index(out=cp[:, h, 8:16], in_max=cv[:, h, 8:16], in_values=cdv),
                     reads=[cand, cv], writes=[cp])
            c.op('dve', lambda: nc.vector.tensor_single_scalar(out=iiu[:, :, :], in_=cp[:, :, :], scalar=4,
                                                               op=ALU.logical_shift_right), reads=[cp], writes=[iiu])
            c.op('dve', lambda: nc.vector.tensor_single_scalar(out=jju[:, :, :], in_=cp[:, :, :], scalar=15,
                                                               op=ALU.bitwise_and), reads=[cp], writes=[jju])
            c.op('dve', lambda: nc.vector.tensor_copy(out=iif[:, :, :], in_=iiu[:, :, :]), reads=[iiu], writes=[iif])
            c.op('dve', lambda: nc.vector.tensor_copy(out=jjf[:, :, :], in_=jju[:, :, :]), reads=[jju], writes=[jjf])
            eq4 = sc[:, :].rearrange("p (h k i) -> p h k i", k=16, i=16)
            io4 = iota16[:, :].unsqueeze(1).unsqueeze(1).to_broadcast([128, PH, 16, 16])
            for (xf, tsel, sel) in ((iif, 0, selI), (jjf, 1, selJ)):
                c.op('dve', lambda xf=xf: nc.vector.tensor_tensor(out=eq4, in0=bc4(xf[:, :, :], 3), in1=io4, op=ALU.is_equal),
                     reads=[xf, iota16], writes=[sc])
                c.op('dve', lambda tsel=tsel: nc.vector.tensor_tensor(out=eq4, in0=eq4, in1=bc4(sif4[:, :, tsel, :], 2),
                                                                      op=ALU.mult), reads=[sc, sif], writes=[sc])
                c.op('dve', lambda sel=sel: nc.vector.tensor_reduce(out=sel[:, :, :], in_=eq4, axis=AX.X, op=ALU.add),
                     reads=[sc], writes=[sel])
            efv = ef[:, :].rearrange("p (h k) -> p h k", k=16)
            c.op('dve', lambda: nc.vector.scalar_tensor_tensor(out=ef[:, :], in0=selI[:, :, :].rearrange("p h k -> p (h k)"),
                                                               scalar=128.0, in1=selJ[:, :, :].rearrange("p h k -> p (h k)"),
                                                               op0=ALU.mult, op1=ALU.add), reads=[selI, selJ], writes=[ef])
            c.op('dve', lambda: nc.vector.tensor_copy(out=eidx[:, :], in_=ef[:, :]), reads=[ef], writes=[eidx])
            c.op('dve', lambda: nc.vector.tensor_scalar(out=negm[:, :], in0=cv[:, :, 0], scalar1=-1.0, scalar2=1.0,
                                                        op0=ALU.mult, op1=ALU.mult), reads=[cv], writes=[negm])
            for h in range(PH):
                c.op('act', lambda h=h: nc.scalar.activation(out=ee[:, h, :], in_=cv[:, h, :], func=AF.Exp,
                                                             bias=negm[:, h:h + 1], scale=1.0),
                     reads=[cv, negm], writes=[ee])
            c.op('dve', lambda: nc.vector.tensor_reduce(out=zz[:, :], in_=ee[:, :, :], axis=AX.X, op=ALU.add),
                 reads=[ee], writes=[zz])
            c.op('dve', lambda: nc.vector.reciprocal(out=zz[:, :], in_=zz[:, :]), reads=[zz], writes=[zz])
            c.op('dve', lambda: nc.vector.tensor_tensor(out=gg[:, :, :], in0=ee[:, :, :],
                                                        in1=zz[:, :].unsqueeze(2).to_broadcast([128, PH, 16]), op=ALU.mult),
                 reads=[ee, zz], writes=[gg])
            for hk in range(128):
                b_ = ug[hk % NUG]
                c.dma('pool', lambda b_=b_, hk=hk: nc.gpsimd.indirect_dma_start(
                    out=b_[:, :], out_offset=None, in_=EU_d[l][:, :],
                    in_offset=bass.IndirectOffsetOnAxis(ap=eidx[:, hk:hk + 1], axis=0)), b_, reads=[eidx, EU_d[l]], writes=[b_])
                jk = junk[hk % 2]
                c.op('dve', lambda b_=b_, hk=hk, jk=jk: nc.vector.scalar_tensor_tensor(
                    out=jk[:, :], in0=b_[:, :], scalar=1.0, in1=h2b[:, :], op0=ALU.mult, op1=ALU.mult,
                    accum_out=araw[:, hk:hk + 1]), reads=[b_, h2b], writes=[jk, araw])
            c.op('dve', lambda: nc.vector.tensor_tensor(out=gt1[:, :], in0=araw[:, :], in1=araw[:, :], op=ALU.mult),
                 reads=[araw], writes=[gt1])
            c.op('dve', lambda: nc.vector.tensor_scalar(out=gt1[:, :], in0=gt1[:, :], scalar1=0.044715, scalar2=1.0,
                                                        op0=ALU.mult, op1=ALU.add), reads=[gt1], writes=[gt1])
            c.op('dve', lambda: nc.vector.tensor_tensor(out=gt1[:, :], in0=gt1[:, :], in1=araw[:, :], op=ALU.mult),
                 reads=[gt1, araw], writes=[gt1])
            c.op('act', lambda: nc.scalar.activation(out=gt1[:, :], in_=gt1[:, :], func=AF.Sigmoid, scale=1.5957691216),
                 reads=[gt1], writes=[gt1])
            c.op('dve', lambda: nc.vector.tensor_tensor(out=wgt[:, :], in0=gt1[:, :], in1=araw[:, :], op=ALU.mult),
                 reads=[gt1, araw], writes=[wgt])
            c.op('dve', lambda: nc.vector.tensor_tensor(out=wgt[:, :], in0=wgt[:, :],
                                                        in1=gg[:, :, :].rearrange("p h k -> p (h k)"), op=ALU.mult),
                 reads=[wgt, gg], writes=[wgt])
            for hk in range(128):
                b_ = ug[(hk + 3) % NUG]
                c.dma('pool', lambda b_=b_, hk=hk: nc.gpsimd.indirect_dma_start(
                    out=b_[:, :], out_offset=None, in_=EV_d[l][:, :],
                    in_offset=bass.IndirectOffsetOnAxis(ap=eidx[:, hk:hk + 1], axis=0)), b_, reads=[eidx, EV_d[l]], writes=[b_])
                if hk == 0:
                    c.op('dve', lambda b_=b_, hk=hk: nc.vector.tensor_scalar(out=acc[:, :], in0=b_[:, :], scalar1=wgt[:, hk:hk + 1],
                                                                             scalar2=1.0, op0=ALU.mult, op1=ALU.mult),
                         reads=[b_, wgt], writes=[acc])
                else:
                    c.op('dve', lambda b_=b_, hk=hk: nc.vector.scalar_tensor_tensor(
                        out=acc[:, :], in0=b_[:, :], scalar=wgt[:, hk:hk + 1], in1=acc[:, :], op0=ALU.mult, op1=ALU.add),
                        reads=[b_, wgt, acc], writes=[acc])
            for ch in range(NCH):
                pst = psum[5 + ch % 2]
                c.op('pe', lambda pst=pst, ch=ch: nc.tensor.transpose(out=pst[:, 0:128], in_=acc[:, ch * 128:(ch + 1) * 128],
                                                                       identity=ident[:, :]),
                     reads=[acc, ident], writes=[pst])
                gidx = A(l, 5, ch)
                if not samp:
                    c.op('dve', lambda pst=pst, ch=ch, gidx=gidx, cols=cols: nc.vector.scalar_tensor_tensor(
                        out=xT[:, ch, cols], in0=pst[:, 0:128], scalar=ada[:, gidx, 0:1], in1=xT[:, ch, cols],
                        op0=ALU.mult, op1=ALU.add), reads=[pst, ada, xT], writes=[xT])
                else:
                    c.op('dve', lambda pst=pst, gidx=gidx: nc.vector.tensor_tensor(
                        out=v3(tmpN[:, 0:128]), in0=v3(pst[:, 0:128]), in1=abc(gidx), op=ALU.mult),
                        reads=[pst, ada], writes=[tmpN])
                    c.op('dve', lambda ch=ch: nc.vector.tensor_tensor(out=xT[:, ch, 0:128], in0=xT[:, ch, 0:128],
                                                                      in1=tmpN[:, 0:128], op=ALU.add),
                         reads=[xT, tmpN], writes=[xT])


    ptab_sb = c.sb([128, SPC * NPAGES], I32, "ptab_sb", dma=True)
    ptf = c.sb([128, SPC * NPAGES], F32, "ptf")
    idxpg = c.sb([128, SPC * NPAGES], I32, "idxpg")
    if do_sample:
        c.dma('sp', lambda: nc.sync.dma_start(out=ptab_sb[:, :], in_=ptab[:, :]), ptab_sb, writes=[ptab_sb])
        c.op('dve', lambda: nc.vector.tensor_copy(out=ptf[:, :], in_=ptab_sb[:, :]), reads=[ptab_sb], writes=[ptf])
        c.op('dve', lambda: nc.vector.tensor_scalar(out=ptf[:, :], in0=ptf[:, :], scalar1=float(PAGE), scalar2=iotap[:, 0:1],
                                                    op0=ALU.mult, op1=ALU.add), reads=[ptf, iotap], writes=[ptf])
        c.op('dve', lambda: nc.vector.tensor_copy(out=idxpg[:, :], in_=ptf[:, :]), reads=[ptf], writes=[idxpg])

    def kv_project(N, samp, t0):
        nt = N // 128
        rms_stats(N)
        for ch in range(NCH):
            c.op('dve', lambda ch=ch: nc.vector.scalar_tensor_tensor(
                out=hT[:, ch, 0:N], in0=xT[:, ch, 0:N], scalar=nkv_sb[:, ch:ch + 1], in1=rstd[:, 0:N],
                op0=ALU.mult, op1=ALU.mult), reads=[xT, nkv_sb, rstd], writes=[hT])
        for (W, outd, is_v) in ((w_k, (k_s if samp else k_p), False), (w_v, (v_s if samp else v_p), True)):
            if DBG_KV < 2 or (is_v and DBG_KV < 3):
                continue
            for sl in range(4):
                t = wslab(W, 0, sl * 256)
                for tb in range(nt):
                    pst = nps()
                    for kc in range(NCH):
                        c.op('pe', lambda pst=pst, t=t, kc=kc, tb=tb: nc.tensor.matmul(
                            pst[:, 0:256], lhsT=hT[:, kc, tb * 128:(tb + 1) * 128], rhs=t[:, kc, 0:256],
                            start=(kc == 0), stop=(kc == NCH - 1)), reads=[t, hT], writes=[pst])
                    c.op('act', lambda pst=pst, tb=tb, sl=sl: nc.scalar.copy(out=tm[:, tb, sl * 256:(sl + 1) * 256], in_=pst[:, 0:256]),
                         reads=[pst], writes=[tm])
                    if is_v:
                        c.op('dve', lambda tb=tb, sl=sl: nc.vector.tensor_copy(
                            out=vbf[:, tb, sl * 256:(sl + 1) * 256], in_=tm[:, tb, sl * 256:(sl + 1) * 256]), reads=[tm], writes=[vbf])
            c.dma('sp', lambda outd=outd: nc.sync.dma_start(out=outd[t0:t0 + N, :].rearrange("(n p) d -> p n d", p=128),
                                                            in_=tm[:, 0:nt, :]), tm, reads=[tm], out_dram=True)
            if is_v:
                if samp:
                    c.dma('sp', lambda: nc.sync.dma_start(out=VS_d[0:128, :], in_=vbf[:, 0, :]), vbf, reads=[vbf], writes=[VS_d])
                else:
                    c.dma('sp', lambda: nc.sync.dma_start(out=V_d[t0:t0 + N, :].rearrange("(n p) d -> p n d", p=128),
                                                          in_=vbf[:, 0:nt, :]), vbf, reads=[vbf], writes=[V_d])
        if DBG_KV < 4:
            return
        for sl in range(4):
            t = wslab(w_k, 0, sl * 256)
            for hh in range(4):
                h = sl * 4 + hh
                pst = nps()
                for kc in range(NCH):
                    c.op('pe', lambda pst=pst, t=t, kc=kc, hh=hh: nc.tensor.matmul(
                        pst[0:64, 0:N], lhsT=t[:, kc, hh * 64:(hh + 1) * 64], rhs=hT[:, kc, 0:N],
                        start=(kc == 0), stop=(kc == NCH - 1)), reads=[t, hT], writes=[pst])
                c.op('act', lambda pst=pst, h=h: nc.scalar.copy(out=ktn[0:64, h, 0:N], in_=pst[0:64, 0:N]),
                     reads=[pst], writes=[ktn])
        if not samp:
            c.dma('sp', lambda: nc.sync.dma_start(out=KT_d[:, :, :].rearrange("h p t -> p h t")[:, :, t0:t0 + N],
                                                  in_=ktn[0:64, :, 0:N]), ktn, reads=[ktn], writes=[KT_d])

    def q_project(j, N, samp):
        l = NA + j
        modnorm(l, 1, N, samp)
        for sl in range(4):
            t = wslab(w_q, j * D, sl * 256)
            for hh in range(4):
                h = sl * 4 + hh
                pst = nps()
                for kc in range(NCH):
                    c.op('pe', lambda pst=pst, t=t, kc=kc, hh=hh: nc.tensor.matmul(
                        pst[0:64, 0:N], lhsT=t[:, kc, hh * 64:(hh + 1) * 64], rhs=hT[:, kc, 0:N],
                        start=(kc == 0), stop=(kc == NCH - 1)), reads=[t, hT], writes=[pst])
                c.op('act', lambda pst=pst, h=h: nc.scalar.activation(out=qT[0:64, h, 0:N], in_=pst[0:64, 0:N],
                                                                      func=AF.Copy, scale=0.125),
                     reads=[pst], writes=[qT])

    def o_project(j, N, samp):
        l = NA + j
        for ch in range(NCH):
            t = wslab(w_o, j * D, ch * 128, kmode=64)
            tv = t[0:64, :, :].rearrange("p k (a n) -> p (k a) n", n=128)
            pst = nps()
            for h in range(NH):
                c.op('pe', lambda pst=pst, tv=tv, h=h: nc.tensor.matmul(
                    pst[:, 0:N], lhsT=tv[:, h, :], rhs=attT[0:64, h, 0:N], start=(h == 0), stop=(h == NH - 1)),
                    reads=[t, attT], writes=[pst])
            resid_add(pst, pst[:, 0:N], None, A(l, 2, ch), ch, N, samp)

    def csum_update(first, nk, W):
        if first:
            c.op('dve', lambda: nc.vector.tensor_copy(out=C32[0:nk, 0:W], in_=lp_t[0:nk, 0:W]), reads=[lp_t], writes=[C32])
        else:
            c.op('dve', lambda: nc.vector.tensor_tensor(out=C32[0:nk, 0:W], in0=C32[0:nk, 0:W], in1=lp_t[0:nk, 0:W], op=ALU.add),
                 reads=[C32, lp_t], writes=[C32])
        c.op('dve', lambda: nc.vector.tensor_copy(out=c16[:, 0:W], in_=C32[:, 0:W]), reads=[C32], writes=[c16])

    def attention_prompt(j, N, t0):
        q_project(j, N, False)
        if DBG_ATT < 3:
            return
        nb = (t0 + N) // 128
        kb0 = t0 // 128
        U2 = [u_t, u_tB]
        LP2 = [lp_t, lp_tB]
        PSZ = [psum[4], psum[5]]
        PSA = [psum[6], psum[0]]
        pso = psum[7]
        for h in range(NH):
            c.dma('sp', lambda h=h: nc.sync.dma_start(out=KTb[0:64, 0:t0 + N], in_=KT_d[h, :, 0:t0 + N]),
                  KTb, reads=[KT_d], writes=[KTb])
            c.dma('sp', lambda h=h: nc.sync.dma_start(
                out=Vb[:, 0:nb, :], in_=V_d[0:t0 + N, h * DH:(h + 1) * DH].rearrange("(kb p) d -> p kb d", p=128)),
                Vb, reads=[V_d], writes=[Vb])
            bcol = j * NH + h
            kbs = list(reversed(range(nb)))

            def stage1(idx, h=h, bcol=bcol):
                kb = kbs[idx]
                r = kb - kb0
                psz, ut, lpt = PSZ[idx % 2], U2[idx % 2], LP2[idx % 2]
                ks = slice(kb * 128, (kb + 1) * 128)
                c.op('pe', lambda: nc.tensor.matmul(psz[:, 0:N], lhsT=KTb[0:64, ks], rhs=qT[0:64, h, 0:N],
                                                    start=True, stop=True), reads=[KTb, qT], writes=[psz])
                c.op('act', lambda: nc.scalar.activation(out=ut[:, 0:N], in_=psz[:, 0:N], func=AF.Exp,
                                                         bias=bsb_sb[:, bcol:bcol + 1], scale=1.0),
                     reads=[psz, bsb_sb], writes=[ut])
                c.op('act', lambda: nc.scalar.activation(out=lpt[:, 0:N], in_=ut[:, 0:N], func=AF.Ln, bias=1.0, scale=1.0),
                     reads=[ut], writes=[lpt])
                if r >= 0:
                    c.op('pool', lambda: nc.gpsimd.affine_select(out=lpt[:, 0:N], in_=lpt[:, 0:N], pattern=[[1, N]],
                                                                 compare_op=ALU.is_gt, fill=0.0, base=-r * 128,
                                                                 channel_multiplier=-1), reads=[lpt], writes=[lpt])

            def stage2(idx, h=h, bcol=bcol):
                kb = kbs[idx]
                r = kb - kb0
                first = idx == 0
                psa, lpt = PSA[idx % 2], LP2[idx % 2]
                ks = slice(kb * 128, (kb + 1) * 128)
                c.op('pe', lambda: nc.tensor.matmul(psa[:, 0:N], lhsT=KTb[0:64, ks], rhs=qT[0:64, h, 0:N],
                                                    start=True, stop=False), reads=[KTb, qT], writes=[psa])
                c.op('pe', lambda: nc.tensor.matmul(psa[:, 0:N], lhsT=negtri[:, :], rhs=lpt[:, 0:N],
                                                    start=False, stop=first), reads=[negtri, lpt], writes=[psa])
                if not first:
                    c.op('pe', lambda: nc.tensor.matmul(psa[:, 0:N], lhsT=negones[:, :], rhs=c16[:, 0:N],
                                                        start=False, stop=True), reads=[negones, c16], writes=[psa])
                c.op('act', lambda: nc.scalar.activation(out=at_t[:, 0:N], in_=psa[:, 0:N], func=AF.Exp,
                                                         bias=bsb_sb[:, bcol:bcol + 1], scale=1.0),
                     reads=[psa, bsb_sb], writes=[at_t])
                if r >= 0:
                    c.op('pool', lambda: nc.gpsimd.affine_select(out=at_t[:, 0:N], in_=at_t[:, 0:N], pattern=[[1, N]],
                                                                 compare_op=ALU.is_gt, fill=0.0, base=-r * 128,
                                                                 channel_multiplier=-1), reads=[at_t], writes=[at_t])
                c.op('pe', lambda: nc.tensor.matmul(pso[0:64, 0:N], lhsT=Vb[:, kb, :], rhs=at_t[:, 0:N],
                                                    start=first, stop=(kb == 0)), reads=[Vb, at_t], writes=[pso])
                if kb > 0:
                    if first:
                        c.op('dve', lambda: nc.vector.tensor_copy(out=C32[:, 0:N], in_=lpt[:, 0:N]), reads=[lpt], writes=[C32])
                    else:
                        c.op('dve', lambda: nc.vector.tensor_tensor(out=C32[:, 0:N], in0=C32[:, 0:N], in1=lpt[:, 0:N], op=ALU.add),
                             reads=[C32, lpt], writes=[C32])
                    c.op('dve', lambda: nc.vector.tensor_copy(out=c16[:, 0:N], in_=C32[:, 0:N]), reads=[C32], writes=[c16])

            stage1(0)
            for idx in range(nb):
                if idx + 1 < nb:
                    stage1(idx + 1)
                stage2(idx)
            c.op('act', lambda h=h: nc.scalar.copy(out=attT[0:64, h, 0:N], in_=pso[0:64, 0:N]), reads=[pso], writes=[attT])
        if DBG_ATT >= 4:
            o_project(j, N, False)

    def attention_sample(j):
        N = 128
        q_project(j, N, True)
        ktp = KTb
        ktp3 = KTb[0:64, 0:NH * 128].rearrange("p (h n) -> p h n", n=128)
        c.op('dve', lambda: nc.vector.tensor_copy(
            out=bias_s[:, :].rearrange("p (h t) -> p h t", t=DEC_T),
            in_=bsb_sb[:, j * NH:(j + 1) * NH].unsqueeze(2).to_broadcast([128, NH, DEC_T])), reads=[bsb_sb], writes=[bias_s])
        for s in range(SPC):
            qs = slice(s * DEC_T, (s + 1) * DEC_T)
            c.dma('sp', lambda s=s: nc.sync.dma_start(out=vbf[0:DEC_T, 1, :], in_=VS_d[s * DEC_T:(s + 1) * DEC_T, :]),
                  vbf, reads=[VS_d], writes=[vbf])
            c.op('dve', lambda: nc.vector.memset(C32[:, 0:128], 0.0), writes=[C32])
            blocks = ['new'] + list(reversed(range(NPAGES)))
            for bi, blk in enumerate(blocks):
                first = bi == 0
                if blk == 'new':
                    nk = DEC_T
                    ksrc = lambda h, qs=qs: ktn[0:64, h, qs]
                    vsrc = lambda h: vbf[0:DEC_T, 1, h * DH:(h + 1) * DH]
                    ktk, vtk = ktn, vbf
                else:
                    nk = 128
                    col = s * NPAGES + blk
                    c.dma('pool', lambda col=col: nc.gpsimd.indirect_dma_start(
                        out=tm[:, 0, :], out_offset=None, in_=cache_k[:, :],
                        in_offset=bass.IndirectOffsetOnAxis(ap=idxpg[:, col:col + 1], axis=0)), tm, reads=[idxpg], writes=[tm])
                    c.dma('pool', lambda col=col: nc.gpsimd.indirect_dma_start(
                        out=tm[:, 1, :], out_offset=None, in_=cache_v[:, :],
                        in_offset=bass.IndirectOffsetOnAxis(ap=idxpg[:, col:col + 1], axis=0)), tm, reads=[idxpg], writes=[tm])
                    for bq in range(4):
                        pst = psum[bq]
                        for i in range(4):
                            h = bq * 4 + i
                            c.op('pe', lambda pst=pst, i=i, h=h: nc.tensor.transpose(
                                out=pst[0:64, i * 128:(i + 1) * 128], in_=tm[:, 0, h * DH:(h + 1) * DH], identity=ident[:, :]),
                                reads=[tm, ident], writes=[pst])
                        c.op('act', lambda pst=pst, bq=bq: nc.scalar.copy(
                            out=ktp3[:, bq * 4:(bq + 1) * 4, :], in_=pst[0:64, :].rearrange("p (a n) -> p a n", n=128)),
                            reads=[pst], writes=[ktp])
                    c.op('dve', lambda: nc.vector.tensor_copy(out=vbf[:, 0, :], in_=tm[:, 1, :]), reads=[tm], writes=[vbf])
                    ksrc = lambda h: ktp3[:, h, :]
                    vsrc = lambda h: vbf[:, 0, h * DH:(h + 1) * DH]
                    ktk, vtk = ktp, vbf
                psz, psa, pso = psum[4], psum[6], psum[7]
                for h in range(NH):
                    c.op('pe', lambda h=h, ksrc=ksrc, nk=nk, qs=qs: nc.tensor.matmul(
                        psz[0:nk, h * DEC_T:(h + 1) * DEC_T], lhsT=ksrc(h), rhs=qT[0:64, h, qs], start=True, stop=True),
                        reads=[ktk, qT], writes=[psz])
                c.op('dve', lambda nk=nk: nc.vector.tensor_tensor(out=tmpN[0:nk, 0:128], in0=psz[0:nk, 0:128], in1=bias_s[0:nk, :],
                                                                  op=ALU.add), reads=[psz, bias_s], writes=[tmpN])
                c.op('act', lambda nk=nk: nc.scalar.activation(out=u_t[0:nk, 0:128], in_=tmpN[0:nk, 0:128], func=AF.Exp),
                     reads=[tmpN], writes=[u_t])
                c.op('act', lambda nk=nk: nc.scalar.activation(out=lp_t[0:nk, 0:128], in_=u_t[0:nk, 0:128], func=AF.Ln,
                                                               bias=1.0, scale=1.0), reads=[u_t], writes=[lp_t])
                if blk == 'new':
                    c.op('pool', lambda nk=nk: nc.gpsimd.affine_select(
                        out=lp_t[0:nk, 0:128], in_=lp_t[0:nk, 0:128], pattern=[[0, NH], [1, DEC_T]], compare_op=ALU.is_gt,
                        fill=0.0, base=0, channel_multiplier=-1), reads=[lp_t], writes=[lp_t])
                c.op('pe', lambda nk=nk, first=first: nc.tensor.matmul(psa[0:nk, 0:128], lhsT=negtri[0:nk, 0:nk], rhs=lp_t[0:nk, 0:128],
                                                                        start=True, stop=first), reads=[negtri, lp_t], writes=[psa])
                if not first:
                    c.op('pe', lambda nk=nk: nc.tensor.matmul(psa[0:nk, 0:128], lhsT=negones[:, 0:nk], rhs=c16[:, 0:128],
                                                               start=False, stop=True), reads=[negones, c16], writes=[psa])
                c.op('act', lambda nk=nk: nc.scalar.activation(out=tmpM[0:nk, 0:128], in_=psa[0:nk, 0:128], func=AF.Exp),
                     reads=[psa], writes=[tmpM])
                c.op('dve', lambda nk=nk: nc.vector.tensor_tensor(out=at_t[0:nk, 0:128], in0=tmpM[0:nk, 0:128], in1=u_t[0:nk, 0:128],
                                                                  op=ALU.mult), reads=[tmpM, u_t], writes=[at_t])
                if blk == 'new':
                    c.op('pool', lambda nk=nk: nc.gpsimd.affine_select(
                        out=at_t[0:nk, 0:128], in_=at_t[0:nk, 0:128], pattern=[[0, NH], [1, DEC_T]], compare_op=ALU.is_gt,
                        fill=0.0, base=0, channel_multiplier=-1), reads=[at_t], writes=[at_t])
                for h in range(NH):
                    c.op('pe', lambda h=h, vsrc=vsrc, nk=nk: nc.tensor.matmul(
                        pso[0:64, h * DEC_T:(h + 1) * DEC_T], lhsT=vsrc(h), rhs=at_t[0:nk, h * DEC_T:(h + 1) * DEC_T],
                        start=True, stop=True), reads=[vtk, at_t], writes=[pso])
                if first:
                    c.op('dve', lambda: nc.vector.tensor_copy(out=oacc[:, :], in_=pso[0:64, 0:128]), reads=[pso], writes=[oacc])
                else:
                    c.op('dve', lambda: nc.vector.tensor_tensor(out=oacc[:, :], in0=oacc[:, :], in1=pso[0:64, 0:128], op=ALU.add),
                         reads=[pso, oacc], writes=[oacc])
                if bi < len(blocks) - 1:
                    csum_update(False, nk, 128)
            c.op('act', lambda qs=qs: nc.scalar.copy(out=attT[0:64, :, qs], in_=oacc[:, :].rearrange("p (h t) -> p h t", t=DEC_T)),
                 reads=[oacc], writes=[attT])
        o_project(j, N, True)

    def final_out(dst_ap, N):
        nt = N // 128
        rms_stats(N)
        for ch in range(NCH):
            c.op('dve', lambda ch=ch: nc.vector.scalar_tensor_tensor(
                out=scrA[:, ch, 0:N], in0=xT[:, ch, 0:N], scalar=fn_sb[:, ch:ch + 1], in1=rstd[:, 0:N],
                op0=ALU.mult, op1=ALU.mult), reads=[xT, fn_sb, rstd], writes=[scrA])
        for tb in range(nt):
            for ch in range(NCH):
                pst = nps()
                c.op('pe', lambda pst=pst, tb=tb, ch=ch: nc.tensor.transpose(
                    out=pst[:, 0:128], in_=scrA[:, ch, tb * 128:(tb + 1) * 128], identity=ident[:, :]),
                    reads=[scrA, ident], writes=[pst])
                c.op('act', lambda pst=pst, tb=tb, ch=ch: nc.scalar.copy(
                    out=tm[:, tb, ch * 128:(ch + 1) * 128], in_=pst[:, 0:128]), reads=[pst], writes=[tm])
        c.dma('sp', lambda: nc.sync.dma_start(out=dst_ap.rearrange("(n p) d -> p n d", p=128), in_=tm[:, 0:nt, :]),
              tm, reads=[tm], out_dram=True)

    def run_tile(src, dst, N, samp, ti, last):
        load_x_tile(src, N)
        for l in range(n_layers):
            if l < NA:
                conv_layer(l, N, samp, ti, last)
            else:
                if l == NA:
                    kv_project(N, samp, 0 if samp else ti * TP)
                if DBG_ATT >= 2:
                    if samp:
                        attention_sample(l - NA)
                    else:
                        attention_prompt(l - NA, N, ti * TP)
            if DO_PEER:
                peer_layer(l, N, samp)
        final_out(dst, N)

    for ti in range(n_ptiles):
        run_tile(x_p[ti * TP:(ti + 1) * TP, :], y_p[ti * TP:(ti + 1) * TP, :], TP, False, ti, ti == n_ptiles - 1)
    if do_sample:
        run_tile(x_s[:, :], y_s[:, :], 128, True, 0, True)

    c.emit()


def fm(v):
    v = np.asarray(v, np.float32)
    lead = int(np.prod(v.shape[:-1])) if v.ndim > 1 else 1
    n = v.shape[-1] // 128
    return np.ascontiguousarray(v.reshape(lead, n, 128).transpose(2, 0, 1).reshape(128, lead * n))


def _shared_inputs(inp):
    f32 = lambda a: np.ascontiguousarray(np.asarray(a, np.float32))
    sh = {}
    sh["w_ada"] = f32(inp["w_ada"]).reshape(DEPTH * D, 6 * D)
    sh["b_ada"] = fm(inp["b_ada"].reshape(DEPTH * 48, 128).reshape(-1))
    sh["nmix"] = fm(inp["norm_mix"].reshape(-1))
    sh["nffn"] = fm(inp["norm_ffn"].reshape(-1))
    sh["nkv"] = fm(inp["norm_kv"])
    sh["fnorm"] = fm(inp["final_norm"])
    sh["w_pw1"] = f32(inp["w_pw1"]).reshape(NA * D, 2 * D)
    sh["b_pw1"] = fm(inp["b_pw1"].reshape(-1))
    wd = np.asarray(inp["w_dw"], np.float32).reshape(NA, CW, NCH, 128)
    sh["w_dw"] = np.ascontiguousarray(wd.transpose(3, 0, 2, 1).reshape(128, NA * NCH * CW))
    sh["b_dw"] = fm(inp["b_dw"].reshape(-1))
    sh["ln_g"] = fm(inp["ln_g"].reshape(-1))
    sh["ln_b"] = fm(inp["ln_b"].reshape(-1))
    sh["w_pw2"] = f32(inp["w_pw2"]).reshape(NA * D, D)
    sh["b_pw2"] = fm(inp["b_pw2"].reshape(-1))
    sh["w_k"] = f32(inp["w_k"])
    sh["w_v"] = f32(inp["w_v"])
    sh["w_q"] = f32(inp["w_q"]).reshape(2 * D, D)
    sh["w_o"] = f32(inp["w_o"]).reshape(2 * D, D)
    sh["bsb"] = np.ascontiguousarray(np.broadcast_to(np.asarray(inp["b_sb"], np.float32).reshape(1, 2 * NH), (128, 2 * NH)))
    sh["w_pq"] = f32(inp["w_pq"]).reshape(DEPTH * D, 2 * D)
    sk = np.asarray(inp["sub_keys"], np.float32).transpose(0, 1, 2, 4, 3)
    sh["skT"] = np.ascontiguousarray(sk.reshape(DEPTH * 16 * 128, 128))
    sh["exp_u"] = f32(inp["expert_u"]).reshape(DEPTH * NEXP, D)
    sh["exp_v"] = f32(inp["expert_v"]).reshape(DEPTH * NEXP, D)
    sh["cache_k"] = f32(inp["cache_k"]).reshape(NPOOL * PAGE, D)
    sh["cache_v"] = f32(inp["cache_v"]).reshape(NPOOL * PAGE, D)
    return sh


def _core_inputs(inp, sh, core, n_ptiles=SEQ // TP):
    b = core % NB
    m = dict(sh)
    m["x_p"] = np.ascontiguousarray(np.asarray(inp["x_prompt"][b, :n_ptiles * TP], np.float32))
    s0 = core * SPC
    m["x_s"] = np.ascontiguousarray(np.asarray(inp["x_sample"][s0:s0 + SPC], np.float32).reshape(128, D))
    m["c_all"] = np.ascontiguousarray(np.concatenate([np.asarray(inp["c_prompt"][b:b + 1], np.float32),
                                                      np.asarray(inp["c_sample"][s0:s0 + SPC], np.float32)], axis=0))
    pt = np.asarray(inp["page_table"][s0:s0 + SPC], np.int32).reshape(1, SPC * NPAGES)
    m["ptab"] = np.ascontiguousarray(np.broadcast_to(pt, (128, SPC * NPAGES)))
    m["st_conv"] = np.ascontiguousarray(np.asarray(inp["state_conv"][:, s0:s0 + SPC], np.float32).reshape(NA * SPC * HIST, D))
    return m


_NC_CACHE = {}


def kernel(**inputs):
    if "nc" not in _NC_CACHE:
        _NC_CACHE["nc"] = build_nc()
    nc = _NC_CACHE["nc"]
    sh = _shared_inputs(inputs)
    in_maps = [_core_inputs(inputs, sh, cix) for cix in range(NCORES)]
    res = run_bass_kernel_spmd(nc, in_maps, core_ids=list(range(NCORES))).results
    y_prompt = np.stack([res[b]["y_p"] for b in range(NB)], axis=0)
    y_sample = np.concatenate([res[cix]["y_s"] for cix in range(NCORES)], axis=0).reshape(DEC_B, DEC_T, D)
    conv_prompt = np.stack([res[b]["conv_p"].reshape(NA, HIST, D) for b in range(NB)], axis=1)
    conv_sample = np.concatenate([res[cix]["conv_s"].reshape(NA, SPC, HIST, D) for cix in range(NCORES)], axis=1)
    k_prompt = np.stack([res[b]["k_p"] for b in range(NB)], axis=0).reshape(NB, SEQ, NH, DH)
    v_prompt = np.stack([res[b]["v_p"] for b in range(NB)], axis=0).reshape(NB, SEQ, NH, DH)
    k_sample = np.concatenate([res[cix]["k_s"] for cix in range(NCORES)], axis=0).reshape(DEC_B, DEC_T, NH, DH)
    v_sample = np.concatenate([res[cix]["v_s"] for cix in range(NCORES)], axis=0).reshape(DEC_B, DEC_T, NH, DH)
    return (y_prompt, y_sample, conv_prompt, conv_sample, k_prompt, v_prompt, k_sample, v_sample)
```
